# Optimizing a Trainium2 kernel written in Bass

```python
import jax, jax.numpy as jnp
from jax import lax
import numpy as np

D_MODEL = 2048
BATCH = 16
SEQ = 2048
DEPTH = 4

CHUNK = 64
N_META = 16
EPS = 1e-6
D_FF = 5632
D_CONV = D_MODEL // 2
CONV_W = 3
D_RWKV = D_MODEL - D_CONV
HEAD_RWKV = 64
H_RWKV = D_RWKV // HEAD_RWKV
DECAY_RANK = max(32, int(round(1.8 * D_RWKV ** 0.5 / 32)) * 32)
AAA_RANK = max(32, int(round(1.8 * D_RWKV ** 0.5 / 32)) * 32)
GATE_RANK = max(32, int(round(0.6 * D_RWKV ** 0.8 / 32)) * 32)
RWKV_PROJ = 3 * D_RWKV + DECAY_RANK + AAA_RANK + GATE_RANK
EVEN_PROJ = 3 * D_CONV + RWKV_PROJ
LNX_EPS = 64e-5
HEAD_FOX = 128
H_FOX = D_MODEL // HEAD_FOX
ODD_PROJ = 4 * D_MODEL + H_FOX
Q_BLOCK = 128
FORGET_BIAS = 2.0
N_EVEN = (DEPTH + 1) // 2
N_ODD = DEPTH // 2

kernel_name = 'hybrid_conv_rwkv7_fox_macaron_encoder'


def rmsnorm(x, g):
    xf = x.astype(jnp.float32)
    y = xf * lax.rsqrt(jnp.mean(xf * xf, axis=-1, keepdims=True) + EPS)
    return (y * g.astype(jnp.float32)).astype(x.dtype)


def swiglu(x, w_in, w_out):
    gu = x @ w_in
    return (jax.nn.silu(gu[..., :D_FF]) * gu[..., D_FF:]) @ w_out


def rwkv7_scan(r, w, k, v, a, b):
    bsz, _, h, n = r.shape
    xs = tuple(jnp.moveaxis(t, 1, 0) for t in (r, w, k, v, a, b))

    def step(S, inp):
        r_t, w_t, k_t, v_t, a_t, b_t = inp
        sa = jnp.einsum('bhvk,bhk->bhv', S, a_t)
        S = S * w_t[:, :, None, :] + sa[..., None] * b_t[:, :, None, :] + v_t[..., None] * k_t[:, :, None, :]
        return S, jnp.einsum('bhvk,bhk->bhv', S, r_t)

    S0 = jnp.zeros((bsz, h, n, n), jnp.float32)
    _, y = lax.scan(step, S0, xs)
    return jnp.moveaxis(y, 0, 1)


def conv_rwkv_mixer(xn, w_in, conv_w, mu, w0, w2, a0, a2, g2, k_k, k_a, r_k, lnx_g, lnx_b, w_out):
    bsz, L, _ = xn.shape
    p = xn @ w_in
    gate_b = p[..., :D_CONV]
    gate_c = p[..., D_CONV:2 * D_CONV]
    hv = p[..., 2 * D_CONV:3 * D_CONV]
    z = gate_c * hv
    zp = jnp.pad(z, ((0, 0), (CONV_W - 1, 0), (0, 0)))
    conv = sum(conv_w[j] * zp[:, j:j + L] for j in range(CONV_W))
    y_a = gate_b * conv
    pb = p[..., 3 * D_CONV:]
    pb_prev = jnp.pad(pb, ((0, 0), (1, 0), (0, 0)))[:, :L]
    u = pb + (pb_prev - pb) * mu
    o = 0
    r = u[..., o:o + D_RWKV]; o += D_RWKV
    k = u[..., o:o + D_RWKV]; o += D_RWKV
    v = u[..., o:o + D_RWKV]; o += D_RWKV
    uw = u[..., o:o + DECAY_RANK]; o += DECAY_RANK
    ua = u[..., o:o + AAA_RANK]; o += AAA_RANK
    ug = u[..., o:o + GATE_RANK]
    w_log = -jax.nn.softplus(-(w0 + jnp.tanh(uw) @ w2).astype(jnp.float32)) - 0.5
    decay = jnp.exp(-jnp.exp(w_log))
    a = jax.nn.sigmoid((a0 + ua @ a2).astype(jnp.float32))
    g = jax.nn.sigmoid(ug) @ g2
    heads = lambda t: t.reshape(bsz, L, H_RWKV, HEAD_RWKV)
    kk = heads((k * k_k).astype(jnp.float32))
    kk = kk / jnp.maximum(jnp.linalg.norm(kk, axis=-1, keepdims=True), 1e-12)
    kf = k.astype(jnp.float32) * (1.0 + (a - 1.0) * k_a.astype(jnp.float32))
    rf = r.astype(jnp.float32)
    vf = v.astype(jnp.float32)
    y = rwkv7_scan(heads(rf), heads(decay), heads(kf), heads(vf), -kk, kk * heads(a))
    mean = jnp.mean(y, axis=-1, keepdims=True)
    var = jnp.mean(jnp.square(y - mean), axis=-1, keepdims=True)
    yn = ((y - mean) * lax.rsqrt(var + LNX_EPS)).reshape(bsz, L, D_RWKV)
    yn = yn * lnx_g.astype(jnp.float32) + lnx_b.astype(jnp.float32)
    bonus = jnp.sum(heads(rf * kf * r_k.astype(jnp.float32)), axis=-1, keepdims=True) * heads(vf)
    y_b = ((yn + bonus.reshape(bsz, L, D_RWKV)) * g.astype(jnp.float32)).astype(xn.dtype)
    return jnp.concatenate([y_a, y_b], axis=-1) @ w_out


def forgetting_attention(q, k, v, c):
    L = q.shape[1]
    dh = q.shape[-1]
    nb = -(-L // Q_BLOCK)
    lp = nb * Q_BLOCK
    pad4 = ((0, 0), (0, lp - L), (0, 0), (0, 0))
    q, k, v = jnp.pad(q, pad4), jnp.pad(k, pad4), jnp.pad(v, pad4)
    cT = jnp.pad(c, ((0, 0), (0, lp - L), (0, 0))).transpose(0, 2, 1)
    scale = dh ** -0.5
    outs = []
    for i in range(nb):
        q0, q1 = i * Q_BLOCK, (i + 1) * Q_BLOCK
        s = jnp.einsum('bqhd,bkhd->bhqk', q[:, q0:q1], k[:, :q1]).astype(jnp.float32) * scale
        s = s + cT[:, :, q0:q1, None] - cT[:, :, None, :q1]
        mask = np.arange(q0, q1)[:, None] >= np.arange(q1)[None, :]
        s = jnp.where(mask, s, -jnp.inf)
        pr = jax.nn.softmax(s, axis=-1).astype(v.dtype)
        outs.append(jnp.einsum('bhqk,bkhd->bqhd', pr, v[:, :q1]))
    return jnp.concatenate(outs, axis=1)[:, :L]


def fox_mixer(xn, w_in, b_f, q_g, k_g, w_out):
    bsz, L, _ = xn.shape
    p = xn @ w_in
    heads = lambda t: t.reshape(bsz, L, H_FOX, HEAD_FOX)
    q = rmsnorm(heads(p[..., :D_MODEL]), q_g)
    k = rmsnorm(heads(p[..., D_MODEL:2 * D_MODEL]), k_g)
    v = heads(p[..., 2 * D_MODEL:3 * D_MODEL])
    og = p[..., 3 * D_MODEL:4 * D_MODEL]
    fl = p[..., 4 * D_MODEL:]
    logf = jax.nn.log_sigmoid(fl.astype(jnp.float32) + b_f.astype(jnp.float32))
    c = jnp.cumsum(logf, axis=1)
    o = forgetting_attention(q, k, v, c).reshape(bsz, L, D_MODEL)
    return (o * jax.nn.sigmoid(og)) @ w_out


def setup_inputs(seed: int = 0) -> dict:
    key = jax.random.key(seed)
    ks = jax.random.split(key, 32)
    nrm = lambda kk, shape, sc: jax.random.normal(kk, shape, jnp.float32) * sc
    D = D_MODEL
    return {
        'x': nrm(ks[0], (BATCH, SEQ, D), 1.0),
        'meta': nrm(ks[1], (N_META, D), 1.0),
        'norm_g': 1.0 + nrm(ks[2], (DEPTH, 6, D), 0.05),
        'ffn_in': nrm(ks[3], (DEPTH, 2, D, 2 * D_FF), D ** -0.5),
        'ffn_out': nrm(ks[4], (DEPTH, 2, D_FF, D), D_FF ** -0.5),
        'e_w_in': nrm(ks[5], (N_EVEN, D, EVEN_PROJ), D ** -0.5),
        'e_conv_w': nrm(ks[6], (N_EVEN, CONV_W, D_CONV), CONV_W ** -0.5),
        'e_mu': jax.random.uniform(ks[7], (N_EVEN, RWKV_PROJ), jnp.float32),
        'e_w0': nrm(ks[8], (N_EVEN, D_RWKV), 0.5),
        'e_w2': nrm(ks[9], (N_EVEN, DECAY_RANK, D_RWKV), DECAY_RANK ** -0.5),
        'e_a0': nrm(ks[10], (N_EVEN, D_RWKV), 0.1),
        'e_a2': nrm(ks[11], (N_EVEN, AAA_RANK, D_RWKV), AAA_RANK ** -0.5),
        'e_g2': nrm(ks[12], (N_EVEN, GATE_RANK, D_RWKV), GATE_RANK ** -0.5),
        'e_k_k': 0.85 + nrm(ks[13], (N_EVEN, D_RWKV), 0.05),
        'e_k_a': 1.0 + nrm(ks[14], (N_EVEN, D_RWKV), 0.05),
        'e_r_k': nrm(ks[15], (N_EVEN, D_RWKV), 0.1),
        'e_lnx_g': 1.0 + nrm(ks[16], (N_EVEN, D_RWKV), 0.05),
        'e_lnx_b': nrm(ks[17], (N_EVEN, D_RWKV), 0.02),
        'e_w_out': nrm(ks[18], (N_EVEN, D_CONV + D_RWKV, D), (D_CONV + D_RWKV) ** -0.5),
        'o_w_in': nrm(ks[19], (N_ODD, D, ODD_PROJ), D ** -0.5),
        'o_b_f': FORGET_BIAS + nrm(ks[20], (N_ODD, H_FOX), 0.5),
        'o_q_g': 1.0 + nrm(ks[21], (N_ODD, HEAD_FOX), 0.05),
        'o_k_g': 1.0 + nrm(ks[22], (N_ODD, HEAD_FOX), 0.05),
        'o_w_out': nrm(ks[23], (N_ODD, D, D), D ** -0.5),
    }


def reference(x, meta, norm_g, ffn_in, ffn_out, e_w_in, e_conv_w, e_mu, e_w0, e_w2, e_a0, e_a2,
              e_g2, e_k_k, e_k_a, e_r_k, e_lnx_g, e_lnx_b, e_w_out, o_w_in, o_b_f, o_q_g, o_k_g,
              o_w_out):
    bsz = x.shape[0]
    h = jnp.concatenate([jnp.broadcast_to(meta[None].astype(x.dtype), (bsz, N_META, D_MODEL)), x], axis=1)
    for l in range(DEPTH):
        g = norm_g[l]
        h = h + 0.5 * rmsnorm(swiglu(rmsnorm(h, g[0]), ffn_in[l, 0], ffn_out[l, 0]), g[1])
        xn = rmsnorm(h, g[2])
        i = l // 2
        if l % 2 == 0:
            m = conv_rwkv_mixer(xn, e_w_in[i], e_conv_w[i], e_mu[i], e_w0[i], e_w2[i], e_a0[i], e_a2[i],
                                e_g2[i], e_k_k[i], e_k_a[i], e_r_k[i], e_lnx_g[i], e_lnx_b[i], e_w_out[i])
        else:
            m = fox_mixer(xn, o_w_in[i], o_b_f[i], o_q_g[i], o_k_g[i], o_w_out[i])
        h = h + rmsnorm(m, g[3])
        h = h + 0.5 * rmsnorm(swiglu(rmsnorm(h, g[4]), ffn_in[l, 1], ffn_out[l, 1]), g[5])
    return h[:, N_META:]
```

```python
from contextlib import ExitStack
import numpy as np
import concourse.bass as bass
import concourse.mybir as mybir
from concourse.bass_utils import run_bass_kernel_spmd

F32 = mybir.dt.float32
BF16 = mybir.dt.bfloat16
AF = mybir.ActivationFunctionType
ALU = mybir.AluOpType
AX = mybir.AxisListType

D = 2048
KC = 16
DFF = 5632
NJ = 44
NSEQ = 2
LSEQ = 2064
TC = NSEQ * LSEQ
NCORES = 8
EPS = 1e-6


class Res:
    __slots__ = ("name", "w", "r")

    def __init__(self, name=""):
        self.name = name
        self.w = None
        self.r = {}


class Sched:
    ENG = ("pe", "act", "dve", "pool", "sp")

    def __init__(self, nc, n_dma_sems=40):
        self.nc = nc
        self.eng = {"pe": nc.tensor, "act": nc.scalar, "dve": nc.vector, "pool": nc.gpsimd, "sp": nc.sync}
        self.sem = {}
        self.cnt = {}
        self.seen = {e: {} for e in self.ENG}
        for e in self.ENG:
            self.sem[e] = nc.alloc_semaphore(name="sem_" + e)
            self.cnt[e] = 0
        self.n_dma = n_dma_sems
        for i in range(n_dma_sems):
            a = ("dma", i)
            self.sem[a] = nc.alloc_semaphore(name="sem_dma%d" % i)
            self.cnt[a] = 0
        self.dma_rr = 0
        self.n_inst = 0

    def _need(self, reads, writes):
        need = {}
        for r in reads:
            if r.w is not None:
                a, c = r.w
                if need.get(a, 0) < c:
                    need[a] = c
        for w in writes:
            if w.w is not None:
                a, c = w.w
                if need.get(a, 0) < c:
                    need[a] = c
            for a, c in w.r.items():
                if need.get(a, 0) < c:
                    need[a] = c
        return need

    def _wait(self, e, need):
        seen = self.seen[e]
        eng = self.eng[e]
        for a, c in need.items():
            if a == e and e == "pe":
                continue
            if seen.get(a, 0) < c:
                eng.wait_ge(self.sem[a], c)
                seen[a] = c
                self.n_inst += 1

    def op(self, e, fn, reads=(), writes=()):
        need = self._need(reads, writes)
        self._wait(e, need)
        ins = fn(self.eng[e])
        ins.then_inc(self.sem[e], 1)
        self.cnt[e] += 1
        self.n_inst += 1
        c = self.cnt[e]
        for r in reads:
            r.r[e] = c
        for w in writes:
            w.w = (e, c)
            w.r = {}
        return ins

    def mm_group(self, fns, reads=(), writes=()):
        need = self._need(reads, writes)
        self._wait("pe", need)
        ins = None
        for fn in fns:
            ins = fn(self.nc.tensor)
            self.n_inst += 1
        ins.then_inc(self.sem["pe"], 1)
        self.cnt["pe"] += 1
        c = self.cnt["pe"]
        for r in reads:
            r.r["pe"] = c
        for w in writes:
            w.w = ("pe", c)
            w.r = {}

    def dma(self, e, out, in_, reads=(), writes=(), **kw):
        i = self.dma_rr
        self.dma_rr = (self.dma_rr + 1) % self.n_dma
        a = ("dma", i)
        need = self._need(reads, writes)
        if self.cnt[a] > 0 and need.get(a, 0) < self.cnt[a]:
            need[a] = self.cnt[a]
        self._wait(e, need)
        ins = self.eng[e].dma_start(out=out, in_=in_, **kw)
        ins.then_inc(self.sem[a], 16)
        self.cnt[a] += 16
        self.n_inst += 1
        c = self.cnt[a]
        for r in reads:
            r.r[a] = c
        for w in writes:
            w.w = (a, c)
            w.r = {}
        return ins

    def barrier(self):
        tot = dict(self.cnt)
        for e in self.ENG:
            self._wait(e, {a: c for a, c in tot.items() if c > 0 and not (a == e)})

    def finish(self):
        tot = {a: c for a, c in self.cnt.items() if c > 0 and a != "sp"}
        self._wait("sp", tot)


TT = 688
SUB = 344
NSUB = TT // SUB


class Ctx:
    pass


_UID = [0]


def _u(n):
    return "%s_%d" % (n, _UID[0])

def rstd_from_banks(S, cx, ps_banks, dst, r_dst, nsub, sub, inv_n, eps_col=None):
    eps_col = cx.eps_col if eps_col is None else eps_col
    for s in range(nsub):
        b = ps_banks[s]
        S.op("act", lambda a, b=b, s=s: a.activation(
            out=dst[:, s * sub:(s + 1) * sub], in_=cx.bank[b][:, 0:sub], func=AF.Sqrt, scale=inv_n, bias=eps_col[:]),
            reads=[cx.r_bank[b]], writes=[r_dst])
    S.op("dve", lambda v: v.reciprocal(out=dst[:, 0:nsub * sub], in_=dst[:, 0:nsub * sub]), reads=[r_dst], writes=[r_dst])


class OutBufs:
    def __init__(self, nc, pfx, K):
        self.nc, self.pfx, self.K = nc, pfx, K

    def __enter__(self):
        nc, p, K = self.nc, self.pfx, self.K
        self._cms = [nc.sbuf_tensor(_u(p + "_rstd2"), [128, TT], F32), nc.sbuf_tensor(_u(p + "_y"), [128, KC, TT], F32),
                     nc.sbuf_tensor(_u(p + "_t1"), [128, 2, TT], F32), nc.sbuf_tensor(_u(p + "_wout"), [128, 2, K, 128], BF16)]
        self.rstd2, self.y, self.t1, self.wout = [c.__enter__() for c in self._cms]
        self.r_rstd2 = Res()
        self.r_y = [Res() for _ in range(KC)]
        self.r_t1 = [Res() for _ in range(2)]
        self.r_wout = [Res() for _ in range(2)]
        self.hb_i = [0]
        self.wo_i = [0]
        return self

    def __exit__(self, *a):
        for c in reversed(self._cms):
            c.__exit__(*a)


def out_proj_tile(S, cx, ob, hT3, t0, src, r_src, K, w_r, gain, gi, next_bank):
    bank, r_bank = cx.bank, cx.r_bank
    y, t1, wout, rstd2, sq, hbuf = ob.y, ob.t1, ob.wout, ob.rstd2, ob.sq, ob.hbuf
    r_h = cx.r_h
    for n in range(KC):
        wo = ob.wo_i[0] % 2
        ob.wo_i[0] += 1
        S.dma("pool", wout[:, wo], w_r[n], writes=[ob.r_wout[wo]], max_dma_last_dim=8192)
        for s in range(NSUB):
            b = next_bank()
            S.mm_group([lambda t, k=k, b=b, s=s, wo=wo: t.matmul(
                bank[b][:, 0:SUB], wout[:, wo, k, :], src[:, k, s * SUB:(s + 1) * SUB],
                start=(k == 0), stop=(k == K - 1)) for k in range(K)],
                reads=[ob.r_wout[wo]] + list(r_src), writes=[r_bank[b]])
            S.op("act", lambda a, b=b, n=n, s=s: a.activation(out=y[:, n, s * SUB:(s + 1) * SUB],
                                                             in_=bank[b][:, 0:SUB], func=AF.Copy),
                 reads=[r_bank[b]], writes=[ob.r_y[n]])
            q = (n * NSUB + s) % 2
            S.op("act", lambda a, b=b, q=q: a.activation(out=sq[:, q, 0:SUB], in_=bank[b][:, 0:SUB], func=AF.Square),
                 reads=[r_bank[b]], writes=[ob.r_sq[q]])
            S.mm_group([lambda t, s=s, q=q, n=n: t.matmul(bank[6 + s][:, 0:SUB], cx.ones_bf[:], sq[:, q, 0:SUB],
                                                          start=(n == 0), stop=(n == KC - 1))],
                       reads=[ob.r_sq[q]], writes=[r_bank[6 + s]])
    rstd_from_banks(S, cx, [6, 7], rstd2, ob.r_rstd2, NSUB, SUB, 1.0 / D)
    for n in range(KC):
        hb = ob.hb_i[0] % 3
        ob.hb_i[0] += 1
        S.dma("sp", hbuf[:, hb, :], hT3[n, :, t0:t0 + TT], reads=[r_h[n]], writes=[ob.r_hbuf[hb]])
        q = n % 2
        S.op("dve", lambda v, n=n, q=q: v.scalar_tensor_tensor(
            out=t1[:, q, :], in0=y[:, n, :], scalar=gain[:, gi + n:gi + n + 1], in1=rstd2[:],
            op0=ALU.mult, op1=ALU.mult), reads=[ob.r_y[n], ob.r_rstd2], writes=[ob.r_t1[q]])
        S.op("dve", lambda v, q=q, hb=hb: v.tensor_tensor(out=t1[:, q, :], in0=t1[:, q, :], in1=hbuf[:, hb, :],
                                                          op=ALU.add),
             reads=[ob.r_t1[q], ob.r_hbuf[hb]], writes=[ob.r_t1[q]])
        S.dma("sp", hT3[n, :, t0:t0 + TT], t1[:, q, :], reads=[ob.r_t1[q]], writes=[r_h[n]])


def ffn_phase(S, cx, hT, w_in_r, w_out_r, gi0, gi1, n_tiles):
    _UID[0] += 1
    nc = S.nc
    hT3 = hT.rearrange("(kc p) t -> kc p t", p=128)
    with (
        nc.sbuf_tensor(_u("f_hbuf"), [128, 3, TT], F32) as hbuf,
        nc.sbuf_tensor(_u("f_sq"), [128, 2, TT], BF16) as sq,
        nc.sbuf_tensor(_u("f_xn"), [128, KC, TT], BF16) as xn,
        nc.sbuf_tensor(_u("f_rstd"), [128, TT], F32) as rstd,
        nc.sbuf_tensor(_u("f_hid"), [128, NJ, TT], BF16) as hid,
        nc.sbuf_tensor(_u("f_sg"), [128, 2, SUB], F32) as sg,
        OutBufs(nc, "f", NJ) as ob,
        nc.sbuf_tensor(_u("f_win"), [128, 3, KC, 256], BF16) as win,
    ):
        r_hbuf = [Res() for _ in range(3)]
        r_sq = [Res() for _ in range(2)]
        r_xn = Res()
        r_rstd = Res()
        r_hid = [Res() for _ in range(NJ)]
        r_sg = [Res() for _ in range(2)]
        r_win = [Res() for _ in range(3)]
        r_h = cx.r_h
        bank = cx.bank
        r_bank = cx.r_bank
        nb = [0]

        def next_bank():
            b = nb[0] % 6
            nb[0] += 1
            return b

        ob.hbuf, ob.r_hbuf, ob.sq, ob.r_sq = hbuf, r_hbuf, sq, r_sq
        hb_i = ob.hb_i
        wi_i = [0]

        def rstd_from(ps_banks, dst, r_dst):
            rstd_from_banks(S, cx, ps_banks, dst, r_dst, NSUB, SUB, 1.0 / D)

        for ti in range(n_tiles):
            t0 = ti * TT
            for kc in range(KC):
                hb = hb_i[0] % 3
                hb_i[0] += 1
                S.dma("sp", hbuf[:, hb, :], hT3[kc, :, t0:t0 + TT], reads=[r_h[kc]], writes=[r_hbuf[hb]])
                q = kc % 2
                S.op("act", lambda a, hb=hb, q=q: a.activation(out=sq[:, q, :], in_=hbuf[:, hb, :], func=AF.Square),
                     reads=[r_hbuf[hb]], writes=[r_sq[q]])
                S.mm_group([lambda t, s=s, q=q, kc=kc: t.matmul(bank[6 + s][:, 0:SUB], cx.ones_bf[:],
                                                                 sq[:, q, s * SUB:(s + 1) * SUB],
                                                                 start=(kc == 0), stop=(kc == KC - 1))
                            for s in range(NSUB)],
                           reads=[r_sq[q]], writes=[r_bank[6], r_bank[7]])
            rstd_from([6, 7], rstd, r_rstd)
            for kc in range(KC):
                hb = hb_i[0] % 3
                hb_i[0] += 1
                S.dma("sp", hbuf[:, hb, :], hT3[kc, :, t0:t0 + TT], reads=[r_h[kc]], writes=[r_hbuf[hb]])
                S.op("dve", lambda v, hb=hb, kc=kc: v.scalar_tensor_tensor(
                    out=xn[:, kc, :], in0=hbuf[:, hb, :], scalar=cx.gcol[:, gi0 + kc:gi0 + kc + 1], in1=rstd[:],
                    op0=ALU.mult, op1=ALU.mult), reads=[r_hbuf[hb], r_rstd], writes=[r_xn])
            for j in range(NJ):
                wi = wi_i[0] % 3
                wi_i[0] += 1
                S.dma("pool", win[:, wi], w_in_r[j], writes=[r_win[wi]], max_dma_last_dim=8192)
                for s in range(NSUB):
                    bg = next_bank()
                    bu = next_bank()
                    for half, b in ((0, bg), (1, bu)):
                        S.mm_group([lambda t, kc=kc, half=half, b=b, s=s, wi=wi: t.matmul(
                            bank[b][:, 0:SUB], win[:, wi, kc, half * 128:(half + 1) * 128],
                            xn[:, kc, s * SUB:(s + 1) * SUB], start=(kc == 0), stop=(kc == KC - 1))
                            for kc in range(KC)], reads=[r_win[wi], r_xn], writes=[r_bank[b]])
                    q = (j * NSUB + s) % 2
                    S.op("act", lambda a, q=q, bg=bg: a.activation(out=sg[:, q, :], in_=bank[bg][:, 0:SUB], func=AF.Silu),
                         reads=[r_bank[bg]], writes=[r_sg[q]])
                    S.op("dve", lambda v, q=q, bu=bu, j=j, s=s: v.tensor_tensor(
                        out=hid[:, j, s * SUB:(s + 1) * SUB], in0=sg[:, q, :], in1=bank[bu][:, 0:SUB], op=ALU.mult),
                        reads=[r_sg[q], r_bank[bu]], writes=[r_hid[j]])
            out_proj_tile(S, cx, ob, hT3, t0, hid, r_hid, NJ, w_out_r, cx.gcol_half, gi1, next_bank)
        S.barrier()


def make_ctx(S, gcol_dram, n_g, consts_dram=None):
    nc = S.nc
    cx = Ctx()
    cx.bank = [nc.alloc_psum_tensor("bank%d" % i, [128, 512], F32) for i in range(8)]
    cx.r_bank = [Res("bank%d" % i) for i in range(8)]
    cx.ones_bf = nc.alloc_sbuf_tensor("ones_bf", [128, 128], BF16)
    cx.ones_f = nc.alloc_sbuf_tensor("ones_f", [128, 128], F32)
    cx.gcol = nc.alloc_sbuf_tensor("sb_gcol", [128, n_g], F32)
    cx.gcol_half = nc.alloc_sbuf_tensor("sb_gcol_half", [128, n_g], F32)
    cx.r_h = [Res("h%d" % k) for k in range(KC)]
    cx.eps_col = nc.alloc_sbuf_tensor("eps_col", [128, 1], F32)
    S.op("dve", lambda v: v.memset(cx.eps_col[:], EPS))
    cx.one_col = nc.alloc_sbuf_tensor("one_col", [128, 1], F32)
    S.op("dve", lambda v: v.memset(cx.one_col[:], 1.0))
    cx.r_scr = Res("scratch")
    cx.r_scr_c = Res("scratch_c")
    r_c = Res("consts")
    S.op("dve", lambda v: v.memset(cx.ones_bf[:], 1.0), writes=[r_c])
    S.op("dve", lambda v: v.memset(cx.ones_f[:], 1.0), writes=[r_c])
    S.dma("sp", cx.gcol[:], gcol_dram, writes=[r_c])
    S.op("dve", lambda v: v.tensor_scalar(out=cx.gcol_half[:], in0=cx.gcol[:], scalar1=0.5, scalar2=None, op0=ALU.mult),
         reads=[r_c], writes=[r_c])
    if consts_dram is not None:
        cx.ident_bf = nc.alloc_sbuf_tensor("ident_bf", [128, 128], BF16)
        cx.maskneg_bf = nc.alloc_sbuf_tensor("maskneg_bf", [128, 128], BF16)
        S.dma("pool", cx.ident_bf[:], consts_dram[0], writes=[r_c])
        S.dma("pool", cx.maskneg_bf[:], consts_dram[1], writes=[r_c])
        cx.ident_f = nc.alloc_sbuf_tensor("ident_f", [128, 128], F32)
        cx.blockones_f = nc.alloc_sbuf_tensor("blockones_f", [128, 128], F32)
        cx.blockones_bf = nc.alloc_sbuf_tensor("blockones_bf", [128, 128], BF16)
        cx.maskG = nc.alloc_sbuf_tensor("maskG", [128, 128], F32)
        cx.mask3 = nc.alloc_sbuf_tensor("mask3", [128, 64], F32)
        cx.identstack = nc.alloc_sbuf_tensor("identstack", [128, 64], F32)
        cx.lnx_eps_col = nc.alloc_sbuf_tensor("lnx_eps_col", [128, 1], F32)
        S.dma("sp", cx.ident_f[:], consts_dram[0], writes=[r_c])
        S.dma("sp", cx.blockones_f[:], consts_dram[2], writes=[r_c])
        S.dma("pool", cx.blockones_bf[:], consts_dram[2], writes=[r_c])
        S.dma("sp", cx.maskG[:], consts_dram[3], writes=[r_c])
        S.dma("sp", cx.mask3[:], consts_dram[4][:, 0:64], writes=[r_c])
        S.dma("sp", cx.identstack[:], consts_dram[5][:, 0:64], writes=[r_c])
        S.op("dve", lambda v: v.memset(cx.lnx_eps_col[:], LNX_EPS), writes=[r_c])
    S.barrier()
    return cx


NS6 = LSEQ // SUB


def seq_prenorm(S, cx, hT3, cb, xn, r_xn, hbig, r_hbig, sqb, r_sqb, rstd, r_rstd, gi):
    bank, r_bank = cx.bank, cx.r_bank
    for kc in range(KC):
        hb = kc % 2
        S.dma("sp", hbig[:, hb, :], hT3[kc, :, cb:cb + LSEQ], reads=[cx.r_h[kc]], writes=[r_hbig[hb]])
        S.op("act", lambda a, hb=hb: a.activation(out=sqb[:, hb, :], in_=hbig[:, hb, :], func=AF.Square),
             reads=[r_hbig[hb]], writes=[r_sqb[hb]])
        S.mm_group([lambda t, s=s, hb=hb, kc=kc: t.matmul(bank[s][:, 0:SUB], cx.ones_bf[:], sqb[:, hb, s * SUB:(s + 1) * SUB],
                                                         start=(kc == 0), stop=(kc == KC - 1)) for s in range(NS6)],
                   reads=[r_sqb[hb]], writes=[r_bank[s] for s in range(NS6)])
    rstd_from_banks(S, cx, list(range(NS6)), rstd, r_rstd, NS6, SUB, 1.0 / D)
    for kc in range(KC):
        hb = kc % 2
        S.dma("sp", hbig[:, hb, :], hT3[kc, :, cb:cb + LSEQ], reads=[cx.r_h[kc]], writes=[r_hbig[hb]])
        S.op("dve", lambda v, hb=hb, kc=kc: v.scalar_tensor_tensor(
            out=xn[:, kc, :], in0=hbig[:, hb, :], scalar=cx.gcol[:, gi + kc:gi + kc + 1], in1=rstd[:],
            op0=ALU.mult, op1=ALU.mult), reads=[r_hbig[hb], r_rstd], writes=[r_xn])


def mixer_out_phase(S, cx, hT, cb, src_scr, w_r, gi):
    _UID[0] += 1
    nc = S.nc
    hT3 = hT.rearrange("(kc p) t -> kc p t", p=128)
    src3 = src_scr.rearrange("(kc p) t -> p kc t", p=128)
    with (
        nc.sbuf_tensor(_u("mo_hbuf"), [128, 3, TT], F32) as hbuf,
        nc.sbuf_tensor(_u("mo_sq"), [128, 2, TT], BF16) as sq,
        nc.sbuf_tensor(_u("mo_src"), [128, 2, KC, TT], BF16) as src,
        OutBufs(nc, "mo", KC) as ob,
    ):
        ob.hbuf, ob.r_hbuf = hbuf, [Res() for _ in range(3)]
        ob.sq, ob.r_sq = sq, [Res() for _ in range(2)]
        r_src = [Res(), Res()]
        nb = [0]

        def next_bank():
            b = nb[0] % 6
            nb[0] += 1
            return b
        for ti in range(LSEQ // TT):
            t0 = ti * TT
            sb = ti % 2
            S.dma("sp", src[:, sb], src3[:, :, t0:t0 + TT], reads=[cx.r_scr], writes=[r_src[sb]])
            out_proj_tile(S, cx, ob, hT3, cb + t0, src[:, sb], [r_src[sb]], KC, w_r, cx.gcol, gi, next_bank)
        S.barrier()


HD = 16
NBLK = 17
QT = [(0, 512), (512, 512), (1024, 512), (1536, 512), (2048, 16)]


def fox_phase(S, cx, hT, cb, W, gi2, gi3, scr):
    _UID[0] += 1
    nc = S.nc
    L = LSEQ
    hT3 = hT.rearrange("(kc p) t -> kc p t", p=128)
    bank, r_bank = cx.bank, cx.r_bank
    with (
        nc.sbuf_tensor(_u("x_xn"), [128, KC, L], BF16) as xn,
        nc.sbuf_tensor(_u("x_small"), [128, 4], F32) as small,
        nc.sbuf_tensor(_u("x_cq"), [6, 2, L], BF16) as cq,
        nc.sbuf_tensor(_u("x_ck"), [6, 2, L], BF16) as ck,
    ):
        r_xn, r_small = Res(), Res()
        r_cq, r_ck = [Res(), Res()], [Res(), Res()]
        S.dma("sp", small[:], W["small"], writes=[r_small])
        S.op("dve", lambda v: v.tensor_scalar(out=small[:, 0:1], in0=small[:, 0:1], scalar1=128 ** -0.5, scalar2=None, op0=ALU.mult),
             reads=[r_small], writes=[r_small])
        S.op("dve", lambda v: v.tensor_scalar(out=small[0:16, 2:3], in0=small[0:16, 2:3], scalar1=-1.0, scalar2=None, op0=ALU.mult),
             reads=[r_small], writes=[r_small])
        with (
            nc.sbuf_tensor(_u("x_hbig"), [128, 2, L], F32) as hbig,
            nc.sbuf_tensor(_u("x_sqb"), [128, 2, L], BF16) as sqb,
            nc.sbuf_tensor(_u("x_rstd"), [128, L], F32) as rstd,
            nc.sbuf_tensor(_u("x_wf"), [128, KC, 16], BF16) as wf,
            nc.sbuf_tensor(_u("x_e"), [16, L], F32) as ee,
            nc.sbuf_tensor(_u("x_cp"), [16, L], F32) as cp,
            nc.sbuf_tensor(_u("x_r1"), [16, L], F32) as r1,
            nc.sbuf_tensor(_u("x_c6"), [16, 6, L], BF16) as c6,
        ):
            r_hbig, r_sqb, r_rstd = [Res(), Res()], [Res(), Res()], Res()
            r_wf, r_e, r_cp, r_r1, r_c6 = Res(), Res(), Res(), Res(), Res()
            seq_prenorm(S, cx, hT3, cb, xn, r_xn, hbig, r_hbig, sqb, r_sqb, rstd, r_rstd, gi2)
            S.dma("pool", wf[:], W["wf_r"], writes=[r_wf])
            for s in range(NS6):
                b = s % 4
                S.mm_group([lambda t, kc=kc, b=b, s=s: t.matmul(bank[b][0:16, 0:SUB], wf[:, kc, :], xn[:, kc, s * SUB:(s + 1) * SUB],
                                                               start=(kc == 0), stop=(kc == KC - 1)) for kc in range(KC)],
                           reads=[r_wf, r_xn], writes=[r_bank[b]])
                S.op("act", lambda a, b=b, s=s: a.activation(out=ee[:, s * SUB:(s + 1) * SUB], in_=bank[b][0:16, 0:SUB],
                                                             func=AF.Exp, scale=-1.0, bias=small[0:16, 2:3]),
                     reads=[r_bank[b], r_small], writes=[r_e])
            S.op("act", lambda a: a.activation(out=ee[:], in_=ee[:], func=AF.Ln, bias=cx.one_col[0:16, :], scale=1.0),
                 reads=[r_e], writes=[r_e])
            S.op("dve", lambda v: v.tensor_tensor_scan(out=cp[:], data0=cx.ones_f[0:16, 0:1].to_broadcast([16, L]), data1=ee[:],
                                                       initial=0.0, op0=ALU.mult, op1=ALU.add), reads=[r_e], writes=[r_cp])
            S.op("dve", lambda v: v.tensor_copy(out=c6[:, 0, :], in_=cp[:]), reads=[r_cp], writes=[r_c6])
            S.op("dve", lambda v: v.tensor_tensor(out=r1[:], in0=cp[:], in1=c6[:, 0, :], op=ALU.subtract), reads=[r_cp, r_c6], writes=[r_r1])
            S.op("dve", lambda v: v.tensor_copy(out=c6[:, 1, :], in_=r1[:]), reads=[r_r1], writes=[r_c6])
            S.op("dve", lambda v: v.tensor_tensor(out=r1[:], in0=r1[:], in1=c6[:, 1, :], op=ALU.subtract), reads=[r_r1, r_c6], writes=[r_r1])
            S.op("dve", lambda v: v.tensor_copy(out=c6[:, 2, :], in_=r1[:]), reads=[r_r1], writes=[r_c6])
            S.op("dve", lambda v: v.tensor_scalar(out=c6[:, 3:6, :], in0=c6[:, 0:3, :], scalar1=-1.0, scalar2=None, op0=ALU.mult),
                 reads=[r_c6], writes=[r_c6])
            S.dma("sp", scr["c"].rearrange("r h t -> h r t"), c6[:], reads=[r_c6], writes=[cx.r_scr_c])
            S.op("dve", lambda v: v.memset(cq[:], 1.0), writes=r_cq)
            S.op("dve", lambda v: v.memset(ck[:], 1.0), writes=r_ck)
            S.barrier()
        with (
            nc.sbuf_tensor(_u("x_wsl"), [128, 2, 4, KC, 128], BF16) as wsl,
            nc.sbuf_tensor(_u("x_qn"), [128, L], BF16) as qn,
            nc.sbuf_tensor(_u("x_kn"), [128, L], BF16) as kn,
            nc.sbuf_tensor(_u("x_vT"), [128, L], BF16) as vT,
            nc.sbuf_tensor(_u("x_vtok"), [128, NBLK, 128], BF16) as vtok,
            nc.sbuf_tensor(_u("x_sig"), [128, L], F32) as sig,
            nc.sbuf_tensor(_u("x_P"), [128, 4, 512], BF16) as P,
            nc.sbuf_tensor(_u("x_rd"), [128, 512], F32) as rd,
            nc.sbuf_tensor(_u("x_tmp"), [128, 512], F32) as tmp,
            nc.sbuf_tensor(_u("x_oo"), [128, 2, L], BF16) as oo,
            nc.sbuf_tensor(_u("x_sqs"), [128, 2, SUB], BF16) as sqs,
            nc.sbuf_tensor(_u("x_rq"), [128, 2, SUB], F32) as rq,
        ):
            r_wsl = [Res(), Res()]
            r_qn, r_kn, r_vT, r_vtok, r_sig = Res(), Res(), Res(), Res(), Res()
            r_P = [Res() for _ in range(4)]
            r_rd, r_tmp = Res(), Res()
            r_oo = [Res(), Res()]
            r_sqs, r_rq = [Res(), Res()], [Res(), Res()]
            pb_i = [0]

            def pbank():
                b = (0, 1, 2, 7)[pb_i[0] % 4]
                pb_i[0] += 1
                return b
            sq_i = [0]
            p_i = [0]
            for hd in range(HD):
                wb = hd % 2
                S.dma("pool", wsl[:, wb], W["win_r"][:, hd].rearrange("f p k c -> p f k c"), writes=[r_wsl[wb]], max_dma_last_dim=8192)
                S.dma("sp", cq[0:3, wb, :], scr["c"][3:6, hd, :], reads=[cx.r_scr_c], writes=[r_cq[wb]])
                S.dma("sp", ck[3:6, wb, :], scr["c"][0:3, hd, :], reads=[cx.r_scr_c], writes=[r_ck[wb]])
                for which, dst, r_dst, gc in ((0, qn, r_qn, 0), (1, kn, r_kn, 1)):
                    for s in range(NS6):
                        b = pbank()
                        S.mm_group([lambda t, kc=kc, b=b, s=s, which=which: t.matmul(
                            bank[b][:, 0:SUB], wsl[:, wb, which, kc, :], xn[:, kc, s * SUB:(s + 1) * SUB],
                            start=(kc == 0), stop=(kc == KC - 1)) for kc in range(KC)], reads=[r_wsl[wb], r_xn], writes=[r_bank[b]])
                        q = sq_i[0] % 2
                        sq_i[0] += 1
                        S.op("act", lambda a, b=b, q=q: a.activation(out=sqs[:, q, :], in_=bank[b][:, 0:SUB], func=AF.Square),
                             reads=[r_bank[b]], writes=[r_sqs[q]])
                        b2 = pbank()
                        S.mm_group([lambda t, b2=b2, q=q: t.matmul(bank[b2][:, 0:SUB], cx.ones_bf[:], sqs[:, q, :], start=True, stop=True)],
                                   reads=[r_sqs[q]], writes=[r_bank[b2]])
                        S.op("act", lambda a, b2=b2, q=q: a.activation(out=rq[:, q, :], in_=bank[b2][:, 0:SUB], func=AF.Sqrt,
                                                                       scale=1.0 / 128, bias=cx.eps_col[:]),
                             reads=[r_bank[b2]], writes=[r_rq[q]])
                        S.op("dve", lambda v, q=q: v.reciprocal(out=rq[:, q, :], in_=rq[:, q, :]), reads=[r_rq[q]], writes=[r_rq[q]])
                        S.op("dve", lambda v, b=b, q=q, s=s, dst=dst, gc=gc: v.scalar_tensor_tensor(
                            out=dst[:, s * SUB:(s + 1) * SUB], in0=bank[b][:, 0:SUB], scalar=small[:, gc:gc + 1], in1=rq[:, q, :],
                            op0=ALU.mult, op1=ALU.mult), reads=[r_bank[b], r_rq[q], r_small], writes=[r_dst])
                for s in range(NS6):
                    b = pbank()
                    S.mm_group([lambda t, kc=kc, b=b, s=s: t.matmul(bank[b][:, 0:SUB], wsl[:, wb, 2, kc, :], xn[:, kc, s * SUB:(s + 1) * SUB],
                                                                   start=(kc == 0), stop=(kc == KC - 1)) for kc in range(KC)],
                               reads=[r_wsl[wb], r_xn], writes=[r_bank[b]])
                    S.op("act", lambda a, b=b, s=s: a.activation(out=vT[:, s * SUB:(s + 1) * SUB], in_=bank[b][:, 0:SUB], func=AF.Copy),
                         reads=[r_bank[b]], writes=[r_vT])
                    b = pbank()
                    S.mm_group([lambda t, kc=kc, b=b, s=s: t.matmul(bank[b][:, 0:SUB], wsl[:, wb, 3, kc, :], xn[:, kc, s * SUB:(s + 1) * SUB],
                                                                   start=(kc == 0), stop=(kc == KC - 1)) for kc in range(KC)],
                               reads=[r_wsl[wb], r_xn], writes=[r_bank[b]])
                    S.op("act", lambda a, b=b, s=s: a.activation(out=sig[:, s * SUB:(s + 1) * SUB], in_=bank[b][:, 0:SUB], func=AF.Sigmoid),
                         reads=[r_bank[b]], writes=[r_sig])
                for g0 in range(0, NBLK, 8):
                    b = pbank()
                    bb = bank[b][:].bitcast(BF16)
                    blks = list(range(g0, min(NBLK, g0 + 8)))
                    S.mm_group([lambda t, j=j, bb=bb, g0=g0: t.transpose(
                        bb[0:min(128, L - j * 128), (j - g0) * 128:(j - g0 + 1) * 128], vT[:, j * 128:min(L, (j + 1) * 128)], cx.ident_bf[:])
                        for j in blks], reads=[r_vT], writes=[r_bank[b]])
                    nfull = sum(1 for j in blks if (j + 1) * 128 <= L)
                    if nfull:
                        S.op("dve", lambda v, bb=bb, g0=g0, nfull=nfull: v.tensor_copy(
                            out=vtok[:, g0:g0 + nfull, :], in_=bb[:, 0:nfull * 128].rearrange("p (j d) -> p j d", d=128)),
                            reads=[r_bank[b]], writes=[r_vtok])
                    if nfull < len(blks):
                        j = blks[-1]
                        rows = L - j * 128
                        S.op("dve", lambda v, bb=bb, g0=g0, j=j, rows=rows: v.tensor_copy(
                            out=vtok[0:rows, j, :], in_=bb[0:rows, (j - g0) * 128:(j - g0 + 1) * 128]),
                            reads=[r_bank[b]], writes=[r_vtok])
                ob_ = hd % 2
                for qi, (t0, Wd) in enumerate(QT):
                    ao, ad = ((3, 4), (5, 6))[qi % 2]
                    nkb = min(NBLK, (t0 + Wd + 127) // 128)
                    for j in range(nkb):
                        s0 = j * 128
                        rows = min(128, L - s0)
                        c0 = max(0, s0 - t0)
                        diag = s0 >= t0
                        b = pbank()
                        fns = [lambda t, b=b, s0=s0, rows=rows, c0=c0: t.matmul(
                                   bank[b][0:rows, c0:Wd], kn[:, s0:s0 + rows], qn[:, t0 + c0:t0 + Wd], start=True, stop=False),
                               lambda t, b=b, s0=s0, rows=rows, c0=c0: t.matmul(
                                   bank[b][0:rows, c0:Wd], ck[0:6, wb, s0:s0 + rows], cq[0:6, wb, t0 + c0:t0 + Wd], start=False, stop=not diag)]
                        if diag:
                            dw = min(128, Wd - c0)
                            fns.append(lambda t, b=b, rows=rows, c0=c0, dw=dw: t.matmul(
                                bank[b][0:rows, c0:c0 + dw], cx.ident_bf[0:rows, 0:rows], cx.maskneg_bf[0:rows, 0:dw], start=False, stop=True))
                        S.mm_group(fns, reads=[r_kn, r_qn, r_ck[wb], r_cq[wb]], writes=[r_bank[b]])
                        pi = p_i[0] % 4
                        p_i[0] += 1
                        S.op("act", lambda a, b=b, rows=rows, c0=c0, pi=pi: a.activation(
                            out=P[0:rows, pi, c0:Wd], in_=bank[b][0:rows, c0:Wd], func=AF.Exp), reads=[r_bank[b]], writes=[r_P[pi]])
                        S.mm_group([lambda t, rows=rows, c0=c0, pi=pi, j=j: t.matmul(
                                        bank[ao][:, c0:Wd], vtok[0:rows, j, :], P[0:rows, pi, c0:Wd], start=(j == 0), stop=(j == nkb - 1)),
                                    lambda t, rows=rows, c0=c0, pi=pi, j=j: t.matmul(
                                        bank[ad][:, c0:Wd], cx.ones_bf[0:rows, :], P[0:rows, pi, c0:Wd], start=(j == 0), stop=(j == nkb - 1))],
                                   reads=[r_vtok, r_P[pi]], writes=[r_bank[ao], r_bank[ad]])
                    S.op("dve", lambda v, ad=ad: v.reciprocal(out=rd[:, 0:Wd], in_=bank[ad][:, 0:Wd]), reads=[r_bank[ad]], writes=[r_rd])
                    S.op("dve", lambda v, ao=ao: v.tensor_tensor(out=tmp[:, 0:Wd], in0=bank[ao][:, 0:Wd], in1=rd[:, 0:Wd], op=ALU.mult),
                         reads=[r_bank[ao], r_rd], writes=[r_tmp])
                    S.op("dve", lambda v, t0=t0: v.tensor_tensor(out=oo[:, ob_, t0:t0 + Wd], in0=tmp[:, 0:Wd], in1=sig[:, t0:t0 + Wd], op=ALU.mult),
                         reads=[r_tmp, r_sig], writes=[r_oo[ob_]])
                S.dma("sp", scr["og"][hd * 128:(hd + 1) * 128, :], oo[:, ob_, :], reads=[r_oo[ob_]], writes=[cx.r_scr])
            S.barrier()
    mixer_out_phase(S, cx, hT, cb, scr["og"], W["wout_r"], gi3)


WT = 704
NCH = 11
CH = 64
SW = 352
LPAD = 3 * WT
MU, CW_, W0_, A0_, KK_, KA_, RK_, LG_, LB_, NSM = 0, 27, 51, 59, 67, 75, 83, 91, 99, 107
LNX_EPS = 64e-5
DEC = 0.6065306597126334


def even_phase(S, cx, hT, cb, W, gi2, gi3, scr):
    _UID[0] += 1
    nc = S.nc
    hT3 = hT.rearrange("(kc p) t -> kc p t", p=128)
    bank, r_bank = cx.bank, cx.r_bank
    ym = scr["ym"]
    nb_i = [0]

    def nb():
        b = nb_i[0] % 8
        nb_i[0] += 1
        return b

    def V(fn, reads, writes):
        S.op("dve", fn, reads, writes)

    def A(fn, reads, writes):
        S.op("act", fn, reads, writes)

    F = lambda name: nc.sbuf_tensor(name, [128, WT], F32)
    names_f = ["ur", "uk", "uv", "aa", "kkn", "kf", "bs", "lw", "cw", "E", "BtT", "KtT", "BhT", "KhT", "gg", "bonv",
               "t1", "t2", "t3", "Y"]
    with ExitStack() as es:
        xn = es.enter_context(nc.sbuf_tensor(_u("e_xn"), [128, KC, WT], BF16))
        hbuf = es.enter_context(nc.sbuf_tensor(_u("e_hbuf"), [128, 2, WT], F32))
        sq = es.enter_context(nc.sbuf_tensor(_u("e_sq"), [128, 2, WT], BF16))
        rstd = es.enter_context(nc.sbuf_tensor(_u("e_rstd"), [128, WT], F32))
        lowA = es.enter_context(nc.sbuf_tensor(_u("e_lowA"), [128, WT], BF16))
        lowG1 = es.enter_context(nc.sbuf_tensor(_u("e_lowG1"), [128, WT], BF16))
        lowG2 = es.enter_context(nc.sbuf_tensor(_u("e_lowG2"), [32, WT], BF16))
        raw = es.enter_context(nc.sbuf_tensor(_u("e_raw"), [128, 2, WT + 2], F32))
        AR = es.enter_context(nc.sbuf_tensor(_u("e_AR"), [128, 2, WT], F32))
        tok = es.enter_context(nc.sbuf_tensor(_u("e_tok"), [128, 4, NCH, CH], F32))
        G1s = es.enter_context(nc.sbuf_tensor(_u("e_G1s"), [128, NCH, 128], F32))
        G2s = es.enter_context(nc.sbuf_tensor(_u("e_G2s"), [128, NCH, 128], F32))
        NM = es.enter_context(nc.sbuf_tensor(_u("e_NM"), [128, 2, 2, NCH, CH], F32))
        X = es.enter_context(nc.sbuf_tensor(_u("e_X"), [128, NCH, 128], F32))
        GamT = es.enter_context(nc.sbuf_tensor(_u("e_GamT"), [128, NCH, CH], F32))
        WC = es.enter_context(nc.sbuf_tensor(_u("e_WC"), [128, NCH], F32))
        Zb = es.enter_context(nc.sbuf_tensor(_u("e_Z"), [128, 2, CH], F32))
        Zst = es.enter_context(nc.sbuf_tensor(_u("e_Zst"), [128, 8, CH], F32))
        carry = es.enter_context(nc.sbuf_tensor(_u("e_carry"), [128, 27], F32))
        zcarry = es.enter_context(nc.sbuf_tensor(_u("e_zcarry"), [128, 8, 2], F32))
        wsl = es.enter_context(nc.sbuf_tensor(_u("e_wsl"), [128, 2, 3, KC, 128], BF16))
        wlow = es.enter_context(nc.sbuf_tensor(_u("e_wlow"), [128, KC, 288], BF16))
        w2a2 = es.enter_context(nc.sbuf_tensor(_u("e_w2a2"), [128, 1024], BF16))
        g2a = es.enter_context(nc.sbuf_tensor(_u("e_g2a"), [128, 1024], BF16))
        g2b = es.enter_context(nc.sbuf_tensor(_u("e_g2b"), [32, 1024], BF16))
        sm = es.enter_context(nc.sbuf_tensor(_u("e_small"), [128, NSM], F32))
        omm = es.enter_context(nc.sbuf_tensor(_u("e_omm"), [128, 27], F32))
        omka = es.enter_context(nc.sbuf_tensor(_u("e_omka"), [128, 8], F32))
        rmask = es.enter_context(nc.sbuf_tensor(_u("e_rmask"), [128, WT], F32))
        sqs = es.enter_context(nc.sbuf_tensor(_u("e_sqs"), [128, 2, SW], BF16))
        ybf = es.enter_context(nc.sbuf_tensor(_u("e_ybf"), [128, 2, WT], BF16))
        Fall = es.enter_context(nc.sbuf_tensor(_u("e_F"), [128, len(names_f), WT], F32))
        Fd = {n: Fall[:, i, :] for i, n in enumerate(names_f)}
        rF = {n: Res(n) for n in names_f}
        r_xn, r_rstd = Res(), Res()
        r_hbuf, r_sq = [Res(), Res()], [Res(), Res()]
        r_lowA, r_lowG1, r_lowG2 = Res(), Res(), Res()
        r_raw = [Res(), Res()]
        r_AR = Res()
        r_tok = [Res() for _ in range(4)]
        r_G1s, r_G2s = Res(), Res()
        r_NM = [[Res(), Res()], [Res(), Res()]]
        r_X, r_GamT, r_WC = Res(), Res(), Res()
        r_Zb = [Res(), Res()]
        r_Zst, r_carry, r_zcarry = Res(), Res(), Res()
        r_wsl = [Res(), Res()]
        r_wlow, r_w2a2, r_g2, r_sm = Res(), Res(), Res(), Res()
        r_sqs = [Res(), Res()]
        r_ybf = [Res(), Res()]
        S.dma("sp", sm[:], W["esmall"], writes=[r_sm])
        S.dma("pool", wlow[:], W["ew_low_r"], writes=[r_wlow], max_dma_last_dim=8192)
        S.dma("pool", w2a2[:], W["w2a2"], writes=[r_w2a2])
        S.dma("pool", g2a[:], W["g2"][0:128, :], writes=[r_g2])
        S.dma("pool", g2b[:], W["g2"][128:160, :], writes=[r_g2])
        V(lambda v: v.tensor_scalar(out=omm[:], in0=sm[:, MU:MU + 27], scalar1=-1.0, scalar2=1.0, op0=ALU.mult, op1=ALU.add),
          [r_sm], [r_sm])
        V(lambda v: v.tensor_scalar(out=omka[:], in0=sm[:, KA_:KA_ + 8], scalar1=-1.0, scalar2=1.0, op0=ALU.mult, op1=ALU.add),
          [r_sm], [r_sm])
        V(lambda v: v.memset(rmask[:], 1.0), [], [r_sm])
        V(lambda v: v.memset(rmask[:].rearrange("p (c t) -> p c t", t=CH)[:, :, 0:1], 0.0), [], [r_sm])
        V(lambda v: v.memset(carry[:], 0.0), [], [r_carry])
        V(lambda v: v.memset(zcarry[:], 0.0), [], [r_zcarry])
        V(lambda v: v.memset(Zst[:], 0.0), [], [r_Zst])
        sq_i = [0]
        raw_i = [0]
        w_i = [0]
        wq = []

        def load_slab(kind, idx):
            wb = w_i[0] % 2
            w_i[0] += 1
            src = W["ew_conv_r"] if kind == 0 else W["ew_rkv_r"]
            S.dma("pool", wsl[:, wb], src[idx].rearrange("f p k c -> p f k c"), writes=[r_wsl[wb]], max_dma_last_dim=8192)
            return wb

        def proj_bank(lhs_fn, m, s, reads):
            b = nb()
            S.mm_group([lambda t, kc=kc, b=b: t.matmul(bank[b][0:m, 0:SW], lhs_fn(kc), xn[:, kc, s * SW:(s + 1) * SW],
                                                       start=(kc == 0), stop=(kc == KC - 1)) for kc in range(KC)],
                       reads=reads + [r_xn], writes=[r_bank[b]])
            return b

        def proj_shift(lhs_fn, m, reads, ci, out_ap, r_out, first_tile):
            rb = raw_i[0] % 2
            raw_i[0] += 1
            for s in range(2):
                b = proj_bank(lhs_fn, m, s, reads)
                A(lambda a, b=b, s=s: a.activation(out=raw[0:m, rb, 1 + s * SW:1 + (s + 1) * SW], in_=bank[b][0:m, 0:SW], func=AF.Copy),
                  [r_bank[b]], [r_raw[rb]])
            A(lambda a: a.activation(out=raw[0:m, rb, 0:1], in_=carry[0:m, ci:ci + 1], func=AF.Copy), [r_carry], [r_raw[rb]])
            A(lambda a: a.activation(out=Fd["t1"][0:m, :], in_=raw[0:m, rb, 0:WT], func=AF.Copy, scale=sm[0:m, MU + ci:MU + ci + 1]),
              [r_raw[rb], r_sm], [rF["t1"]])
            V(lambda v: v.scalar_tensor_tensor(out=out_ap, in0=raw[0:m, rb, 1:WT + 1], scalar=omm[0:m, ci:ci + 1], in1=Fd["t1"][0:m, :],
                                               op0=ALU.mult, op1=ALU.add), [r_raw[rb], rF["t1"], r_sm], [r_out])
            A(lambda a: a.activation(out=carry[0:m, ci:ci + 1], in_=raw[0:m, rb, WT:WT + 1], func=AF.Copy), [r_raw[rb]], [r_carry])

        def head_sum(src_bf_fn, reads, s):
            b = nb()
            S.mm_group([lambda t, b=b: t.matmul(bank[b][:, 0:SW], cx.blockones_bf[:], src_bf_fn(), start=True, stop=True)],
                       reads=reads, writes=[r_bank[b]])
            return b

        for ti in range(3):
            t0 = ti * WT
            wv = min(WT, LSEQ - t0)
            first = ti == 0
            if wv < WT:
                V(lambda v: v.memset(xn[:, :, wv:WT], 0.0), [], [r_xn])
            for kc in range(KC):
                hb = kc % 2
                S.dma("sp", hbuf[:, hb, 0:wv], hT3[kc, :, cb + t0:cb + t0 + wv], reads=[cx.r_h[kc]], writes=[r_hbuf[hb]])
                A(lambda a, hb=hb: a.activation(out=sq[:, hb, 0:wv], in_=hbuf[:, hb, 0:wv], func=AF.Square), [r_hbuf[hb]], [r_sq[hb]])
                if wv < WT:
                    A(lambda a, hb=hb: a.activation(out=sq[:, hb, wv:WT], in_=xn[:, 0, wv:WT], func=AF.Copy), [r_xn], [r_sq[hb]])
                S.mm_group([lambda t, s=s, hb=hb, kc=kc: t.matmul(bank[s][:, 0:SW], cx.ones_bf[:], sq[:, hb, s * SW:(s + 1) * SW],
                                                                 start=(kc == 0), stop=(kc == KC - 1)) for s in range(2)],
                           reads=[r_sq[hb]], writes=[r_bank[0], r_bank[1]])
            rstd_from_banks(S, cx, [0, 1], rstd, r_rstd, 2, SW, 1.0 / D)
            for kc in range(KC):
                hb = kc % 2
                S.dma("sp", hbuf[:, hb, 0:wv], hT3[kc, :, cb + t0:cb + t0 + wv], reads=[cx.r_h[kc]], writes=[r_hbuf[hb]])
                V(lambda v, hb=hb, kc=kc: v.scalar_tensor_tensor(out=xn[:, kc, 0:wv], in0=hbuf[:, hb, 0:wv],
                                                                 scalar=cx.gcol[:, gi2 + kc:gi2 + kc + 1], in1=rstd[:, 0:wv],
                                                                 op0=ALU.mult, op1=ALU.mult), [r_hbuf[hb], r_rstd], [r_xn])
            proj_shift(lambda kc: wlow[:, kc, 0:128], 128, [r_wlow], 24, Fd["t2"], rF["t2"], first)
            A(lambda a: a.activation(out=lowA[0:64, :], in_=Fd["t2"][0:64, :], func=AF.Tanh), [rF["t2"]], [r_lowA])
            A(lambda a: a.activation(out=lowA[64:128, :], in_=Fd["t2"][64:128, :], func=AF.Copy), [rF["t2"]], [r_lowA])
            proj_shift(lambda kc: wlow[:, kc, 128:256], 128, [r_wlow], 25, Fd["t2"], rF["t2"], first)
            A(lambda a: a.activation(out=lowG1[:], in_=Fd["t2"], func=AF.Sigmoid), [rF["t2"]], [r_lowG1])
            proj_shift(lambda kc: wlow[:, kc, 256:288], 32, [r_wlow], 26, Fd["t2"][0:32, :], rF["t2"], first)
            A(lambda a: a.activation(out=lowG2[:], in_=Fd["t2"][0:32, :], func=AF.Sigmoid), [rF["t2"]], [r_lowG2])
            for jc in range(8):
                wb = load_slab(0, jc)
                gb, gc, zb = Fd["t2"], Fd["t3"], raw
                rb = raw_i[0] % 2
                raw_i[0] += 1
                for s in range(2):
                    b0 = proj_bank(lambda kc: wsl[:, wb, 0, kc, :], 128, s, [r_wsl[wb]])
                    A(lambda a, b0=b0, s=s: a.activation(out=gb[:, s * SW:(s + 1) * SW], in_=bank[b0][:, 0:SW], func=AF.Copy),
                      [r_bank[b0]], [rF["t2"]])
                    b1 = proj_bank(lambda kc: wsl[:, wb, 1, kc, :], 128, s, [r_wsl[wb]])
                    A(lambda a, b1=b1, s=s: a.activation(out=gc[:, s * SW:(s + 1) * SW], in_=bank[b1][:, 0:SW], func=AF.Copy),
                      [r_bank[b1]], [rF["t3"]])
                    b2 = proj_bank(lambda kc: wsl[:, wb, 2, kc, :], 128, s, [r_wsl[wb]])
                    V(lambda v, b2=b2, s=s: v.tensor_tensor(out=zb[:, rb, 2 + s * SW:2 + (s + 1) * SW], in0=gc[:, s * SW:(s + 1) * SW],
                                                            in1=bank[b2][:, 0:SW], op=ALU.mult), [rF["t3"], r_bank[b2]], [r_raw[rb]])
                V(lambda v: v.tensor_copy(out=zb[:, rb, 0:2], in_=zcarry[:, jc, :]), [r_zcarry], [r_raw[rb]])
                A(lambda a: a.activation(out=Fd["t1"], in_=zb[:, rb, 0:WT], func=AF.Copy, scale=sm[:, CW_ + jc:CW_ + jc + 1]),
                  [r_raw[rb], r_sm], [rF["t1"]])
                V(lambda v: v.scalar_tensor_tensor(out=Fd["t1"], in0=zb[:, rb, 1:WT + 1], scalar=sm[:, CW_ + 8 + jc:CW_ + 9 + jc],
                                                   in1=Fd["t1"], op0=ALU.mult, op1=ALU.add), [r_raw[rb], rF["t1"], r_sm], [rF["t1"]])
                V(lambda v: v.scalar_tensor_tensor(out=Fd["t1"], in0=zb[:, rb, 2:WT + 2], scalar=sm[:, CW_ + 16 + jc:CW_ + 17 + jc],
                                                   in1=Fd["t1"], op0=ALU.mult, op1=ALU.add), [r_raw[rb], rF["t1"], r_sm], [rF["t1"]])
                V(lambda v: v.tensor_copy(out=zcarry[:, jc, :], in_=zb[:, rb, WT:WT + 2]), [r_raw[rb]], [r_zcarry])
                yb = jc % 2
                V(lambda v, yb=yb: v.tensor_tensor(out=ybf[:, yb, :], in0=gb, in1=Fd["t1"], op=ALU.mult), [rF["t2"], rF["t1"]], [r_ybf[yb]])
                S.dma("sp", ym[jc * 128:(jc + 1) * 128, t0:t0 + WT], ybf[:, yb, :], reads=[r_ybf[yb]], writes=[cx.r_scr])
            for j in range(8):
                wb = load_slab(1, j)
                proj_shift(lambda kc: wsl[:, wb, 0, kc, :], 128, [r_wsl[wb]], 0 + j, Fd["ur"], rF["ur"], first)
                proj_shift(lambda kc: wsl[:, wb, 1, kc, :], 128, [r_wsl[wb]], 8 + j, Fd["uk"], rF["uk"], first)
                proj_shift(lambda kc: wsl[:, wb, 2, kc, :], 128, [r_wsl[wb]], 16 + j, Fd["uv"], rF["uv"], first)
                cs = slice(j * 128, (j + 1) * 128)
                for s in range(2):
                    ss = slice(s * SW, (s + 1) * SW)
                    b = nb()
                    S.mm_group([lambda t, b=b: t.matmul(bank[b][:, 0:SW], w2a2[0:64, cs], lowA[0:64, ss], start=True, stop=True)],
                               reads=[r_w2a2, r_lowA], writes=[r_bank[b]])
                    A(lambda a, b=b: a.activation(out=Fd["lw"][:, ss], in_=bank[b][:, 0:SW], func=AF.Sigmoid, bias=sm[:, W0_ + j:W0_ + j + 1]),
                      [r_bank[b], r_sm], [rF["lw"]])
                    b = nb()
                    S.mm_group([lambda t, b=b: t.matmul(bank[b][:, 0:SW], w2a2[64:128, cs], lowA[64:128, ss], start=True, stop=True)],
                               reads=[r_w2a2, r_lowA], writes=[r_bank[b]])
                    A(lambda a, b=b: a.activation(out=Fd["aa"][:, ss], in_=bank[b][:, 0:SW], func=AF.Sigmoid, bias=sm[:, A0_ + j:A0_ + j + 1]),
                      [r_bank[b], r_sm], [rF["aa"]])
                    b = nb()
                    S.mm_group([lambda t, b=b: t.matmul(bank[b][:, 0:SW], g2a[:, cs], lowG1[:, ss], start=True, stop=False),
                                lambda t, b=b: t.matmul(bank[b][:, 0:SW], g2b[:, cs], lowG2[:, ss], start=False, stop=True)],
                               reads=[r_g2, r_lowG1, r_lowG2], writes=[r_bank[b]])
                    A(lambda a, b=b: a.activation(out=Fd["gg"][:, ss], in_=bank[b][:, 0:SW], func=AF.Copy), [r_bank[b]], [rF["gg"]])
                    q = sq_i[0] % 2
                    sq_i[0] += 1
                    A(lambda a, q=q: a.activation(out=sqs[:, q, :], in_=Fd["uk"][:, ss], func=AF.Square, scale=sm[:, KK_ + j:KK_ + j + 1]),
                      [rF["uk"], r_sm], [r_sqs[q]])
                    b = head_sum(lambda q=q: sqs[:, q, :], [r_sqs[q]], s)
                    A(lambda a, b=b: a.activation(out=Fd["t2"][:, ss], in_=bank[b][:, 0:SW], func=AF.Sqrt), [r_bank[b]], [rF["t2"]])
                V(lambda v: v.tensor_scalar(out=Fd["lw"], in0=Fd["lw"], scalar1=-DEC, scalar2=None, op0=ALU.mult), [rF["lw"]], [rF["lw"]])
                V(lambda v: v.tensor_scalar(out=Fd["t2"], in0=Fd["t2"], scalar1=1e-12, scalar2=None, op0=ALU.max), [rF["t2"]], [rF["t2"]])
                V(lambda v: v.reciprocal(out=Fd["t2"], in_=Fd["t2"]), [rF["t2"]], [rF["t2"]])
                V(lambda v: v.scalar_tensor_tensor(out=Fd["kkn"], in0=Fd["uk"], scalar=sm[:, KK_ + j:KK_ + j + 1], in1=Fd["t2"],
                                                   op0=ALU.mult, op1=ALU.mult), [rF["uk"], rF["t2"], r_sm], [rF["kkn"]])
                V(lambda v: v.tensor_scalar(out=Fd["t3"], in0=Fd["aa"], scalar1=sm[:, KA_ + j:KA_ + j + 1], scalar2=omka[:, j:j + 1],
                                            op0=ALU.mult, op1=ALU.add), [rF["aa"], r_sm], [rF["t3"]])
                V(lambda v: v.tensor_tensor(out=Fd["kf"], in0=Fd["uk"], in1=Fd["t3"], op=ALU.mult), [rF["uk"], rF["t3"]], [rF["kf"]])
                V(lambda v: v.tensor_tensor(out=Fd["bs"], in0=Fd["kkn"], in1=Fd["aa"], op=ALU.mult), [rF["kkn"], rF["aa"]], [rF["bs"]])
                V(lambda v: v.tensor_tensor(out=Fd["t3"], in0=Fd["ur"], in1=Fd["kf"], op=ALU.mult), [rF["ur"], rF["kf"]], [rF["t3"]])
                for s in range(2):
                    ss = slice(s * SW, (s + 1) * SW)
                    q = sq_i[0] % 2
                    sq_i[0] += 1
                    A(lambda a, q=q, ss=ss: a.activation(out=sqs[:, q, :], in_=Fd["t3"][:, ss], func=AF.Copy, scale=sm[:, RK_ + j:RK_ + j + 1]),
                      [rF["t3"], r_sm], [r_sqs[q]])
                    b = head_sum(lambda q=q: sqs[:, q, :], [r_sqs[q]], s)
                    V(lambda v, b=b, ss=ss: v.tensor_tensor(out=Fd["bonv"][:, ss], in0=Fd["uv"][:, ss], in1=bank[b][:, 0:SW], op=ALU.mult),
                      [rF["uv"], r_bank[b]], [rF["bonv"]])
                V(lambda v: v.tensor_tensor_scan(out=Fd["cw"], data0=rmask[:], data1=Fd["lw"], initial=0.0, op0=ALU.mult, op1=ALU.add),
                  [rF["lw"], r_sm], [rF["cw"]])
                cw3 = Fd["cw"].rearrange("p (c t) -> p c t", t=CH)
                A(lambda a: a.activation(out=WC[:], in_=cw3[:, :, CH - 1], func=AF.Exp), [rF["cw"]], [r_WC])
                V(lambda v: v.tensor_tensor(out=Fd["t3"], in0=Fd["cw"], in1=Fd["lw"], op=ALU.subtract), [rF["cw"], rF["lw"]], [rF["t3"]])
                A(lambda a: a.activation(out=Fd["E"], in_=Fd["t3"], func=AF.Exp), [rF["t3"]], [rF["E"]])
                V(lambda v: v.scalar_tensor_tensor(out=AR[:, 0, :], in0=Fd["kkn"], scalar=-1.0, in1=Fd["E"], op0=ALU.mult, op1=ALU.mult),
                  [rF["kkn"], rF["E"]], [r_AR])
                A(lambda a: a.activation(out=Fd["E"], in_=Fd["cw"], func=AF.Exp), [rF["cw"]], [rF["E"]])
                V(lambda v: v.tensor_tensor(out=AR[:, 1, :], in0=Fd["ur"], in1=Fd["E"], op=ALU.mult), [rF["ur"], rF["E"]], [r_AR])
                A(lambda a: a.activation(out=Fd["E"], in_=Fd["cw"], func=AF.Exp, scale=-1.0), [rF["cw"]], [rF["E"]])
                V(lambda v: v.tensor_tensor(out=Fd["BtT"], in0=Fd["bs"], in1=Fd["E"], op=ALU.mult), [rF["bs"], rF["E"]], [rF["BtT"]])
                V(lambda v: v.tensor_tensor(out=Fd["KtT"], in0=Fd["kf"], in1=Fd["E"], op=ALU.mult), [rF["kf"], rF["E"]], [rF["KtT"]])
                V(lambda v: v.tensor_tensor(out=Fd["t3"].rearrange("p (c t) -> p c t", t=CH), in0=cw3[:, :, CH - 1:CH].to_broadcast([128, NCH, CH]),
                                            in1=cw3, op=ALU.subtract), [rF["cw"]], [rF["t3"]])
                A(lambda a: a.activation(out=Fd["E"], in_=Fd["t3"], func=AF.Exp), [rF["t3"]], [rF["E"]])
                V(lambda v: v.tensor_tensor(out=Fd["BhT"], in0=Fd["bs"], in1=Fd["E"], op=ALU.mult), [rF["bs"], rF["E"]], [rF["BhT"]])
                V(lambda v: v.tensor_tensor(out=Fd["KhT"], in0=Fd["kf"], in1=Fd["E"], op=ALU.mult), [rF["kf"], rF["E"]], [rF["KhT"]])
                srcs = [(AR[:, 0, :], r_AR), (Fd["BhT"], rF["BhT"]), (Fd["KhT"], rF["KhT"]), (Fd["uv"], rF["uv"])]
                for ai, (src, r_src) in enumerate(srcs):
                    for g0 in range(0, NCH, 8):
                        b = nb()
                        cl = list(range(g0, min(NCH, g0 + 8)))
                        S.mm_group([lambda t, b=b, c=c, h=h, src=src, g0=g0: t.matmul(
                            bank[b][h * 64:(h + 1) * 64, (c - g0) * CH:(c - g0 + 1) * CH], src[h * 64:(h + 1) * 64, c * CH:(c + 1) * CH],
                            cx.ident_f[h * 64:(h + 1) * 64, h * 64:(h + 1) * 64], start=True, stop=True) for c in cl for h in range(2)],
                            reads=[r_src], writes=[r_bank[b]])
                        A(lambda a, b=b, g0=g0, n=len(cl), ai=ai: a.activation(
                            out=tok[:, ai, g0:g0 + n, :], in_=bank[b][:, 0:n * CH].rearrange("p (c t) -> p c t", t=CH), func=AF.Copy),
                          [r_bank[b]], [r_tok[ai]])
                for which, dstG, r_dstG, lhs in ((0, G1s, r_G1s, Fd["BtT"]), (1, G2s, r_G2s, Fd["KtT"])):
                    for g0 in range(0, NCH, 4):
                        b = nb()
                        cl = list(range(g0, min(NCH, g0 + 4)))
                        S.mm_group([lambda t, b=b, c=c, h=h, g0=g0, lhs=lhs: t.matmul(
                            bank[b][h * 64:(h + 1) * 64, (c - g0) * 128:(c - g0 + 1) * 128], lhs[h * 64:(h + 1) * 64, c * CH:(c + 1) * CH],
                            AR[h * 64:(h + 1) * 64, :, c * CH:(c + 1) * CH], start=True, stop=True) for c in cl for h in range(2)],
                            reads=[rF["BtT"], rF["KtT"], r_AR], writes=[r_bank[b]])
                        V(lambda v, b=b, g0=g0, n=len(cl), dstG=dstG: v.tensor_tensor(
                            out=dstG[:, g0:g0 + n, :], in0=bank[b][:, 0:n * 128].rearrange("p (c t) -> p c t", t=128),
                            in1=cx.maskG[:].unsqueeze(1).to_broadcast([128, n, 128]), op=ALU.mult), [r_bank[b]], [r_dstG])
                pp = 0
                for g0 in range(0, NCH, 8):
                    b = nb()
                    cl = list(range(g0, min(NCH, g0 + 8)))
                    S.mm_group([lambda t, b=b, c=c, h=h, g0=g0: t.matmul(
                        bank[b][h * 64:(h + 1) * 64, (c - g0) * CH:(c - g0 + 1) * CH], AR[h * 64:(h + 1) * 64, 0, c * CH:(c + 1) * CH],
                        Fd["BtT"][h * 64:(h + 1) * 64, c * CH:(c + 1) * CH], start=True, stop=True) for c in cl for h in range(2)],
                        reads=[rF["BtT"], r_AR], writes=[r_bank[b]])
                    V(lambda v, b=b, g0=g0, n=len(cl): v.tensor_tensor(
                        out=NM[:, 0, 1, g0:g0 + n, :], in0=bank[b][:, 0:n * CH].rearrange("p (c t) -> p c t", t=CH),
                        in1=cx.mask3[:].unsqueeze(1).to_broadcast([128, n, CH]), op=ALU.mult), [r_bank[b]], [r_NM[0][1]])
                A(lambda a: a.activation(out=NM[:, 0, 0, :, :], in_=G1s[:, :, 0:CH], func=AF.Copy), [r_G1s], [r_NM[0][0]])
                A(lambda a: a.activation(out=X[:, :, 0:CH], in_=tok[:, 0, :, :], func=AF.Copy), [r_tok[0]], [r_X])
                for g0 in range(0, NCH, 8):
                    b = nb()
                    cl = list(range(g0, min(NCH, g0 + 8)))
                    S.mm_group([lambda t, b=b, c=c, h=h, g0=g0: t.matmul(
                        bank[b][h * 64:(h + 1) * 64, (c - g0) * CH:(c - g0 + 1) * CH], G2s[h * 64:(h + 1) * 64, c, 0:CH],
                        tok[h * 64:(h + 1) * 64, 3, c, :], start=True, stop=True) for c in cl for h in range(2)],
                        reads=[r_G2s, r_tok[3]], writes=[r_bank[b]])
                    A(lambda a, b=b, g0=g0, n=len(cl): a.activation(
                        out=X[:, g0:g0 + n, CH:128], in_=bank[b][:, 0:n * CH].rearrange("p (c t) -> p c t", t=CH), func=AF.Copy),
                      [r_bank[b]], [r_X])
                for it in range(6):
                    Ncur, Mcur = NM[:, pp, 0], NM[:, pp, 1]
                    for g0 in range(0, NCH, 4):
                        b = nb()
                        cl = list(range(g0, min(NCH, g0 + 4)))
                        S.mm_group([lambda t, b=b, c=c, h=h, g0=g0, Ncur=Ncur: t.matmul(
                            bank[b][h * 64:(h + 1) * 64, (c - g0) * 128:(c - g0 + 1) * 128], Ncur[h * 64:(h + 1) * 64, c, :],
                            X[h * 64:(h + 1) * 64, c, :], start=True, stop=True) for c in cl for h in range(2)],
                            reads=[r_NM[pp][0], r_X], writes=[r_bank[b]])
                        V(lambda v, b=b, g0=g0, n=len(cl): v.tensor_tensor(
                            out=X[:, g0:g0 + n, :], in0=X[:, g0:g0 + n, :], in1=bank[b][:, 0:n * 128].rearrange("p (c t) -> p c t", t=128),
                            op=ALU.add), [r_bank[b], r_X], [r_X])
                    if it < 5:
                        for which in range(2):
                            lhs, rhs = (Mcur, Ncur) if which == 0 else (Ncur, Mcur)
                            for g0 in range(0, NCH, 8):
                                b = nb()
                                cl = list(range(g0, min(NCH, g0 + 8)))
                                S.mm_group([lambda t, b=b, c=c, h=h, g0=g0, lhs=lhs, rhs=rhs: t.matmul(
                                    bank[b][h * 64:(h + 1) * 64, (c - g0) * CH:(c - g0 + 1) * CH], lhs[h * 64:(h + 1) * 64, c, :],
                                    rhs[h * 64:(h + 1) * 64, c, :], start=True, stop=True) for c in cl for h in range(2)],
                                    reads=[r_NM[pp][0], r_NM[pp][1]], writes=[r_bank[b]])
                                A(lambda a, b=b, g0=g0, n=len(cl), which=which, pp=pp: a.activation(
                                    out=NM[:, 1 - pp, which, g0:g0 + n, :], in_=bank[b][:, 0:n * CH].rearrange("p (c t) -> p c t", t=CH),
                                    func=AF.Copy), [r_bank[b]], [r_NM[1 - pp][which]])
                        pp = 1 - pp
                for g0 in range(0, NCH, 8):
                    b = nb()
                    cl = list(range(g0, min(NCH, g0 + 8)))
                    S.mm_group([lambda t, b=b, c=c, h=h, g0=g0: t.matmul(
                        bank[b][h * 64:(h + 1) * 64, (c - g0) * CH:(c - g0 + 1) * CH], X[h * 64:(h + 1) * 64, c, 0:CH],
                        G1s[h * 64:(h + 1) * 64, c, CH:128], start=True, stop=True) for c in cl for h in range(2)],
                        reads=[r_X, r_G1s], writes=[r_bank[b]])
                    V(lambda v, b=b, g0=g0, n=len(cl): v.tensor_tensor(
                        out=AR[:, 1, g0 * CH:(g0 + n) * CH], in0=AR[:, 1, g0 * CH:(g0 + n) * CH], in1=bank[b][:, 0:n * CH], op=ALU.add),
                      [r_bank[b], r_AR], [r_AR])
                    b = nb()
                    S.mm_group([lambda t, b=b, c=c, h=h, g0=g0: t.matmul(
                        bank[b][h * 64:(h + 1) * 64, (c - g0) * CH:(c - g0 + 1) * CH], X[h * 64:(h + 1) * 64, c, 0:CH],
                        tok[h * 64:(h + 1) * 64, 1, c, :], start=True, stop=True) for c in cl for h in range(2)],
                        reads=[r_X, r_tok[1]], writes=[r_bank[b]])
                    for c in cl:
                        V(lambda v, b=b, c=c, g0=g0: v.scalar_tensor_tensor(
                            out=GamT[:, c, :], in0=cx.identstack[:], scalar=WC[:, c:c + 1], in1=bank[b][:, (c - g0) * CH:(c - g0 + 1) * CH],
                            op0=ALU.mult, op1=ALU.add), [r_bank[b], r_WC], [r_GamT])
                V(lambda v: v.tensor_copy(out=Zb[:, 0, :], in_=Zst[:, j, :]), [r_Zst], [r_Zb[0]])
                zp = 0
                for c in range(NCH):
                    b = nb()
                    fns = []
                    for h in range(2):
                        hs = slice(h * 64, (h + 1) * 64)
                        fns += [lambda t, b=b, hs=hs, c=c, zp=zp: t.matmul(bank[b][hs, 0:CH], Zb[hs, zp, :], AR[hs, 1, c * CH:(c + 1) * CH],
                                                                         start=True, stop=False),
                                lambda t, b=b, hs=hs, c=c: t.matmul(bank[b][hs, 0:CH], X[hs, c, CH:128], G1s[hs, c, CH:128], start=False, stop=False),
                                lambda t, b=b, hs=hs, c=c: t.matmul(bank[b][hs, 0:CH], tok[hs, 3, c, :], G2s[hs, c, CH:128], start=False, stop=True)]
                    S.mm_group(fns, reads=[r_Zb[zp], r_AR, r_X, r_G1s, r_G2s, r_tok[3]], writes=[r_bank[b]])
                    A(lambda a, b=b, c=c: a.activation(out=Fd["Y"][:, c * CH:(c + 1) * CH], in_=bank[b][:, 0:CH], func=AF.Copy),
                      [r_bank[b]], [rF["Y"]])
                    b = nb()
                    fns = []
                    for h in range(2):
                        hs = slice(h * 64, (h + 1) * 64)
                        fns += [lambda t, b=b, hs=hs, c=c, zp=zp: t.matmul(bank[b][hs, 0:CH], GamT[hs, c, :], Zb[hs, zp, :], start=True, stop=False),
                                lambda t, b=b, hs=hs, c=c: t.matmul(bank[b][hs, 0:CH], tok[hs, 1, c, :], X[hs, c, CH:128], start=False, stop=False),
                                lambda t, b=b, hs=hs, c=c: t.matmul(bank[b][hs, 0:CH], tok[hs, 2, c, :], tok[hs, 3, c, :], start=False, stop=True)]
                    S.mm_group(fns, reads=[r_Zb[zp], r_GamT, r_X, r_tok[1], r_tok[2], r_tok[3]], writes=[r_bank[b]])
                    V(lambda v, b=b, zp=zp: v.tensor_copy(out=Zb[:, 1 - zp, :], in_=bank[b][:, 0:CH]), [r_bank[b]], [r_Zb[1 - zp]])
                    zp = 1 - zp
                V(lambda v, zp=zp: v.tensor_copy(out=Zst[:, j, :], in_=Zb[:, zp, :]), [r_Zb[zp]], [r_Zst])
                yb = j % 2
                for s in range(2):
                    ss = slice(s * SW, (s + 1) * SW)
                    b1 = nb()
                    S.mm_group([lambda t, b1=b1, ss=ss: t.matmul(bank[b1][:, 0:SW], cx.blockones_f[:], Fd["Y"][:, ss], start=True, stop=True)],
                               reads=[rF["Y"]], writes=[r_bank[b1]])
                    A(lambda a, ss=ss: a.activation(out=Fd["t1"][:, ss], in_=Fd["Y"][:, ss], func=AF.Square), [rF["Y"]], [rF["t1"]])
                    b2 = nb()
                    S.mm_group([lambda t, b2=b2, ss=ss: t.matmul(bank[b2][:, 0:SW], cx.blockones_f[:], Fd["t1"][:, ss], start=True, stop=True)],
                               reads=[rF["t1"]], writes=[r_bank[b2]])
                    A(lambda a, b1=b1, ss=ss: a.activation(out=Fd["t2"][:, ss], in_=bank[b1][:, 0:SW], func=AF.Copy, scale=1.0 / 64),
                      [r_bank[b1]], [rF["t2"]])
                    V(lambda v, ss=ss: v.tensor_tensor(out=Fd["t3"][:, ss], in0=Fd["Y"][:, ss], in1=Fd["t2"][:, ss], op=ALU.subtract),
                      [rF["Y"], rF["t2"]], [rF["t3"]])
                    V(lambda v, ss=ss: v.tensor_tensor(out=Fd["t2"][:, ss], in0=Fd["t2"][:, ss], in1=Fd["t2"][:, ss], op=ALU.mult),
                      [rF["t2"]], [rF["t2"]])
                    V(lambda v, b2=b2, ss=ss: v.scalar_tensor_tensor(out=Fd["E"][:, ss], in0=bank[b2][:, 0:SW], scalar=1.0 / 64, in1=Fd["t2"][:, ss],
                                                                     op0=ALU.mult, op1=ALU.subtract), [r_bank[b2], rF["t2"]], [rF["E"]])
                    A(lambda a, ss=ss: a.activation(out=Fd["E"][:, ss], in_=Fd["E"][:, ss], func=AF.Sqrt, bias=cx.lnx_eps_col[:]), [rF["E"]], [rF["E"]])
                V(lambda v: v.reciprocal(out=Fd["E"], in_=Fd["E"]), [rF["E"]], [rF["E"]])
                V(lambda v: v.tensor_tensor(out=Fd["t3"], in0=Fd["t3"], in1=Fd["E"], op=ALU.mult), [rF["t3"], rF["E"]], [rF["t3"]])
                V(lambda v: v.tensor_scalar(out=Fd["t3"], in0=Fd["t3"], scalar1=sm[:, LG_ + j:LG_ + j + 1], scalar2=sm[:, LB_ + j:LB_ + j + 1],
                                            op0=ALU.mult, op1=ALU.add), [rF["t3"], r_sm], [rF["t3"]])
                V(lambda v: v.tensor_tensor(out=Fd["t3"], in0=Fd["t3"], in1=Fd["bonv"], op=ALU.add), [rF["t3"], rF["bonv"]], [rF["t3"]])
                V(lambda v, yb=yb: v.tensor_tensor(out=ybf[:, yb, :], in0=Fd["t3"], in1=Fd["gg"], op=ALU.mult), [rF["t3"], rF["gg"]], [r_ybf[yb]])
                S.dma("sp", ym[1024 + j * 128:1024 + (j + 1) * 128, t0:t0 + WT], ybf[:, yb, :], reads=[r_ybf[yb]], writes=[cx.r_scr])
        S.barrier()
    mixer_out_phase(S, cx, hT, cb, ym[:, 0:LSEQ], W["wout_r"], gi3)


DEPTH = 4
N_META = 16


def _consts_np():
    c = np.zeros((6, 128, 128), np.float32)
    c[0] = np.eye(128)
    r = np.arange(128)[:, None]
    cc = np.arange(128)[None, :]
    c[1] = np.where(cc < r, -30000.0, 0.0)
    c[2, :64, :64] = 1
    c[2, 64:, 64:] = 1
    s = (np.arange(128) % 64)[:, None]
    t = (np.arange(128) % 64)[None, :]
    c[3] = np.where(np.arange(128)[None, :] < 64, t > s, t >= s).astype(np.float32)
    c[4, :, :64] = (np.arange(64)[None, :] < s).astype(np.float32)
    c[5, :, :64] = (np.arange(64)[None, :] == s).astype(np.float32)
    return c


def _prep_w_in(w):
    wg = w[:, :DFF].reshape(KC, 128, NJ, 128)
    wu = w[:, DFF:].reshape(KC, 128, NJ, 128)
    return np.ascontiguousarray(np.concatenate([wg, wu], axis=3).transpose(2, 1, 0, 3))


def _prep_w_out(w, K):
    return np.ascontiguousarray(w.reshape(K, 128, KC, 128).transpose(2, 1, 0, 3))


def _prep_even(e_w_in, e_conv_w, e_mu, e_w0, e_w2, e_a0, e_a2, e_g2, e_k_k, e_k_a, e_r_k, e_lnx_g, e_lnx_b, e_w_out):
    def slabs(cols0, n):
        w = e_w_in[:, cols0:cols0 + n * 128].reshape(KC, 128, n, 128)
        return w.transpose(2, 1, 0, 3)
    conv = np.stack([slabs(0, 8), slabs(1024, 8), slabs(2048, 8)], axis=1)
    rkv = np.stack([slabs(3072, 8), slabs(4096, 8), slabs(5120, 8)], axis=1)
    low = np.ascontiguousarray(e_w_in[:, 6144:6432].reshape(KC, 128, 288).transpose(1, 0, 2))
    sm = np.zeros((128, NSM), np.float32)
    col = lambda v: v.reshape(-1, 128).T
    sm[:, MU:MU + 24] = col(e_mu[:3072])
    sm[:64, MU + 24] = e_mu[3072:3136]
    sm[64:, MU + 24] = e_mu[3136:3200]
    sm[:, MU + 25] = e_mu[3200:3328]
    sm[:32, MU + 26] = e_mu[3328:3360]
    for jj in range(3):
        sm[:, CW_ + 8 * jj:CW_ + 8 * jj + 8] = col(e_conv_w[jj])
    for base, v in ((W0_, e_w0), (A0_, e_a0), (KK_, e_k_k), (KA_, e_k_a), (RK_, e_r_k), (LG_, e_lnx_g), (LB_, e_lnx_b)):
        sm[:, base:base + 8] = col(v)
    return dict(ew_conv_r=np.ascontiguousarray(conv), ew_rkv_r=np.ascontiguousarray(rkv), ew_low_r=low, esmall=sm,
                w2a2=np.ascontiguousarray(np.concatenate([e_w2, e_a2], axis=0)), g2=np.ascontiguousarray(e_g2),
                wout_r=_prep_w_out(e_w_out, KC))


def _prep_fox(w_in, b_f, q_g, k_g, w_out):
    win_r = np.ascontiguousarray(w_in[:, :4 * D].reshape(KC, 128, 4, 16, 128).transpose(2, 3, 1, 0, 4))
    wf_r = np.ascontiguousarray(w_in[:, 4 * D:].reshape(KC, 128, 16).transpose(1, 0, 2))
    small = np.zeros((128, 4), np.float32)
    small[:, 0] = q_g
    small[:, 1] = k_g
    small[:16, 2] = b_f
    return dict(win_r=win_r, wf_r=wf_r, small=small, wout_r=_prep_w_out(w_out, KC))


EVEN_SHAPES = dict(ew_conv_r=[8, 3, 128, 16, 128], ew_rkv_r=[8, 3, 128, 16, 128], ew_low_r=[128, 16, 288], esmall=[128, NSM],
                   w2a2=[128, 1024], g2=[160, 1024], wout_r=[16, 128, 16, 128])
FOX_SHAPES = dict(win_r=[4, 16, 128, 16, 128], wf_r=[128, 16, 16], small=[128, 4], wout_r=[16, 128, 16, 128])


def build_program():
    nc = bass.Bass("TRN2", target_bir_lowering=False)
    ext = lambda name, shape: nc.dram_tensor(name, shape, F32, kind="ExternalInput").ap()
    h0 = ext("h0", [D, TC])
    gcol = ext("gcol", [128, DEPTH * 6 * KC])
    consts = ext("consts", [6, 128, 128])
    ffn_in = ext("ffn_in_r", [DEPTH * 2, NJ, 128, KC, 256])
    ffn_out = ext("ffn_out_r", [DEPTH * 2, KC, 128, NJ, 128])
    EW = [{k: ext("e%d_%s" % (i, k), v) for k, v in EVEN_SHAPES.items()} for i in range(2)]
    OW = [{k: ext("o%d_%s" % (i, k), v) for k, v in FOX_SHAPES.items()} for i in range(2)]
    hT = nc.dram_tensor("hT", [D, TC], F32, kind="ExternalOutput").ap()
    scr = {"c": nc.dram_tensor("scr_c", [6, 16, LSEQ], BF16, kind="Internal").ap(),
           "og": nc.dram_tensor("scr_og", [D, LSEQ], BF16, kind="Internal").ap(),
           "ym": nc.dram_tensor("scr_ym", [D, LPAD], BF16, kind="Internal").ap()}
    S = Sched(nc)
    cx = make_ctx(S, gcol, DEPTH * 6 * KC, consts)
    S.dma("sp", hT, h0, writes=cx.r_h)
    for l in range(DEPTH):
        gi = lambda i: (l * 6 + i) * KC
        ffn_phase(S, cx, hT, ffn_in[l * 2], ffn_out[l * 2], gi(0), gi(1), TC // TT)
        for seq in range(NSEQ):
            if l % 2 == 0:
                even_phase(S, cx, hT, seq * LSEQ, EW[l // 2], gi(2), gi(3), scr)
            else:
                fox_phase(S, cx, hT, seq * LSEQ, OW[l // 2], gi(2), gi(3), scr)
        ffn_phase(S, cx, hT, ffn_in[l * 2 + 1], ffn_out[l * 2 + 1], gi(4), gi(5), TC // TT)
    S.finish()
    return nc, S


def kernel(x, meta, norm_g, ffn_in, ffn_out, e_w_in, e_conv_w, e_mu, e_w0, e_w2, e_a0, e_a2, e_g2, e_k_k, e_k_a, e_r_k,
           e_lnx_g, e_lnx_b, e_w_out, o_w_in, o_b_f, o_q_g, o_k_g, o_w_out):
    f = lambda a: np.asarray(a, dtype=np.float32)
    x, meta, norm_g = f(x), f(meta), f(norm_g)
    shared = {}
    shared["gcol"] = np.ascontiguousarray(norm_g.reshape(DEPTH * 6, KC, 128).transpose(2, 0, 1).reshape(128, DEPTH * 6 * KC))
    shared["consts"] = _consts_np()
    fi, fo = f(ffn_in), f(ffn_out)
    shared["ffn_in_r"] = np.stack([_prep_w_in(fi[l, k]) for l in range(DEPTH) for k in range(2)])
    shared["ffn_out_r"] = np.stack([_prep_w_out(fo[l, k], NJ) for l in range(DEPTH) for k in range(2)])
    ev = [e_w_in, e_conv_w, e_mu, e_w0, e_w2, e_a0, e_a2, e_g2, e_k_k, e_k_a, e_r_k, e_lnx_g, e_lnx_b, e_w_out]
    for i in range(2):
        for k, v in _prep_even(*[f(a)[i] for a in ev]).items():
            shared["e%d_%s" % (i, k)] = v
        for k, v in _prep_fox(f(o_w_in)[i], f(o_b_f)[i], f(o_q_g)[i], f(o_k_g)[i], f(o_w_out)[i]).items():
            shared["o%d_%s" % (i, k)] = v
    in_maps = []
    for c in range(NCORES):
        hs = [np.concatenate([meta, x[c * NSEQ + s]], axis=0) for s in range(NSEQ)]
        h0 = np.ascontiguousarray(np.concatenate(hs, axis=0).T)
        m = dict(shared)
        m["h0"] = h0
        in_maps.append(m)
    nc, _ = build_program()
    res = run_bass_kernel_spmd(nc, in_maps, core_ids=list(range(NCORES)))
    out = np.empty((NCORES * NSEQ, LSEQ - N_META, D), np.float32)
    for c in range(NCORES):
        hT = res.results[c]["hT"]
        for s in range(NSEQ):
            out[c * NSEQ + s] = hT[:, s * LSEQ + N_META:(s + 1) * LSEQ].T
    return out
```

```python
from contextlib import ExitStack
import numpy as np
import concourse.bass as bass
import concourse.mybir as mybir
from concourse.bass_utils import run_bass_kernel_spmd

F32 = mybir.dt.float32
BF16 = mybir.dt.bfloat16
AF = mybir.ActivationFunctionType
ALU = mybir.AluOpType
AX = mybir.AxisListType

D = 2048
KC = 16
DFF = 5632
NJ = 44
NSEQ = 2
LSEQ = 2064
TC = NSEQ * LSEQ
NCORES = 8
EPS = 1e-6


class Res:
    __slots__ = ("name", "w", "r")

    def __init__(self, name=""):
        self.name = name
        self.w = None
        self.r = {}


class Sched:
    ENG = ("pe", "act", "dve", "pool", "sp")

    def __init__(self, nc, n_dma_sems=40):
        self.nc = nc
        self.eng = {"pe": nc.tensor, "act": nc.scalar, "dve": nc.vector, "pool": nc.gpsimd, "sp": nc.sync}
        self.sem = {}
        self.cnt = {}
        self.seen = {e: {} for e in self.ENG}
        for e in self.ENG:
            self.sem[e] = nc.alloc_semaphore(name="sem_" + e)
            self.cnt[e] = 0
        self.n_dma = n_dma_sems
        for i in range(n_dma_sems):
            a = ("dma", i)
            self.sem[a] = nc.alloc_semaphore(name="sem_dma%d" % i)
            self.cnt[a] = 0
        self.dma_rr = 0
        self.n_inst = 0

    def _need(self, reads, writes):
        need = {}
        for r in reads:
            if r.w is not None:
                a, c = r.w
                if need.get(a, 0) < c:
                    need[a] = c
        for w in writes:
            if w.w is not None:
                a, c = w.w
                if need.get(a, 0) < c:
                    need[a] = c
            for a, c in w.r.items():
                if need.get(a, 0) < c:
                    need[a] = c
        return need

    def _wait(self, e, need):
        seen = self.seen[e]
        eng = self.eng[e]
        for a, c in need.items():
            if a == e and e == "pe":
                continue
            if seen.get(a, 0) < c:
                eng.wait_ge(self.sem[a], c)
                seen[a] = c
                self.n_inst += 1

    def op(self, e, fn, reads=(), writes=()):
        need = self._need(reads, writes)
        self._wait(e, need)
        ins = fn(self.eng[e])
        ins.then_inc(self.sem[e], 1)
        self.cnt[e] += 1
        self.n_inst += 1
        c = self.cnt[e]
        for r in reads:
            r.r[e] = c
        for w in writes:
            w.w = (e, c)
            w.r = {}
        return ins

    def mm_group(self, fns, reads=(), writes=()):
        need = self._need(reads, writes)
        self._wait("pe", need)
        ins = None
        for fn in fns:
            ins = fn(self.nc.tensor)
            self.n_inst += 1
        ins.then_inc(self.sem["pe"], 1)
        self.cnt["pe"] += 1
        c = self.cnt["pe"]
        for r in reads:
            r.r["pe"] = c
        for w in writes:
            w.w = ("pe", c)
            w.r = {}

    def dma(self, e, out, in_, reads=(), writes=(), **kw):
        i = self.dma_rr
        self.dma_rr = (self.dma_rr + 1) % self.n_dma
        a = ("dma", i)
        need = self._need(reads, writes)
        if self.cnt[a] > 0 and need.get(a, 0) < self.cnt[a]:
            need[a] = self.cnt[a]
        self._wait(e, need)
        ins = self.eng[e].dma_start(out=out, in_=in_, **kw)
        ins.then_inc(self.sem[a], 16)
        self.cnt[a] += 16
        self.n_inst += 1
        c = self.cnt[a]
        for r in reads:
            r.r[a] = c
        for w in writes:
            w.w = (a, c)
            w.r = {}
        return ins

    def barrier(self):
        tot = dict(self.cnt)
        for e in self.ENG:
            self._wait(e, {a: c for a, c in tot.items() if c > 0 and not (a == e)})

    def finish(self):
        tot = {a: c for a, c in self.cnt.items() if c > 0 and a != "sp"}
        self._wait("sp", tot)


TT = 688
SUB = 344
NSUB = TT // SUB


class Ctx:
    pass


_UID = [0]


def _u(n):
    return "%s_%d" % (n, _UID[0])

def rstd_from_banks(S, cx, ps_banks, dst, r_dst, nsub, sub, inv_n, eps_col=None):
    eps_col = cx.eps_col if eps_col is None else eps_col
    for s in range(nsub):
        b = ps_banks[s]
        S.op("act", lambda a, b=b, s=s: a.activation(
            out=dst[:, s * sub:(s + 1) * sub], in_=cx.bank[b][:, 0:sub], func=AF.Sqrt, scale=inv_n, bias=eps_col[:]),
            reads=[cx.r_bank[b]], writes=[r_dst])
    S.op("dve", lambda v: v.reciprocal(out=dst[:, 0:nsub * sub], in_=dst[:, 0:nsub * sub]), reads=[r_dst], writes=[r_dst])


class OutBufs:
    def __init__(self, nc, pfx, K):
        self.nc, self.pfx, self.K = nc, pfx, K

    def __enter__(self):
        nc, p, K = self.nc, self.pfx, self.K
        self._cms = [nc.sbuf_tensor(_u(p + "_rstd2"), [128, TT], F32), nc.sbuf_tensor(_u(p + "_y"), [128, KC, TT], F32),
                     nc.sbuf_tensor(_u(p + "_t1"), [128, 2, TT], F32), nc.sbuf_tensor(_u(p + "_wout"), [128, 2, K, 128], BF16)]
        self.rstd2, self.y, self.t1, self.wout = [c.__enter__() for c in self._cms]
        self.r_rstd2 = Res()
        self.r_y = [Res() for _ in range(KC)]
        self.r_t1 = [Res() for _ in range(2)]
        self.r_wout = [Res() for _ in range(2)]
        self.hb_i = [0]
        self.wo_i = [0]
        self.sq_i = [0]
        return self

    def __exit__(self, *a):
        for c in reversed(self._cms):
            c.__exit__(*a)


def out_proj_D(S, cx, ob, src, r_src, K, w_r, next_bank):
    bank, r_bank = cx.bank, cx.r_bank
    y, wout, rstd2, sq = ob.y, ob.wout, ob.rstd2, ob.sq
    pending = []
    for n in range(KC):
        wo = ob.wo_i[0] % 2
        ob.wo_i[0] += 1
        S.dma("pool", wout[:, wo], w_r[n], writes=[ob.r_wout[wo]], max_dma_last_dim=8192)
        for s in range(NSUB):
            b = next_bank()
            S.mm_group([lambda t, k=k, b=b, s=s, wo=wo: t.matmul(
                bank[b][:, 0:SUB], wout[:, wo, k, :], src[:, k, s * SUB:(s + 1) * SUB],
                start=(k == 0), stop=(k == K - 1)) for k in range(K)],
                reads=[ob.r_wout[wo]] + list(r_src), writes=[r_bank[b]])
            S.op("act", lambda a, b=b, n=n, s=s: a.activation(out=y[:, n, s * SUB:(s + 1) * SUB],
                                                             in_=bank[b][:, 0:SUB], func=AF.Copy),
                 reads=[r_bank[b]], writes=[ob.r_y[n]])
            q = ob.sq_i[0] % 2
            ob.sq_i[0] += 1
            S.op("act", lambda a, b=b, q=q: a.activation(out=sq[:, q, 0:SUB], in_=bank[b][:, 0:SUB], func=AF.Square),
                 reads=[r_bank[b]], writes=[ob.r_sq[q]])
            if pending:
                pending.pop()()
            pending.append(lambda s=s, q=q, n=n: S.mm_group(
                [lambda t: t.matmul(bank[6 + s][:, 0:SUB], cx.ones_bf[:], sq[:, q, 0:SUB], start=(n == 0), stop=(n == KC - 1))],
                reads=[ob.r_sq[q]], writes=[r_bank[6 + s]]))
    pending.pop()()
    rstd_from_banks(S, cx, [6, 7], rstd2, ob.r_rstd2, NSUB, SUB, 1.0 / D)


def out_proj_E_step(S, cx, ob, hT3, t0, gain, gi, n):
    y, t1, rstd2, hbuf = ob.y, ob.t1, ob.rstd2, ob.hbuf
    r_h = cx.r_h
    hb = ob.hb_i[0] % 3
    ob.hb_i[0] += 1
    S.dma("sp", hbuf[:, hb, :], hT3[n, :, t0:t0 + TT], reads=[r_h[n]], writes=[ob.r_hbuf[hb]])
    q = n % 2
    S.op("dve", lambda v: v.scalar_tensor_tensor(
        out=t1[:, q, :], in0=y[:, n, :], scalar=gain[:, gi + n:gi + n + 1], in1=rstd2[:],
        op0=ALU.mult, op1=ALU.mult), reads=[ob.r_y[n], ob.r_rstd2], writes=[ob.r_t1[q]])
    S.op("dve", lambda v: v.tensor_tensor(out=t1[:, q, :], in0=t1[:, q, :], in1=hbuf[:, hb, :], op=ALU.add),
         reads=[ob.r_t1[q], ob.r_hbuf[hb]], writes=[ob.r_t1[q]])
    S.dma("sp", hT3[n, :, t0:t0 + TT], t1[:, q, :], reads=[ob.r_t1[q]], writes=[r_h[n]])


def out_proj_tile(S, cx, ob, hT3, t0, src, r_src, K, w_r, gain, gi, next_bank):
    out_proj_D(S, cx, ob, src, r_src, K, w_r, next_bank)
    for n in range(KC):
        out_proj_E_step(S, cx, ob, hT3, t0, gain, gi, n)


def ffn_phase(S, cx, hT, w_in_r, w_out_r, gi0, gi1, n_tiles):
    _UID[0] += 1
    nc = S.nc
    hT3 = hT.rearrange("(kc p) t -> kc p t", p=128)
    with (
        nc.sbuf_tensor(_u("f_hbuf"), [128, 3, TT], F32) as hbuf,
        nc.sbuf_tensor(_u("f_sq"), [128, 2, TT], BF16) as sq,
        nc.sbuf_tensor(_u("f_xn"), [128, KC, TT], BF16) as xn,
        nc.sbuf_tensor(_u("f_rstd"), [128, TT], F32) as rstd,
        nc.sbuf_tensor(_u("f_hid"), [128, NJ, TT], BF16) as hid,
        nc.sbuf_tensor(_u("f_sg"), [128, 2, SUB], F32) as sg,
        OutBufs(nc, "f", NJ) as ob,
        nc.sbuf_tensor(_u("f_win"), [128, 3, KC, 256], BF16) as win,
    ):
        r_hbuf = [Res() for _ in range(3)]
        r_sq = [Res() for _ in range(2)]
        r_xn = Res()
        r_rstd = Res()
        r_hid = [Res() for _ in range(NJ)]
        r_sg = [Res() for _ in range(2)]
        r_win = [Res() for _ in range(3)]
        r_h = cx.r_h
        bank = cx.bank
        r_bank = cx.r_bank
        nb = [0]

        def next_bank():
            b = nb[0] % 6
            nb[0] += 1
            return b

        ob.hbuf, ob.r_hbuf, ob.sq, ob.r_sq = hbuf, r_hbuf, sq, r_sq
        hb_i = ob.hb_i
        wi_i = [0]

        def rstd_from(ps_banks, dst, r_dst):
            rstd_from_banks(S, cx, ps_banks, dst, r_dst, NSUB, SUB, 1.0 / D)

        def stageA_step(ti, kc):
            t0 = ti * TT
            hb = hb_i[0] % 3
            hb_i[0] += 1
            S.dma("sp", hbuf[:, hb, :], hT3[kc, :, t0:t0 + TT], reads=[r_h[kc]], writes=[r_hbuf[hb]])
            q = ob.sq_i[0] % 2
            ob.sq_i[0] += 1
            S.op("act", lambda a: a.activation(out=sq[:, q, :], in_=hbuf[:, hb, :], func=AF.Square),
                 reads=[r_hbuf[hb]], writes=[r_sq[q]])
            S.mm_group([lambda t, s=s: t.matmul(bank[6 + s][:, 0:SUB], cx.ones_bf[:], sq[:, q, s * SUB:(s + 1) * SUB],
                                                start=(kc == 0), stop=(kc == KC - 1)) for s in range(NSUB)],
                       reads=[r_sq[q]], writes=[r_bank[6], r_bank[7]])
            if kc == KC - 1:
                rstd_from([6, 7], rstd, r_rstd)

        def stageB(ti):
            t0 = ti * TT
            for kc in range(KC):
                hb = hb_i[0] % 3
                hb_i[0] += 1
                S.dma("sp", hbuf[:, hb, :], hT3[kc, :, t0:t0 + TT], reads=[r_h[kc]], writes=[r_hbuf[hb]])
                S.op("dve", lambda v, hb=hb, kc=kc: v.scalar_tensor_tensor(
                    out=xn[:, kc, :], in0=hbuf[:, hb, :], scalar=cx.gcol[:, gi0 + kc:gi0 + kc + 1], in1=rstd[:],
                    op0=ALU.mult, op1=ALU.mult), reads=[r_hbuf[hb], r_rstd], writes=[r_xn])

        for kc in range(KC):
            stageA_step(0, kc)
        stageB(0)
        for ti in range(n_tiles):
            t0 = ti * TT
            for j in range(NJ):
                wi = wi_i[0] % 3
                wi_i[0] += 1
                S.dma("pool", win[:, wi], w_in_r[j], writes=[r_win[wi]], max_dma_last_dim=8192)
                banks_j = []
                for s in range(NSUB):
                    bg = next_bank()
                    bu = next_bank()
                    banks_j.append((bg, bu))
                    for half, b in ((0, bg), (1, bu)):
                        S.mm_group([lambda t, kc=kc, half=half, b=b, s=s, wi=wi: t.matmul(
                            bank[b][:, 0:SUB], win[:, wi, kc, half * 128:(half + 1) * 128],
                            xn[:, kc, s * SUB:(s + 1) * SUB], start=(kc == 0), stop=(kc == KC - 1))
                            for kc in range(KC)], reads=[r_win[wi], r_xn], writes=[r_bank[b]])
                if ti > 0 and j < KC:
                    out_proj_E_step(S, cx, ob, hT3, (ti - 1) * TT, cx.gcol_half, gi1, j)
                if ti + 1 < n_tiles and KC <= j < 2 * KC:
                    stageA_step(ti + 1, j - KC)
                for s in range(NSUB):
                    bg, bu = banks_j[s]
                    q = (j * NSUB + s) % 2
                    S.op("act", lambda a, q=q, bg=bg: a.activation(out=sg[:, q, :], in_=bank[bg][:, 0:SUB], func=AF.Silu),
                         reads=[r_bank[bg]], writes=[r_sg[q]])
                    S.op("dve", lambda v, q=q, bu=bu, j=j, s=s: v.tensor_tensor(
                        out=hid[:, j, s * SUB:(s + 1) * SUB], in0=sg[:, q, :], in1=bank[bu][:, 0:SUB], op=ALU.mult),
                        reads=[r_sg[q], r_bank[bu]], writes=[r_hid[j]])
            if ti + 1 < n_tiles:
                stageB(ti + 1)
            out_proj_D(S, cx, ob, hid, r_hid, NJ, w_out_r, next_bank)
        for n in range(KC):
            out_proj_E_step(S, cx, ob, hT3, (n_tiles - 1) * TT, cx.gcol_half, gi1, n)
        S.barrier()


def make_ctx(S, gcol_dram, n_g, consts_dram=None):
    nc = S.nc
    cx = Ctx()
    cx.bank = [nc.alloc_psum_tensor("bank%d" % i, [128, 512], F32) for i in range(8)]
    cx.r_bank = [Res("bank%d" % i) for i in range(8)]
    cx.ones_bf = nc.alloc_sbuf_tensor("ones_bf", [128, 128], BF16)
    cx.ones_f = nc.alloc_sbuf_tensor("ones_f", [128, 128], F32)
    cx.gcol = nc.alloc_sbuf_tensor("sb_gcol", [128, n_g], F32)
    cx.gcol_half = nc.alloc_sbuf_tensor("sb_gcol_half", [128, n_g], F32)
    cx.r_h = [Res("h%d" % k) for k in range(KC)]
    cx.eps_col = nc.alloc_sbuf_tensor("eps_col", [128, 1], F32)
    S.op("dve", lambda v: v.memset(cx.eps_col[:], EPS))
    cx.one_col = nc.alloc_sbuf_tensor("one_col", [128, 1], F32)
    S.op("dve", lambda v: v.memset(cx.one_col[:], 1.0))
    cx.r_scr = Res("scratch")
    cx.r_scr_c = Res("scratch_c")
    r_c = Res("consts")
    S.op("dve", lambda v: v.memset(cx.ones_bf[:], 1.0), writes=[r_c])
    S.op("dve", lambda v: v.memset(cx.ones_f[:], 1.0), writes=[r_c])
    S.dma("sp", cx.gcol[:], gcol_dram, writes=[r_c])
    S.op("dve", lambda v: v.tensor_scalar(out=cx.gcol_half[:], in0=cx.gcol[:], scalar1=0.5, scalar2=None, op0=ALU.mult),
         reads=[r_c], writes=[r_c])
    if consts_dram is not None:
        cx.ident_bf = nc.alloc_sbuf_tensor("ident_bf", [128, 128], BF16)
        cx.maskneg_bf = nc.alloc_sbuf_tensor("maskneg_bf", [128, 128], BF16)
        S.dma("pool", cx.ident_bf[:], consts_dram[0], writes=[r_c])
        S.dma("pool", cx.maskneg_bf[:], consts_dram[1], writes=[r_c])
        cx.ident_f = nc.alloc_sbuf_tensor("ident_f", [128, 128], F32)
        cx.blockones_f = nc.alloc_sbuf_tensor("blockones_f", [128, 128], F32)
        cx.blockones_bf = nc.alloc_sbuf_tensor("blockones_bf", [128, 128], BF16)
        cx.maskG = nc.alloc_sbuf_tensor("maskG", [128, 128], F32)
        cx.mask3 = nc.alloc_sbuf_tensor("mask3", [128, 64], F32)
        cx.identstack = nc.alloc_sbuf_tensor("identstack", [128, 64], F32)
        cx.lnx_eps_col = nc.alloc_sbuf_tensor("lnx_eps_col", [128, 1], F32)
        S.dma("sp", cx.ident_f[:], consts_dram[0], writes=[r_c])
        S.dma("sp", cx.blockones_f[:], consts_dram[2], writes=[r_c])
        S.dma("pool", cx.blockones_bf[:], consts_dram[2], writes=[r_c])
        S.dma("sp", cx.maskG[:], consts_dram[3], writes=[r_c])
        S.dma("sp", cx.mask3[:], consts_dram[4][:, 0:64], writes=[r_c])
        S.dma("sp", cx.identstack[:], consts_dram[5][:, 0:64], writes=[r_c])
        S.op("dve", lambda v: v.memset(cx.lnx_eps_col[:], LNX_EPS), writes=[r_c])
    S.barrier()
    return cx


NS6 = LSEQ // SUB


def seq_prenorm(S, cx, hT3, cb, xn, r_xn, hbig, r_hbig, sqb, r_sqb, rstd, r_rstd, gi):
    bank, r_bank = cx.bank, cx.r_bank
    for kc in range(KC):
        hb = kc % 2
        S.dma("sp", hbig[:, hb, :], hT3[kc, :, cb:cb + LSEQ], reads=[cx.r_h[kc]], writes=[r_hbig[hb]])
        S.op("act", lambda a, hb=hb: a.activation(out=sqb[:, hb, :], in_=hbig[:, hb, :], func=AF.Square),
             reads=[r_hbig[hb]], writes=[r_sqb[hb]])
        S.mm_group([lambda t, s=s, hb=hb, kc=kc: t.matmul(bank[s][:, 0:SUB], cx.ones_bf[:], sqb[:, hb, s * SUB:(s + 1) * SUB],
                                                         start=(kc == 0), stop=(kc == KC - 1)) for s in range(NS6)],
                   reads=[r_sqb[hb]], writes=[r_bank[s] for s in range(NS6)])
    rstd_from_banks(S, cx, list(range(NS6)), rstd, r_rstd, NS6, SUB, 1.0 / D)
    for kc in range(KC):
        hb = kc % 2
        S.dma("sp", hbig[:, hb, :], hT3[kc, :, cb:cb + LSEQ], reads=[cx.r_h[kc]], writes=[r_hbig[hb]])
        S.op("dve", lambda v, hb=hb, kc=kc: v.scalar_tensor_tensor(
            out=xn[:, kc, :], in0=hbig[:, hb, :], scalar=cx.gcol[:, gi + kc:gi + kc + 1], in1=rstd[:],
            op0=ALU.mult, op1=ALU.mult), reads=[r_hbig[hb], r_rstd], writes=[r_xn])


def mixer_out_phase(S, cx, hT, cb, src_scr, w_r, gi):
    _UID[0] += 1
    nc = S.nc
    hT3 = hT.rearrange("(kc p) t -> kc p t", p=128)
    src3 = src_scr.rearrange("(kc p) t -> p kc t", p=128)
    with (
        nc.sbuf_tensor(_u("mo_hbuf"), [128, 3, TT], F32) as hbuf,
        nc.sbuf_tensor(_u("mo_sq"), [128, 2, TT], BF16) as sq,
        nc.sbuf_tensor(_u("mo_src"), [128, 2, KC, TT], BF16) as src,
        OutBufs(nc, "mo", KC) as ob,
    ):
        ob.hbuf, ob.r_hbuf = hbuf, [Res() for _ in range(3)]
        ob.sq, ob.r_sq = sq, [Res() for _ in range(2)]
        r_src = [Res(), Res()]
        nb = [0]

        def next_bank():
            b = nb[0] % 6
            nb[0] += 1
            return b
        for ti in range(LSEQ // TT):
            t0 = ti * TT
            sb = ti % 2
            S.dma("sp", src[:, sb], src3[:, :, t0:t0 + TT], reads=[cx.r_scr], writes=[r_src[sb]])
            out_proj_tile(S, cx, ob, hT3, cb + t0, src[:, sb], [r_src[sb]], KC, w_r, cx.gcol, gi, next_bank)
        S.barrier()


HD = 16
NBLK = 17
QT = [(0, 512), (512, 512), (1024, 512), (1536, 512), (2048, 16)]


def fox_phase(S, cx, hT, cb, W, gi2, gi3, scr):
    _UID[0] += 1
    nc = S.nc
    L = LSEQ
    hT3 = hT.rearrange("(kc p) t -> kc p t", p=128)
    bank, r_bank = cx.bank, cx.r_bank
    with (
        nc.sbuf_tensor(_u("x_xn"), [128, KC, L], BF16) as xn,
        nc.sbuf_tensor(_u("x_small"), [128, 4], F32) as small,
        nc.sbuf_tensor(_u("x_cq"), [6, 2, L], BF16) as cq,
        nc.sbuf_tensor(_u("x_ck"), [6, 2, L], BF16) as ck,
    ):
        r_xn, r_small = Res(), Res()
        r_cq, r_ck = [Res(), Res()], [Res(), Res()]
        S.dma("sp", small[:], W["small"], writes=[r_small])
        S.op("dve", lambda v: v.tensor_scalar(out=small[:, 0:1], in0=small[:, 0:1], scalar1=128 ** -0.5, scalar2=None, op0=ALU.mult),
             reads=[r_small], writes=[r_small])
        S.op("dve", lambda v: v.tensor_scalar(out=small[0:16, 2:3], in0=small[0:16, 2:3], scalar1=-1.0, scalar2=None, op0=ALU.mult),
             reads=[r_small], writes=[r_small])
        with (
            nc.sbuf_tensor(_u("x_hbig"), [128, 2, L], F32) as hbig,
            nc.sbuf_tensor(_u("x_sqb"), [128, 2, L], BF16) as sqb,
            nc.sbuf_tensor(_u("x_rstd"), [128, L], F32) as rstd,
            nc.sbuf_tensor(_u("x_wf"), [128, KC, 16], BF16) as wf,
            nc.sbuf_tensor(_u("x_e"), [16, L], F32) as ee,
            nc.sbuf_tensor(_u("x_cp"), [16, L], F32) as cp,
            nc.sbuf_tensor(_u("x_r1"), [16, L], F32) as r1,
            nc.sbuf_tensor(_u("x_c6"), [16, 6, L], BF16) as c6,
        ):
            r_hbig, r_sqb, r_rstd = [Res(), Res()], [Res(), Res()], Res()
            r_wf, r_e, r_cp, r_r1, r_c6 = Res(), Res(), Res(), Res(), Res()
            seq_prenorm(S, cx, hT3, cb, xn, r_xn, hbig, r_hbig, sqb, r_sqb, rstd, r_rstd, gi2)
            S.dma("pool", wf[:], W["wf_r"], writes=[r_wf])
            for s in range(NS6):
                b = s % 4
                S.mm_group([lambda t, kc=kc, b=b, s=s: t.matmul(bank[b][0:16, 0:SUB], wf[:, kc, :], xn[:, kc, s * SUB:(s + 1) * SUB],
                                                               start=(kc == 0), stop=(kc == KC - 1)) for kc in range(KC)],
                           reads=[r_wf, r_xn], writes=[r_bank[b]])
                S.op("act", lambda a, b=b, s=s: a.activation(out=ee[:, s * SUB:(s + 1) * SUB], in_=bank[b][0:16, 0:SUB],
                                                             func=AF.Exp, scale=-1.0, bias=small[0:16, 2:3]),
                     reads=[r_bank[b], r_small], writes=[r_e])
            S.op("act", lambda a: a.activation(out=ee[:], in_=ee[:], func=AF.Ln, bias=cx.one_col[0:16, :], scale=1.0),
                 reads=[r_e], writes=[r_e])
            S.op("dve", lambda v: v.tensor_tensor_scan(out=cp[:], data0=cx.ones_f[0:16, 0:1].to_broadcast([16, L]), data1=ee[:],
                                                       initial=0.0, op0=ALU.mult, op1=ALU.add), reads=[r_e], writes=[r_cp])
            S.op("dve", lambda v: v.tensor_copy(out=c6[:, 0, :], in_=cp[:]), reads=[r_cp], writes=[r_c6])
            S.op("dve", lambda v: v.tensor_tensor(out=r1[:], in0=cp[:], in1=c6[:, 0, :], op=ALU.subtract), reads=[r_cp, r_c6], writes=[r_r1])
            S.op("dve", lambda v: v.tensor_copy(out=c6[:, 1, :], in_=r1[:]), reads=[r_r1], writes=[r_c6])
            S.op("dve", lambda v: v.tensor_tensor(out=r1[:], in0=r1[:], in1=c6[:, 1, :], op=ALU.subtract), reads=[r_r1, r_c6], writes=[r_r1])
            S.op("dve", lambda v: v.tensor_copy(out=c6[:, 2, :], in_=r1[:]), reads=[r_r1], writes=[r_c6])
            S.op("dve", lambda v: v.tensor_scalar(out=c6[:, 3:6, :], in0=c6[:, 0:3, :], scalar1=-1.0, scalar2=None, op0=ALU.mult),
                 reads=[r_c6], writes=[r_c6])
            S.dma("sp", scr["c"].rearrange("r h t -> h r t"), c6[:], reads=[r_c6], writes=[cx.r_scr_c])
            S.op("dve", lambda v: v.memset(cq[:], 1.0), writes=r_cq)
            S.op("dve", lambda v: v.memset(ck[:], 1.0), writes=r_ck)
            S.barrier()
        with (
            nc.sbuf_tensor(_u("x_wsl"), [128, 2, 4, KC, 128], BF16) as wsl,
            nc.sbuf_tensor(_u("x_qn"), [128, L], BF16) as qn,
            nc.sbuf_tensor(_u("x_kn"), [128, L], BF16) as kn,
            nc.sbuf_tensor(_u("x_vT"), [128, L], BF16) as vT,
            nc.sbuf_tensor(_u("x_vtok"), [128, NBLK, 128], BF16) as vtok,
            nc.sbuf_tensor(_u("x_sig"), [128, L], F32) as sig,
            nc.sbuf_tensor(_u("x_P"), [128, 4, 512], BF16) as P,
            nc.sbuf_tensor(_u("x_rd"), [128, 512], F32) as rd,
            nc.sbuf_tensor(_u("x_tmp"), [128, 512], F32) as tmp,
            nc.sbuf_tensor(_u("x_oo"), [128, 2, L], BF16) as oo,
            nc.sbuf_tensor(_u("x_sqs"), [128, 2, SUB], BF16) as sqs,
            nc.sbuf_tensor(_u("x_rq"), [128, 2, SUB], F32) as rq,
        ):
            r_wsl = [Res(), Res()]
            r_qn, r_kn, r_vT, r_vtok, r_sig = Res(), Res(), Res(), Res(), Res()
            r_P = [Res() for _ in range(4)]
            r_rd, r_tmp = Res(), Res()
            r_oo = [Res(), Res()]
            r_sqs, r_rq = [Res(), Res()], [Res(), Res()]
            pa_i = [0]

            def pbank_all():
                b = pa_i[0] % 8
                pa_i[0] += 1
                return b
            pb_i = [0]

            def pbank():
                b = (0, 1, 2, 7)[pb_i[0] % 4]
                pb_i[0] += 1
                return b
            sq_i = [0]
            p_i = [0]
            for hd in range(HD):
                wb = hd % 2
                S.dma("pool", wsl[:, wb], W["win_r"][:, hd].rearrange("f p k c -> p f k c"), writes=[r_wsl[wb]], max_dma_last_dim=8192)
                S.dma("sp", cq[0:3, wb, :], scr["c"][3:6, hd, :], reads=[cx.r_scr_c], writes=[r_cq[wb]])
                S.dma("sp", ck[3:6, wb, :], scr["c"][0:3, hd, :], reads=[cx.r_scr_c], writes=[r_ck[wb]])
                for which, dst, r_dst, gc in ((0, qn, r_qn, 0), (1, kn, r_kn, 1)):
                    for s in range(NS6):
                        b = pbank_all()
                        S.mm_group([lambda t, kc=kc, b=b, s=s, which=which: t.matmul(
                            bank[b][:, 0:SUB], wsl[:, wb, which, kc, :], xn[:, kc, s * SUB:(s + 1) * SUB],
                            start=(kc == 0), stop=(kc == KC - 1)) for kc in range(KC)], reads=[r_wsl[wb], r_xn], writes=[r_bank[b]])
                        q = sq_i[0] % 2
                        sq_i[0] += 1
                        S.op("act", lambda a, b=b, q=q: a.activation(out=sqs[:, q, :], in_=bank[b][:, 0:SUB], func=AF.Square),
                             reads=[r_bank[b]], writes=[r_sqs[q]])
                        b2 = pbank_all()
                        S.mm_group([lambda t, b2=b2, q=q: t.matmul(bank[b2][:, 0:SUB], cx.ones_bf[:], sqs[:, q, :], start=True, stop=True)],
                                   reads=[r_sqs[q]], writes=[r_bank[b2]])
                        S.op("act", lambda a, b2=b2, q=q: a.activation(out=rq[:, q, :], in_=bank[b2][:, 0:SUB], func=AF.Sqrt,
                                                                       scale=1.0 / 128, bias=cx.eps_col[:]),
                             reads=[r_bank[b2]], writes=[r_rq[q]])
                        S.op("dve", lambda v, q=q: v.reciprocal(out=rq[:, q, :], in_=rq[:, q, :]), reads=[r_rq[q]], writes=[r_rq[q]])
                        S.op("dve", lambda v, b=b, q=q, s=s, dst=dst, gc=gc: v.scalar_tensor_tensor(
                            out=dst[:, s * SUB:(s + 1) * SUB], in0=bank[b][:, 0:SUB], scalar=small[:, gc:gc + 1], in1=rq[:, q, :],
                            op0=ALU.mult, op1=ALU.mult), reads=[r_bank[b], r_rq[q], r_small], writes=[r_dst])
                for s in range(NS6):
                    b = pbank_all()
                    S.mm_group([lambda t, kc=kc, b=b, s=s: t.matmul(bank[b][:, 0:SUB], wsl[:, wb, 2, kc, :], xn[:, kc, s * SUB:(s + 1) * SUB],
                                                                   start=(kc == 0), stop=(kc == KC - 1)) for kc in range(KC)],
                               reads=[r_wsl[wb], r_xn], writes=[r_bank[b]])
                    S.op("act", lambda a, b=b, s=s: a.activation(out=vT[:, s * SUB:(s + 1) * SUB], in_=bank[b][:, 0:SUB], func=AF.Copy),
                         reads=[r_bank[b]], writes=[r_vT])
                    b = pbank_all()
                    S.mm_group([lambda t, kc=kc, b=b, s=s: t.matmul(bank[b][:, 0:SUB], wsl[:, wb, 3, kc, :], xn[:, kc, s * SUB:(s + 1) * SUB],
                                                                   start=(kc == 0), stop=(kc == KC - 1)) for kc in range(KC)],
                               reads=[r_wsl[wb], r_xn], writes=[r_bank[b]])
                    S.op("act", lambda a, b=b, s=s: a.activation(out=sig[:, s * SUB:(s + 1) * SUB], in_=bank[b][:, 0:SUB], func=AF.Sigmoid),
                         reads=[r_bank[b]], writes=[r_sig])
                for g0 in range(0, NBLK, 8):
                    b = pbank_all()
                    bb = bank[b][:].bitcast(BF16)
                    blks = list(range(g0, min(NBLK, g0 + 8)))
                    S.mm_group([lambda t, j=j, bb=bb, g0=g0: t.transpose(
                        bb[0:min(128, L - j * 128), (j - g0) * 128:(j - g0 + 1) * 128], vT[:, j * 128:min(L, (j + 1) * 128)], cx.ident_bf[:])
                        for j in blks], reads=[r_vT], writes=[r_bank[b]])
                    nfull = sum(1 for j in blks if (j + 1) * 128 <= L)
                    if nfull:
                        S.op("dve", lambda v, bb=bb, g0=g0, nfull=nfull: v.tensor_copy(
                            out=vtok[:, g0:g0 + nfull, :], in_=bb[:, 0:nfull * 128].rearrange("p (j d) -> p j d", d=128)),
                            reads=[r_bank[b]], writes=[r_vtok])
                    if nfull < len(blks):
                        j = blks[-1]
                        rows = L - j * 128
                        S.op("dve", lambda v, bb=bb, g0=g0, j=j, rows=rows: v.tensor_copy(
                            out=vtok[0:rows, j, :], in_=bb[0:rows, (j - g0) * 128:(j - g0 + 1) * 128]),
                            reads=[r_bank[b]], writes=[r_vtok])
                ob_ = hd % 2
                for qi, (t0, Wd) in enumerate(QT):
                    ao, ad = ((3, 4), (5, 6))[qi % 2]
                    nkb = min(NBLK, (t0 + Wd + 127) // 128)
                    for j in range(nkb):
                        s0 = j * 128
                        rows = min(128, L - s0)
                        c0 = max(0, s0 - t0)
                        diag = s0 >= t0
                        b = pbank()
                        fns = [lambda t, b=b, s0=s0, rows=rows, c0=c0: t.matmul(
                                   bank[b][0:rows, c0:Wd], kn[:, s0:s0 + rows], qn[:, t0 + c0:t0 + Wd], start=True, stop=False),
                               lambda t, b=b, s0=s0, rows=rows, c0=c0: t.matmul(
                                   bank[b][0:rows, c0:Wd], ck[0:6, wb, s0:s0 + rows], cq[0:6, wb, t0 + c0:t0 + Wd], start=False, stop=not diag)]
                        if diag:
                            dw = min(128, Wd - c0)
                            fns.append(lambda t, b=b, rows=rows, c0=c0, dw=dw: t.matmul(
                                bank[b][0:rows, c0:c0 + dw], cx.ident_bf[0:rows, 0:rows], cx.maskneg_bf[0:rows, 0:dw], start=False, stop=True))
                        S.mm_group(fns, reads=[r_kn, r_qn, r_ck[wb], r_cq[wb]], writes=[r_bank[b]])
                        pi = p_i[0] % 4
                        p_i[0] += 1
                        S.op("act", lambda a, b=b, rows=rows, c0=c0, pi=pi: a.activation(
                            out=P[0:rows, pi, c0:Wd], in_=bank[b][0:rows, c0:Wd], func=AF.Exp), reads=[r_bank[b]], writes=[r_P[pi]])
                        S.mm_group([lambda t, rows=rows, c0=c0, pi=pi, j=j: t.matmul(
                                        bank[ao][:, c0:Wd], vtok[0:rows, j, :], P[0:rows, pi, c0:Wd], start=(j == 0), stop=(j == nkb - 1)),
                                    lambda t, rows=rows, c0=c0, pi=pi, j=j: t.matmul(
                                        bank[ad][:, c0:Wd], cx.ones_bf[0:rows, :], P[0:rows, pi, c0:Wd], start=(j == 0), stop=(j == nkb - 1))],
                                   reads=[r_vtok, r_P[pi]], writes=[r_bank[ao], r_bank[ad]])
                    S.op("dve", lambda v, ad=ad: v.reciprocal(out=rd[:, 0:Wd], in_=bank[ad][:, 0:Wd]), reads=[r_bank[ad]], writes=[r_rd])
                    S.op("dve", lambda v, ao=ao: v.tensor_tensor(out=tmp[:, 0:Wd], in0=bank[ao][:, 0:Wd], in1=rd[:, 0:Wd], op=ALU.mult),
                         reads=[r_bank[ao], r_rd], writes=[r_tmp])
                    S.op("dve", lambda v, t0=t0: v.tensor_tensor(out=oo[:, ob_, t0:t0 + Wd], in0=tmp[:, 0:Wd], in1=sig[:, t0:t0 + Wd], op=ALU.mult),
                         reads=[r_tmp, r_sig], writes=[r_oo[ob_]])
                S.dma("sp", scr["og"][hd * 128:(hd + 1) * 128, :], oo[:, ob_, :], reads=[r_oo[ob_]], writes=[cx.r_scr])
            S.barrier()
    mixer_out_phase(S, cx, hT, cb, scr["og"], W["wout_r"], gi3)


WT = 704
NCH = 11
CH = 64
SW = 352
LPAD = 3 * WT
MU, CW_, W0_, A0_, KK_, KA_, RK_, LG_, LB_, NSM = 0, 27, 51, 59, 67, 75, 83, 91, 99, 107
LNX_EPS = 64e-5
DEC = 0.6065306597126334


def even_phase(S, cx, hT, cb, W, gi2, gi3, scr):
    _UID[0] += 1
    nc = S.nc
    hT3 = hT.rearrange("(kc p) t -> kc p t", p=128)
    bank, r_bank = cx.bank, cx.r_bank
    ym = scr["ym"]
    nb_i = [0]

    def nb():
        b = nb_i[0] % 8
        nb_i[0] += 1
        return b

    def V(fn, reads, writes):
        S.op("dve", fn, reads, writes)

    def A(fn, reads, writes):
        S.op("act", fn, reads, writes)

    F = lambda name: nc.sbuf_tensor(name, [128, WT], F32)
    names_f = ["ur", "uk", "uv", "aa", "kkn", "kf", "bs", "lw", "cw", "E", "BtT", "KtT", "BhT", "KhT", "gg", "bonv",
               "t1", "t2", "t3", "Y"]
    with ExitStack() as es:
        xn = es.enter_context(nc.sbuf_tensor(_u("e_xn"), [128, KC, WT], BF16))
        hbuf = es.enter_context(nc.sbuf_tensor(_u("e_hbuf"), [128, 2, WT], F32))
        sq = es.enter_context(nc.sbuf_tensor(_u("e_sq"), [128, 2, WT], BF16))
        rstd = es.enter_context(nc.sbuf_tensor(_u("e_rstd"), [128, WT], F32))
        lowA = es.enter_context(nc.sbuf_tensor(_u("e_lowA"), [128, WT], BF16))
        lowG1 = es.enter_context(nc.sbuf_tensor(_u("e_lowG1"), [128, WT], BF16))
        lowG2 = es.enter_context(nc.sbuf_tensor(_u("e_lowG2"), [32, WT], BF16))
        raw = es.enter_context(nc.sbuf_tensor(_u("e_raw"), [128, 2, WT + 2], F32))
        AR = es.enter_context(nc.sbuf_tensor(_u("e_AR"), [128, 2, WT], F32))
        tok = es.enter_context(nc.sbuf_tensor(_u("e_tok"), [128, 4, NCH, CH], F32))
        G1s = es.enter_context(nc.sbuf_tensor(_u("e_G1s"), [128, NCH, 128], F32))
        G2s = es.enter_context(nc.sbuf_tensor(_u("e_G2s"), [128, NCH, 128], F32))
        NM = es.enter_context(nc.sbuf_tensor(_u("e_NM"), [128, 2, 2, NCH, CH], F32))
        X = es.enter_context(nc.sbuf_tensor(_u("e_X"), [128, NCH, 128], F32))
        GamT = es.enter_context(nc.sbuf_tensor(_u("e_GamT"), [128, NCH, CH], F32))
        WC = es.enter_context(nc.sbuf_tensor(_u("e_WC"), [128, NCH], F32))
        Zb = es.enter_context(nc.sbuf_tensor(_u("e_Z"), [128, 2, CH], F32))
        Zst = es.enter_context(nc.sbuf_tensor(_u("e_Zst"), [128, 8, CH], F32))
        carry = es.enter_context(nc.sbuf_tensor(_u("e_carry"), [128, 27], F32))
        zcarry = es.enter_context(nc.sbuf_tensor(_u("e_zcarry"), [128, 8, 2], F32))
        wsl = es.enter_context(nc.sbuf_tensor(_u("e_wsl"), [128, 2, 3, KC, 128], BF16))
        wlow = es.enter_context(nc.sbuf_tensor(_u("e_wlow"), [128, KC, 288], BF16))
        w2a2 = es.enter_context(nc.sbuf_tensor(_u("e_w2a2"), [128, 1024], BF16))
        g2a = es.enter_context(nc.sbuf_tensor(_u("e_g2a"), [128, 1024], BF16))
        g2b = es.enter_context(nc.sbuf_tensor(_u("e_g2b"), [32, 1024], BF16))
        sm = es.enter_context(nc.sbuf_tensor(_u("e_small"), [128, NSM], F32))
        omm = es.enter_context(nc.sbuf_tensor(_u("e_omm"), [128, 27], F32))
        omka = es.enter_context(nc.sbuf_tensor(_u("e_omka"), [128, 8], F32))
        rmask = es.enter_context(nc.sbuf_tensor(_u("e_rmask"), [128, WT], F32))
        sqs = es.enter_context(nc.sbuf_tensor(_u("e_sqs"), [128, 2, SW], BF16))
        ybf = es.enter_context(nc.sbuf_tensor(_u("e_ybf"), [128, 2, WT], BF16))
        Fall = es.enter_context(nc.sbuf_tensor(_u("e_F"), [128, len(names_f), WT], F32))
        Fd = {n: Fall[:, i, :] for i, n in enumerate(names_f)}
        rF = {n: Res(n) for n in names_f}
        r_xn, r_rstd = Res(), Res()
        r_hbuf, r_sq = [Res(), Res()], [Res(), Res()]
        r_lowA, r_lowG1, r_lowG2 = Res(), Res(), Res()
        r_raw = [Res(), Res()]
        r_AR = Res()
        r_tok = [Res() for _ in range(4)]
        r_G1s, r_G2s = Res(), Res()
        r_NMc = [[[Res() for _ in range(NCH)] for _ in range(2)] for _ in range(2)]
        r_Xc = [Res() for _ in range(NCH)]
        r_GamT, r_WC = Res(), Res()
        r_Zb = [Res(), Res()]
        r_Zst, r_carry, r_zcarry = Res(), Res(), Res()
        r_wsl = [Res(), Res()]
        r_wlow, r_w2a2, r_g2, r_sm = Res(), Res(), Res(), Res()
        r_sqs = [Res(), Res()]
        r_ybf = [Res(), Res()]
        S.dma("sp", sm[:], W["esmall"], writes=[r_sm])
        S.dma("pool", wlow[:], W["ew_low_r"], writes=[r_wlow], max_dma_last_dim=8192)
        S.dma("pool", w2a2[:], W["w2a2"], writes=[r_w2a2])
        S.dma("pool", g2a[:], W["g2"][0:128, :], writes=[r_g2])
        S.dma("pool", g2b[:], W["g2"][128:160, :], writes=[r_g2])
        V(lambda v: v.tensor_scalar(out=omm[:], in0=sm[:, MU:MU + 27], scalar1=-1.0, scalar2=1.0, op0=ALU.mult, op1=ALU.add),
          [r_sm], [r_sm])
        V(lambda v: v.tensor_scalar(out=omka[:], in0=sm[:, KA_:KA_ + 8], scalar1=-1.0, scalar2=1.0, op0=ALU.mult, op1=ALU.add),
          [r_sm], [r_sm])
        V(lambda v: v.memset(rmask[:], 1.0), [], [r_sm])
        V(lambda v: v.memset(rmask[:].rearrange("p (c t) -> p c t", t=CH)[:, :, 0:1], 0.0), [], [r_sm])
        V(lambda v: v.memset(carry[:], 0.0), [], [r_carry])
        V(lambda v: v.memset(zcarry[:], 0.0), [], [r_zcarry])
        V(lambda v: v.memset(Zst[:], 0.0), [], [r_Zst])
        sq_i = [0]
        raw_i = [0]
        w_i = [0]
        wq = []

        def load_slab(kind, idx):
            wb = w_i[0] % 2
            w_i[0] += 1
            src = W["ew_conv_r"] if kind == 0 else W["ew_rkv_r"]
            S.dma("pool", wsl[:, wb], src[idx].rearrange("f p k c -> p f k c"), writes=[r_wsl[wb]], max_dma_last_dim=8192)
            return wb

        def proj_bank(lhs_fn, m, s, reads):
            b = nb()
            S.mm_group([lambda t, kc=kc, b=b: t.matmul(bank[b][0:m, 0:SW], lhs_fn(kc), xn[:, kc, s * SW:(s + 1) * SW],
                                                       start=(kc == 0), stop=(kc == KC - 1)) for kc in range(KC)],
                       reads=reads + [r_xn], writes=[r_bank[b]])
            return b

        def proj_shift(lhs_fn, m, reads, ci, out_ap, r_out, first_tile):
            rb = raw_i[0] % 2
            raw_i[0] += 1
            for s in range(2):
                b = proj_bank(lhs_fn, m, s, reads)
                A(lambda a, b=b, s=s: a.activation(out=raw[0:m, rb, 1 + s * SW:1 + (s + 1) * SW], in_=bank[b][0:m, 0:SW], func=AF.Copy),
                  [r_bank[b]], [r_raw[rb]])
            A(lambda a: a.activation(out=raw[0:m, rb, 0:1], in_=carry[0:m, ci:ci + 1], func=AF.Copy), [r_carry], [r_raw[rb]])
            A(lambda a: a.activation(out=Fd["t1"][0:m, :], in_=raw[0:m, rb, 0:WT], func=AF.Copy, scale=sm[0:m, MU + ci:MU + ci + 1]),
              [r_raw[rb], r_sm], [rF["t1"]])
            V(lambda v: v.scalar_tensor_tensor(out=out_ap, in0=raw[0:m, rb, 1:WT + 1], scalar=omm[0:m, ci:ci + 1], in1=Fd["t1"][0:m, :],
                                               op0=ALU.mult, op1=ALU.add), [r_raw[rb], rF["t1"], r_sm], [r_out])
            A(lambda a: a.activation(out=carry[0:m, ci:ci + 1], in_=raw[0:m, rb, WT:WT + 1], func=AF.Copy), [r_raw[rb]], [r_carry])

        def head_sum(src_bf_fn, reads, s):
            b = nb()
            S.mm_group([lambda t, b=b: t.matmul(bank[b][:, 0:SW], cx.blockones_bf[:], src_bf_fn(), start=True, stop=True)],
                       reads=reads, writes=[r_bank[b]])
            return b

        for ti in range(3):
            t0 = ti * WT
            wv = min(WT, LSEQ - t0)
            first = ti == 0
            if wv < WT:
                V(lambda v: v.memset(xn[:, :, wv:WT], 0.0), [], [r_xn])
            for kc in range(KC):
                hb = kc % 2
                S.dma("sp", hbuf[:, hb, 0:wv], hT3[kc, :, cb + t0:cb + t0 + wv], reads=[cx.r_h[kc]], writes=[r_hbuf[hb]])
                A(lambda a, hb=hb: a.activation(out=sq[:, hb, 0:wv], in_=hbuf[:, hb, 0:wv], func=AF.Square), [r_hbuf[hb]], [r_sq[hb]])
                if wv < WT:
                    A(lambda a, hb=hb: a.activation(out=sq[:, hb, wv:WT], in_=xn[:, 0, wv:WT], func=AF.Copy), [r_xn], [r_sq[hb]])
                S.mm_group([lambda t, s=s, hb=hb, kc=kc: t.matmul(bank[s][:, 0:SW], cx.ones_bf[:], sq[:, hb, s * SW:(s + 1) * SW],
                                                                 start=(kc == 0), stop=(kc == KC - 1)) for s in range(2)],
                           reads=[r_sq[hb]], writes=[r_bank[0], r_bank[1]])
            rstd_from_banks(S, cx, [0, 1], rstd, r_rstd, 2, SW, 1.0 / D)
            for kc in range(KC):
                hb = kc % 2
                S.dma("sp", hbuf[:, hb, 0:wv], hT3[kc, :, cb + t0:cb + t0 + wv], reads=[cx.r_h[kc]], writes=[r_hbuf[hb]])
                V(lambda v, hb=hb, kc=kc: v.scalar_tensor_tensor(out=xn[:, kc, 0:wv], in0=hbuf[:, hb, 0:wv],
                                                                 scalar=cx.gcol[:, gi2 + kc:gi2 + kc + 1], in1=rstd[:, 0:wv],
                                                                 op0=ALU.mult, op1=ALU.mult), [r_hbuf[hb], r_rstd], [r_xn])
            proj_shift(lambda kc: wlow[:, kc, 0:128], 128, [r_wlow], 24, Fd["t2"], rF["t2"], first)
            A(lambda a: a.activation(out=lowA[0:64, :], in_=Fd["t2"][0:64, :], func=AF.Tanh), [rF["t2"]], [r_lowA])
            A(lambda a: a.activation(out=lowA[64:128, :], in_=Fd["t2"][64:128, :], func=AF.Copy), [rF["t2"]], [r_lowA])
            proj_shift(lambda kc: wlow[:, kc, 128:256], 128, [r_wlow], 25, Fd["t2"], rF["t2"], first)
            A(lambda a: a.activation(out=lowG1[:], in_=Fd["t2"], func=AF.Sigmoid), [rF["t2"]], [r_lowG1])
            proj_shift(lambda kc: wlow[:, kc, 256:288], 32, [r_wlow], 26, Fd["t2"][0:32, :], rF["t2"], first)
            A(lambda a: a.activation(out=lowG2[:], in_=Fd["t2"][0:32, :], func=AF.Sigmoid), [rF["t2"]], [r_lowG2])
            for jc in range(8):
                wb = load_slab(0, jc)
                gb, gc, zb = Fd["t2"], Fd["t3"], raw
                rb = raw_i[0] % 2
                raw_i[0] += 1
                for s in range(2):
                    b0 = proj_bank(lambda kc: wsl[:, wb, 0, kc, :], 128, s, [r_wsl[wb]])
                    A(lambda a, b0=b0, s=s: a.activation(out=gb[:, s * SW:(s + 1) * SW], in_=bank[b0][:, 0:SW], func=AF.Copy),
                      [r_bank[b0]], [rF["t2"]])
                    b1 = proj_bank(lambda kc: wsl[:, wb, 1, kc, :], 128, s, [r_wsl[wb]])
                    A(lambda a, b1=b1, s=s: a.activation(out=gc[:, s * SW:(s + 1) * SW], in_=bank[b1][:, 0:SW], func=AF.Copy),
                      [r_bank[b1]], [rF["t3"]])
                    b2 = proj_bank(lambda kc: wsl[:, wb, 2, kc, :], 128, s, [r_wsl[wb]])
                    V(lambda v, b2=b2, s=s: v.tensor_tensor(out=zb[:, rb, 2 + s * SW:2 + (s + 1) * SW], in0=gc[:, s * SW:(s + 1) * SW],
                                                            in1=bank[b2][:, 0:SW], op=ALU.mult), [rF["t3"], r_bank[b2]], [r_raw[rb]])
                V(lambda v: v.tensor_copy(out=zb[:, rb, 0:2], in_=zcarry[:, jc, :]), [r_zcarry], [r_raw[rb]])
                A(lambda a: a.activation(out=Fd["t1"], in_=zb[:, rb, 0:WT], func=AF.Copy, scale=sm[:, CW_ + jc:CW_ + jc + 1]),
                  [r_raw[rb], r_sm], [rF["t1"]])
                V(lambda v: v.scalar_tensor_tensor(out=Fd["t1"], in0=zb[:, rb, 1:WT + 1], scalar=sm[:, CW_ + 8 + jc:CW_ + 9 + jc],
                                                   in1=Fd["t1"], op0=ALU.mult, op1=ALU.add), [r_raw[rb], rF["t1"], r_sm], [rF["t1"]])
                V(lambda v: v.scalar_tensor_tensor(out=Fd["t1"], in0=zb[:, rb, 2:WT + 2], scalar=sm[:, CW_ + 16 + jc:CW_ + 17 + jc],
                                                   in1=Fd["t1"], op0=ALU.mult, op1=ALU.add), [r_raw[rb], rF["t1"], r_sm], [rF["t1"]])
                V(lambda v: v.tensor_copy(out=zcarry[:, jc, :], in_=zb[:, rb, WT:WT + 2]), [r_raw[rb]], [r_zcarry])
                yb = jc % 2
                V(lambda v, yb=yb: v.tensor_tensor(out=ybf[:, yb, :], in0=gb, in1=Fd["t1"], op=ALU.mult), [rF["t2"], rF["t1"]], [r_ybf[yb]])
                S.dma("sp", ym[jc * 128:(jc + 1) * 128, t0:t0 + WT], ybf[:, yb, :], reads=[r_ybf[yb]], writes=[cx.r_scr])
            for j in range(8):
                wb = load_slab(1, j)
                proj_shift(lambda kc: wsl[:, wb, 0, kc, :], 128, [r_wsl[wb]], 0 + j, Fd["ur"], rF["ur"], first)
                proj_shift(lambda kc: wsl[:, wb, 1, kc, :], 128, [r_wsl[wb]], 8 + j, Fd["uk"], rF["uk"], first)
                proj_shift(lambda kc: wsl[:, wb, 2, kc, :], 128, [r_wsl[wb]], 16 + j, Fd["uv"], rF["uv"], first)
                cs = slice(j * 128, (j + 1) * 128)
                for s in range(2):
                    ss = slice(s * SW, (s + 1) * SW)
                    b = nb()
                    S.mm_group([lambda t, b=b: t.matmul(bank[b][:, 0:SW], w2a2[0:64, cs], lowA[0:64, ss], start=True, stop=True)],
                               reads=[r_w2a2, r_lowA], writes=[r_bank[b]])
                    A(lambda a, b=b: a.activation(out=Fd["lw"][:, ss], in_=bank[b][:, 0:SW], func=AF.Sigmoid, bias=sm[:, W0_ + j:W0_ + j + 1]),
                      [r_bank[b], r_sm], [rF["lw"]])
                    b = nb()
                    S.mm_group([lambda t, b=b: t.matmul(bank[b][:, 0:SW], w2a2[64:128, cs], lowA[64:128, ss], start=True, stop=True)],
                               reads=[r_w2a2, r_lowA], writes=[r_bank[b]])
                    A(lambda a, b=b: a.activation(out=Fd["aa"][:, ss], in_=bank[b][:, 0:SW], func=AF.Sigmoid, bias=sm[:, A0_ + j:A0_ + j + 1]),
                      [r_bank[b], r_sm], [rF["aa"]])
                    b = nb()
                    S.mm_group([lambda t, b=b: t.matmul(bank[b][:, 0:SW], g2a[:, cs], lowG1[:, ss], start=True, stop=False),
                                lambda t, b=b: t.matmul(bank[b][:, 0:SW], g2b[:, cs], lowG2[:, ss], start=False, stop=True)],
                               reads=[r_g2, r_lowG1, r_lowG2], writes=[r_bank[b]])
                    A(lambda a, b=b: a.activation(out=Fd["gg"][:, ss], in_=bank[b][:, 0:SW], func=AF.Copy), [r_bank[b]], [rF["gg"]])
                    q = sq_i[0] % 2
                    sq_i[0] += 1
                    A(lambda a, q=q: a.activation(out=sqs[:, q, :], in_=Fd["uk"][:, ss], func=AF.Square, scale=sm[:, KK_ + j:KK_ + j + 1]),
                      [rF["uk"], r_sm], [r_sqs[q]])
                    b = head_sum(lambda q=q: sqs[:, q, :], [r_sqs[q]], s)
                    A(lambda a, b=b: a.activation(out=Fd["t2"][:, ss], in_=bank[b][:, 0:SW], func=AF.Sqrt), [r_bank[b]], [rF["t2"]])
                V(lambda v: v.tensor_scalar(out=Fd["lw"], in0=Fd["lw"], scalar1=-DEC, scalar2=None, op0=ALU.mult), [rF["lw"]], [rF["lw"]])
                V(lambda v: v.tensor_scalar(out=Fd["t2"], in0=Fd["t2"], scalar1=1e-12, scalar2=None, op0=ALU.max), [rF["t2"]], [rF["t2"]])
                V(lambda v: v.reciprocal(out=Fd["t2"], in_=Fd["t2"]), [rF["t2"]], [rF["t2"]])
                V(lambda v: v.scalar_tensor_tensor(out=Fd["kkn"], in0=Fd["uk"], scalar=sm[:, KK_ + j:KK_ + j + 1], in1=Fd["t2"],
                                                   op0=ALU.mult, op1=ALU.mult), [rF["uk"], rF["t2"], r_sm], [rF["kkn"]])
                V(lambda v: v.tensor_scalar(out=Fd["t3"], in0=Fd["aa"], scalar1=sm[:, KA_ + j:KA_ + j + 1], scalar2=omka[:, j:j + 1],
                                            op0=ALU.mult, op1=ALU.add), [rF["aa"], r_sm], [rF["t3"]])
                V(lambda v: v.tensor_tensor(out=Fd["kf"], in0=Fd["uk"], in1=Fd["t3"], op=ALU.mult), [rF["uk"], rF["t3"]], [rF["kf"]])
                V(lambda v: v.tensor_tensor(out=Fd["bs"], in0=Fd["kkn"], in1=Fd["aa"], op=ALU.mult), [rF["kkn"], rF["aa"]], [rF["bs"]])
                V(lambda v: v.tensor_tensor(out=Fd["t3"], in0=Fd["ur"], in1=Fd["kf"], op=ALU.mult), [rF["ur"], rF["kf"]], [rF["t3"]])
                for s in range(2):
                    ss = slice(s * SW, (s + 1) * SW)
                    q = sq_i[0] % 2
                    sq_i[0] += 1
                    A(lambda a, q=q, ss=ss: a.activation(out=sqs[:, q, :], in_=Fd["t3"][:, ss], func=AF.Copy, scale=sm[:, RK_ + j:RK_ + j + 1]),
                      [rF["t3"], r_sm], [r_sqs[q]])
                    b = head_sum(lambda q=q: sqs[:, q, :], [r_sqs[q]], s)
                    V(lambda v, b=b, ss=ss: v.tensor_tensor(out=Fd["bonv"][:, ss], in0=Fd["uv"][:, ss], in1=bank[b][:, 0:SW], op=ALU.mult),
                      [rF["uv"], r_bank[b]], [rF["bonv"]])
                V(lambda v: v.tensor_tensor_scan(out=Fd["cw"], data0=rmask[:], data1=Fd["lw"], initial=0.0, op0=ALU.mult, op1=ALU.add),
                  [rF["lw"], r_sm], [rF["cw"]])
                cw3 = Fd["cw"].rearrange("p (c t) -> p c t", t=CH)
                A(lambda a: a.activation(out=WC[:], in_=cw3[:, :, CH - 1], func=AF.Exp), [rF["cw"]], [r_WC])
                V(lambda v: v.tensor_tensor(out=Fd["t3"], in0=Fd["cw"], in1=Fd["lw"], op=ALU.subtract), [rF["cw"], rF["lw"]], [rF["t3"]])
                A(lambda a: a.activation(out=Fd["E"], in_=Fd["t3"], func=AF.Exp), [rF["t3"]], [rF["E"]])
                V(lambda v: v.scalar_tensor_tensor(out=AR[:, 0, :], in0=Fd["kkn"], scalar=-1.0, in1=Fd["E"], op0=ALU.mult, op1=ALU.mult),
                  [rF["kkn"], rF["E"]], [r_AR])
                A(lambda a: a.activation(out=Fd["E"], in_=Fd["cw"], func=AF.Exp), [rF["cw"]], [rF["E"]])
                V(lambda v: v.tensor_tensor(out=AR[:, 1, :], in0=Fd["ur"], in1=Fd["E"], op=ALU.mult), [rF["ur"], rF["E"]], [r_AR])
                A(lambda a: a.activation(out=Fd["E"], in_=Fd["cw"], func=AF.Exp, scale=-1.0), [rF["cw"]], [rF["E"]])
                V(lambda v: v.tensor_tensor(out=Fd["BtT"], in0=Fd["bs"], in1=Fd["E"], op=ALU.mult), [rF["bs"], rF["E"]], [rF["BtT"]])
                V(lambda v: v.tensor_tensor(out=Fd["KtT"], in0=Fd["kf"], in1=Fd["E"], op=ALU.mult), [rF["kf"], rF["E"]], [rF["KtT"]])
                V(lambda v: v.tensor_tensor(out=Fd["t3"].rearrange("p (c t) -> p c t", t=CH), in0=cw3[:, :, CH - 1:CH].to_broadcast([128, NCH, CH]),
                                            in1=cw3, op=ALU.subtract), [rF["cw"]], [rF["t3"]])
                A(lambda a: a.activation(out=Fd["E"], in_=Fd["t3"], func=AF.Exp), [rF["t3"]], [rF["E"]])
                V(lambda v: v.tensor_tensor(out=Fd["BhT"], in0=Fd["bs"], in1=Fd["E"], op=ALU.mult), [rF["bs"], rF["E"]], [rF["BhT"]])
                V(lambda v: v.tensor_tensor(out=Fd["KhT"], in0=Fd["kf"], in1=Fd["E"], op=ALU.mult), [rF["kf"], rF["E"]], [rF["KhT"]])
                srcs = [(AR[:, 0, :], r_AR), (Fd["BhT"], rF["BhT"]), (Fd["KhT"], rF["KhT"]), (Fd["uv"], rF["uv"])]
                for ai, (src, r_src) in enumerate(srcs):
                    for g0 in range(0, NCH, 8):
                        b = nb()
                        cl = list(range(g0, min(NCH, g0 + 8)))
                        S.mm_group([lambda t, b=b, c=c, h=h, src=src, g0=g0: t.matmul(
                            bank[b][h * 64:(h + 1) * 64, (c - g0) * CH:(c - g0 + 1) * CH], src[h * 64:(h + 1) * 64, c * CH:(c + 1) * CH],
                            cx.ident_f[h * 64:(h + 1) * 64, h * 64:(h + 1) * 64], start=True, stop=True) for c in cl for h in range(2)],
                            reads=[r_src], writes=[r_bank[b]])
                        A(lambda a, b=b, g0=g0, n=len(cl), ai=ai: a.activation(
                            out=tok[:, ai, g0:g0 + n, :], in_=bank[b][:, 0:n * CH].rearrange("p (c t) -> p c t", t=CH), func=AF.Copy),
                          [r_bank[b]], [r_tok[ai]])
                for which, dstG, r_dstG, lhs in ((0, G1s, r_G1s, Fd["BtT"]), (1, G2s, r_G2s, Fd["KtT"])):
                    for g0 in range(0, NCH, 4):
                        b = nb()
                        cl = list(range(g0, min(NCH, g0 + 4)))
                        S.mm_group([lambda t, b=b, c=c, h=h, g0=g0, lhs=lhs: t.matmul(
                            bank[b][h * 64:(h + 1) * 64, (c - g0) * 128:(c - g0 + 1) * 128], lhs[h * 64:(h + 1) * 64, c * CH:(c + 1) * CH],
                            AR[h * 64:(h + 1) * 64, :, c * CH:(c + 1) * CH], start=True, stop=True) for c in cl for h in range(2)],
                            reads=[rF["BtT"], rF["KtT"], r_AR], writes=[r_bank[b]])
                        V(lambda v, b=b, g0=g0, n=len(cl), dstG=dstG: v.tensor_tensor(
                            out=dstG[:, g0:g0 + n, :], in0=bank[b][:, 0:n * 128].rearrange("p (c t) -> p c t", t=128),
                            in1=cx.maskG[:].unsqueeze(1).to_broadcast([128, n, 128]), op=ALU.mult), [r_bank[b]], [r_dstG])
                pp = 0
                for g0 in range(0, NCH, 8):
                    b = nb()
                    cl = list(range(g0, min(NCH, g0 + 8)))
                    S.mm_group([lambda t, b=b, c=c, h=h, g0=g0: t.matmul(
                        bank[b][h * 64:(h + 1) * 64, (c - g0) * CH:(c - g0 + 1) * CH], AR[h * 64:(h + 1) * 64, 0, c * CH:(c + 1) * CH],
                        Fd["BtT"][h * 64:(h + 1) * 64, c * CH:(c + 1) * CH], start=True, stop=True) for c in cl for h in range(2)],
                        reads=[rF["BtT"], r_AR], writes=[r_bank[b]])
                    V(lambda v, b=b, g0=g0, n=len(cl): v.tensor_tensor(
                        out=NM[:, 0, 1, g0:g0 + n, :], in0=bank[b][:, 0:n * CH].rearrange("p (c t) -> p c t", t=CH),
                        in1=cx.mask3[:].unsqueeze(1).to_broadcast([128, n, CH]), op=ALU.mult), [r_bank[b]], [r_NMc[0][1][c] for c in cl])
                A(lambda a: a.activation(out=NM[:, 0, 0, :, :], in_=G1s[:, :, 0:CH], func=AF.Copy), [r_G1s], r_NMc[0][0])
                A(lambda a: a.activation(out=X[:, :, 0:CH], in_=tok[:, 0, :, :], func=AF.Copy), [r_tok[0]], r_Xc)
                for g0 in range(0, NCH, 8):
                    b = nb()
                    cl = list(range(g0, min(NCH, g0 + 8)))
                    S.mm_group([lambda t, b=b, c=c, h=h, g0=g0: t.matmul(
                        bank[b][h * 64:(h + 1) * 64, (c - g0) * CH:(c - g0 + 1) * CH], G2s[h * 64:(h + 1) * 64, c, 0:CH],
                        tok[h * 64:(h + 1) * 64, 3, c, :], start=True, stop=True) for c in cl for h in range(2)],
                        reads=[r_G2s, r_tok[3]], writes=[r_bank[b]])
                    A(lambda a, b=b, g0=g0, n=len(cl): a.activation(
                        out=X[:, g0:g0 + n, CH:128], in_=bank[b][:, 0:n * CH].rearrange("p (c t) -> p c t", t=CH), func=AF.Copy),
                      [r_bank[b]], [r_Xc[c] for c in cl])
                for it in range(6):
                    Ncur, Mcur = NM[:, pp, 0], NM[:, pp, 1]
                    for g0 in range(0, NCH, 4):
                        b = nb()
                        cl = list(range(g0, min(NCH, g0 + 4)))
                        S.mm_group([lambda t, b=b, c=c, h=h, g0=g0, Ncur=Ncur: t.matmul(
                            bank[b][h * 64:(h + 1) * 64, (c - g0) * 128:(c - g0 + 1) * 128], Ncur[h * 64:(h + 1) * 64, c, :],
                            X[h * 64:(h + 1) * 64, c, :], start=True, stop=True) for c in cl for h in range(2)],
                            reads=[r_NMc[pp][0][c] for c in cl] + [r_Xc[c] for c in cl], writes=[r_bank[b]])
                        V(lambda v, b=b, g0=g0, n=len(cl): v.tensor_tensor(
                            out=X[:, g0:g0 + n, :], in0=X[:, g0:g0 + n, :], in1=bank[b][:, 0:n * 128].rearrange("p (c t) -> p c t", t=128),
                            op=ALU.add), [r_bank[b]] + [r_Xc[c] for c in cl], [r_Xc[c] for c in cl])
                    if it < 5:
                        for which in range(2):
                            lhs, rhs = (Mcur, Ncur) if which == 0 else (Ncur, Mcur)
                            for g0 in range(0, NCH, 8):
                                b = nb()
                                cl = list(range(g0, min(NCH, g0 + 8)))
                                S.mm_group([lambda t, b=b, c=c, h=h, g0=g0, lhs=lhs, rhs=rhs: t.matmul(
                                    bank[b][h * 64:(h + 1) * 64, (c - g0) * CH:(c - g0 + 1) * CH], lhs[h * 64:(h + 1) * 64, c, :],
                                    rhs[h * 64:(h + 1) * 64, c, :], start=True, stop=True) for c in cl for h in range(2)],
                                    reads=[r_NMc[pp][0][c] for c in cl] + [r_NMc[pp][1][c] for c in cl], writes=[r_bank[b]])
                                A(lambda a, b=b, g0=g0, n=len(cl), which=which, pp=pp: a.activation(
                                    out=NM[:, 1 - pp, which, g0:g0 + n, :], in_=bank[b][:, 0:n * CH].rearrange("p (c t) -> p c t", t=CH),
                                    func=AF.Copy), [r_bank[b]], [r_NMc[1 - pp][which][c] for c in cl])
                        pp = 1 - pp
                for g0 in range(0, NCH, 8):
                    b = nb()
                    cl = list(range(g0, min(NCH, g0 + 8)))
                    S.mm_group([lambda t, b=b, c=c, h=h, g0=g0: t.matmul(
                        bank[b][h * 64:(h + 1) * 64, (c - g0) * CH:(c - g0 + 1) * CH], X[h * 64:(h + 1) * 64, c, 0:CH],
                        G1s[h * 64:(h + 1) * 64, c, CH:128], start=True, stop=True) for c in cl for h in range(2)],
                        reads=[r_Xc[c] for c in cl] + [r_G1s], writes=[r_bank[b]])
                    V(lambda v, b=b, g0=g0, n=len(cl): v.tensor_tensor(
                        out=AR[:, 1, g0 * CH:(g0 + n) * CH], in0=AR[:, 1, g0 * CH:(g0 + n) * CH], in1=bank[b][:, 0:n * CH], op=ALU.add),
                      [r_bank[b], r_AR], [r_AR])
                    b = nb()
                    S.mm_group([lambda t, b=b, c=c, h=h, g0=g0: t.matmul(
                        bank[b][h * 64:(h + 1) * 64, (c - g0) * CH:(c - g0 + 1) * CH], X[h * 64:(h + 1) * 64, c, 0:CH],
                        tok[h * 64:(h + 1) * 64, 1, c, :], start=True, stop=True) for c in cl for h in range(2)],
                        reads=[r_Xc[c] for c in cl] + [r_tok[1]], writes=[r_bank[b]])
                    for c in cl:
                        V(lambda v, b=b, c=c, g0=g0: v.scalar_tensor_tensor(
                            out=GamT[:, c, :], in0=cx.identstack[:], scalar=WC[:, c:c + 1], in1=bank[b][:, (c - g0) * CH:(c - g0 + 1) * CH],
                            op0=ALU.mult, op1=ALU.add), [r_bank[b], r_WC], [r_GamT])
                V(lambda v: v.tensor_copy(out=Zb[:, 0, :], in_=Zst[:, j, :]), [r_Zst], [r_Zb[0]])
                zp = 0
                for c in range(NCH):
                    b = nb()
                    fns = []
                    for h in range(2):
                        hs = slice(h * 64, (h + 1) * 64)
                        fns += [lambda t, b=b, hs=hs, c=c, zp=zp: t.matmul(bank[b][hs, 0:CH], Zb[hs, zp, :], AR[hs, 1, c * CH:(c + 1) * CH],
                                                                         start=True, stop=False),
                                lambda t, b=b, hs=hs, c=c: t.matmul(bank[b][hs, 0:CH], X[hs, c, CH:128], G1s[hs, c, CH:128], start=False, stop=False),
                                lambda t, b=b, hs=hs, c=c: t.matmul(bank[b][hs, 0:CH], tok[hs, 3, c, :], G2s[hs, c, CH:128], start=False, stop=True)]
                    S.mm_group(fns, reads=[r_Zb[zp], r_AR, r_Xc[c], r_G1s, r_G2s, r_tok[3]], writes=[r_bank[b]])
                    A(lambda a, b=b, c=c: a.activation(out=Fd["Y"][:, c * CH:(c + 1) * CH], in_=bank[b][:, 0:CH], func=AF.Copy),
                      [r_bank[b]], [rF["Y"]])
                    b = nb()
                    fns = []
                    for h in range(2):
                        hs = slice(h * 64, (h + 1) * 64)
                        fns += [lambda t, b=b, hs=hs, c=c, zp=zp: t.matmul(bank[b][hs, 0:CH], GamT[hs, c, :], Zb[hs, zp, :], start=True, stop=False),
                                lambda t, b=b, hs=hs, c=c: t.matmul(bank[b][hs, 0:CH], tok[hs, 1, c, :], X[hs, c, CH:128], start=False, stop=False),
                                lambda t, b=b, hs=hs, c=c: t.matmul(bank[b][hs, 0:CH], tok[hs, 2, c, :], tok[hs, 3, c, :], start=False, stop=True)]
                    S.mm_group(fns, reads=[r_Zb[zp], r_GamT, r_Xc[c], r_tok[1], r_tok[2], r_tok[3]], writes=[r_bank[b]])
                    V(lambda v, b=b, zp=zp: v.tensor_copy(out=Zb[:, 1 - zp, :], in_=bank[b][:, 0:CH]), [r_bank[b]], [r_Zb[1 - zp]])
                    zp = 1 - zp
                V(lambda v, zp=zp: v.tensor_copy(out=Zst[:, j, :], in_=Zb[:, zp, :]), [r_Zb[zp]], [r_Zst])
                yb = j % 2
                for s in range(2):
                    ss = slice(s * SW, (s + 1) * SW)
                    b1 = nb()
                    S.mm_group([lambda t, b1=b1, ss=ss: t.matmul(bank[b1][:, 0:SW], cx.blockones_f[:], Fd["Y"][:, ss], start=True, stop=True)],
                               reads=[rF["Y"]], writes=[r_bank[b1]])
                    A(lambda a, ss=ss: a.activation(out=Fd["t1"][:, ss], in_=Fd["Y"][:, ss], func=AF.Square), [rF["Y"]], [rF["t1"]])
                    b2 = nb()
                    S.mm_group([lambda t, b2=b2, ss=ss: t.matmul(bank[b2][:, 0:SW], cx.blockones_f[:], Fd["t1"][:, ss], start=True, stop=True)],
                               reads=[rF["t1"]], writes=[r_bank[b2]])
                    A(lambda a, b1=b1, ss=ss: a.activation(out=Fd["t2"][:, ss], in_=bank[b1][:, 0:SW], func=AF.Copy, scale=1.0 / 64),
                      [r_bank[b1]], [rF["t2"]])
                    V(lambda v, ss=ss: v.tensor_tensor(out=Fd["t3"][:, ss], in0=Fd["Y"][:, ss], in1=Fd["t2"][:, ss], op=ALU.subtract),
                      [rF["Y"], rF["t2"]], [rF["t3"]])
                    V(lambda v, ss=ss: v.tensor_tensor(out=Fd["t2"][:, ss], in0=Fd["t2"][:, ss], in1=Fd["t2"][:, ss], op=ALU.mult),
                      [rF["t2"]], [rF["t2"]])
                    V(lambda v, b2=b2, ss=ss: v.scalar_tensor_tensor(out=Fd["E"][:, ss], in0=bank[b2][:, 0:SW], scalar=1.0 / 64, in1=Fd["t2"][:, ss],
                                                                     op0=ALU.mult, op1=ALU.subtract), [r_bank[b2], rF["t2"]], [rF["E"]])
                    A(lambda a, ss=ss: a.activation(out=Fd["E"][:, ss], in_=Fd["E"][:, ss], func=AF.Sqrt, bias=cx.lnx_eps_col[:]), [rF["E"]], [rF["E"]])
                V(lambda v: v.reciprocal(out=Fd["E"], in_=Fd["E"]), [rF["E"]], [rF["E"]])
                V(lambda v: v.tensor_tensor(out=Fd["t3"], in0=Fd["t3"], in1=Fd["E"], op=ALU.mult), [rF["t3"], rF["E"]], [rF["t3"]])
                V(lambda v: v.tensor_scalar(out=Fd["t3"], in0=Fd["t3"], scalar1=sm[:, LG_ + j:LG_ + j + 1], scalar2=sm[:, LB_ + j:LB_ + j + 1],
                                            op0=ALU.mult, op1=ALU.add), [rF["t3"], r_sm], [rF["t3"]])
                V(lambda v: v.tensor_tensor(out=Fd["t3"], in0=Fd["t3"], in1=Fd["bonv"], op=ALU.add), [rF["t3"], rF["bonv"]], [rF["t3"]])
                V(lambda v, yb=yb: v.tensor_tensor(out=ybf[:, yb, :], in0=Fd["t3"], in1=Fd["gg"], op=ALU.mult), [rF["t3"], rF["gg"]], [r_ybf[yb]])
                S.dma("sp", ym[1024 + j * 128:1024 + (j + 1) * 128, t0:t0 + WT], ybf[:, yb, :], reads=[r_ybf[yb]], writes=[cx.r_scr])
        S.barrier()
    mixer_out_phase(S, cx, hT, cb, ym[:, 0:LSEQ], W["wout_r"], gi3)


DEPTH = 4
N_META = 16


def _consts_np():
    c = np.zeros((6, 128, 128), np.float32)
    c[0] = np.eye(128)
    r = np.arange(128)[:, None]
    cc = np.arange(128)[None, :]
    c[1] = np.where(cc < r, -30000.0, 0.0)
    c[2, :64, :64] = 1
    c[2, 64:, 64:] = 1
    s = (np.arange(128) % 64)[:, None]
    t = (np.arange(128) % 64)[None, :]
    c[3] = np.where(np.arange(128)[None, :] < 64, t > s, t >= s).astype(np.float32)
    c[4, :, :64] = (np.arange(64)[None, :] < s).astype(np.float32)
    c[5, :, :64] = (np.arange(64)[None, :] == s).astype(np.float32)
    return c


def _prep_w_in(w):
    wg = w[:, :DFF].reshape(KC, 128, NJ, 128)
    wu = w[:, DFF:].reshape(KC, 128, NJ, 128)
    return np.ascontiguousarray(np.concatenate([wg, wu], axis=3).transpose(2, 1, 0, 3))


def _prep_w_out(w, K):
    return np.ascontiguousarray(w.reshape(K, 128, KC, 128).transpose(2, 1, 0, 3))


def _prep_even(e_w_in, e_conv_w, e_mu, e_w0, e_w2, e_a0, e_a2, e_g2, e_k_k, e_k_a, e_r_k, e_lnx_g, e_lnx_b, e_w_out):
    def slabs(cols0, n):
        w = e_w_in[:, cols0:cols0 + n * 128].reshape(KC, 128, n, 128)
        return w.transpose(2, 1, 0, 3)
    conv = np.stack([slabs(0, 8), slabs(1024, 8), slabs(2048, 8)], axis=1)
    rkv = np.stack([slabs(3072, 8), slabs(4096, 8), slabs(5120, 8)], axis=1)
    low = np.ascontiguousarray(e_w_in[:, 6144:6432].reshape(KC, 128, 288).transpose(1, 0, 2))
    sm = np.zeros((128, NSM), np.float32)
    col = lambda v: v.reshape(-1, 128).T
    sm[:, MU:MU + 24] = col(e_mu[:3072])
    sm[:64, MU + 24] = e_mu[3072:3136]
    sm[64:, MU + 24] = e_mu[3136:3200]
    sm[:, MU + 25] = e_mu[3200:3328]
    sm[:32, MU + 26] = e_mu[3328:3360]
    for jj in range(3):
        sm[:, CW_ + 8 * jj:CW_ + 8 * jj + 8] = col(e_conv_w[jj])
    for base, v in ((W0_, e_w0), (A0_, e_a0), (KK_, e_k_k), (KA_, e_k_a), (RK_, e_r_k), (LG_, e_lnx_g), (LB_, e_lnx_b)):
        sm[:, base:base + 8] = col(v)
    return dict(ew_conv_r=np.ascontiguousarray(conv), ew_rkv_r=np.ascontiguousarray(rkv), ew_low_r=low, esmall=sm,
                w2a2=np.ascontiguousarray(np.concatenate([e_w2, e_a2], axis=0)), g2=np.ascontiguousarray(e_g2),
                wout_r=_prep_w_out(e_w_out, KC))


def _prep_fox(w_in, b_f, q_g, k_g, w_out):
    win_r = np.ascontiguousarray(w_in[:, :4 * D].reshape(KC, 128, 4, 16, 128).transpose(2, 3, 1, 0, 4))
    wf_r = np.ascontiguousarray(w_in[:, 4 * D:].reshape(KC, 128, 16).transpose(1, 0, 2))
    small = np.zeros((128, 4), np.float32)
    small[:, 0] = q_g
    small[:, 1] = k_g
    small[:16, 2] = b_f
    return dict(win_r=win_r, wf_r=wf_r, small=small, wout_r=_prep_w_out(w_out, KC))


EVEN_SHAPES = dict(ew_conv_r=[8, 3, 128, 16, 128], ew_rkv_r=[8, 3, 128, 16, 128], ew_low_r=[128, 16, 288], esmall=[128, NSM],
                   w2a2=[128, 1024], g2=[160, 1024], wout_r=[16, 128, 16, 128])
FOX_SHAPES = dict(win_r=[4, 16, 128, 16, 128], wf_r=[128, 16, 16], small=[128, 4], wout_r=[16, 128, 16, 128])


def build_program():
    nc = bass.Bass("TRN2", target_bir_lowering=False)
    ext = lambda name, shape: nc.dram_tensor(name, shape, F32, kind="ExternalInput").ap()
    h0 = ext("h0", [D, TC])
    gcol = ext("gcol", [128, DEPTH * 6 * KC])
    consts = ext("consts", [6, 128, 128])
    ffn_in = ext("ffn_in_r", [DEPTH * 2, NJ, 128, KC, 256])
    ffn_out = ext("ffn_out_r", [DEPTH * 2, KC, 128, NJ, 128])
    EW = [{k: ext("e%d_%s" % (i, k), v) for k, v in EVEN_SHAPES.items()} for i in range(2)]
    OW = [{k: ext("o%d_%s" % (i, k), v) for k, v in FOX_SHAPES.items()} for i in range(2)]
    hT = nc.dram_tensor("hT", [D, TC], F32, kind="ExternalOutput").ap()
    scr = {"c": nc.dram_tensor("scr_c", [6, 16, LSEQ], BF16, kind="Internal").ap(),
           "og": nc.dram_tensor("scr_og", [D, LSEQ], BF16, kind="Internal").ap(),
           "ym": nc.dram_tensor("scr_ym", [D, LPAD], BF16, kind="Internal").ap()}
    S = Sched(nc)
    cx = make_ctx(S, gcol, DEPTH * 6 * KC, consts)
    S.dma("sp", hT, h0, writes=cx.r_h)
    for l in range(DEPTH):
        gi = lambda i: (l * 6 + i) * KC
        ffn_phase(S, cx, hT, ffn_in[l * 2], ffn_out[l * 2], gi(0), gi(1), TC // TT)
        for seq in range(NSEQ):
            if l % 2 == 0:
                even_phase(S, cx, hT, seq * LSEQ, EW[l // 2], gi(2), gi(3), scr)
            else:
                fox_phase(S, cx, hT, seq * LSEQ, OW[l // 2], gi(2), gi(3), scr)
        ffn_phase(S, cx, hT, ffn_in[l * 2 + 1], ffn_out[l * 2 + 1], gi(4), gi(5), TC // TT)
    S.finish()
    return nc, S


def kernel(x, meta, norm_g, ffn_in, ffn_out, e_w_in, e_conv_w, e_mu, e_w0, e_w2, e_a0, e_a2, e_g2, e_k_k, e_k_a, e_r_k,
           e_lnx_g, e_lnx_b, e_w_out, o_w_in, o_b_f, o_q_g, o_k_g, o_w_out):
    f = lambda a: np.asarray(a, dtype=np.float32)
    x, meta, norm_g = f(x), f(meta), f(norm_g)
    shared = {}
    shared["gcol"] = np.ascontiguousarray(norm_g.reshape(DEPTH * 6, KC, 128).transpose(2, 0, 1).reshape(128, DEPTH * 6 * KC))
    shared["consts"] = _consts_np()
    fi, fo = f(ffn_in), f(ffn_out)
    shared["ffn_in_r"] = np.stack([_prep_w_in(fi[l, k]) for l in range(DEPTH) for k in range(2)])
    shared["ffn_out_r"] = np.stack([_prep_w_out(fo[l, k], NJ) for l in range(DEPTH) for k in range(2)])
    ev = [e_w_in, e_conv_w, e_mu, e_w0, e_w2, e_a0, e_a2, e_g2, e_k_k, e_k_a, e_r_k, e_lnx_g, e_lnx_b, e_w_out]
    for i in range(2):
        for k, v in _prep_even(*[f(a)[i] for a in ev]).items():
            shared["e%d_%s" % (i, k)] = v
        for k, v in _prep_fox(f(o_w_in)[i], f(o_b_f)[i], f(o_q_g)[i], f(o_k_g)[i], f(o_w_out)[i]).items():
            shared["o%d_%s" % (i, k)] = v
    in_maps = []
    for c in range(NCORES):
        hs = [np.concatenate([meta, x[c * NSEQ + s]], axis=0) for s in range(NSEQ)]
        h0 = np.ascontiguousarray(np.concatenate(hs, axis=0).T)
        m = dict(shared)
        m["h0"] = h0
        in_maps.append(m)
    nc, _ = build_program()
    res = run_bass_kernel_spmd(nc, in_maps, core_ids=list(range(NCORES)))
    out = np.empty((NCORES * NSEQ, LSEQ - N_META, D), np.float32)
    for c in range(NCORES):
        hT = res.results[c]["hT"]
        for s in range(NSEQ):
            out[c * NSEQ + s] = hT[:, s * LSEQ + N_META:(s + 1) * LSEQ].T
    return out
```

```python
from contextlib import ExitStack
import numpy as np
import concourse.bass as bass
import concourse.mybir as mybir
from concourse.bass_utils import run_bass_kernel_spmd

F32 = mybir.dt.float32
BF16 = mybir.dt.bfloat16
AF = mybir.ActivationFunctionType
ALU = mybir.AluOpType
AX = mybir.AxisListType

D = 2048
KC = 16
DFF = 5632
NJ = 44
NSEQ = 2
LSEQ = 2064
TC = NSEQ * LSEQ
NCORES = 8
EPS = 1e-6


class Res:
    __slots__ = ("name", "w", "r")

    def __init__(self, name=""):
        self.name = name
        self.w = None
        self.r = {}


class Sched:
    ENG = ("pe", "act", "dve", "pool", "sp")

    def __init__(self, nc, n_dma_sems=40):
        self.nc = nc
        self.eng = {"pe": nc.tensor, "act": nc.scalar, "dve": nc.vector, "pool": nc.gpsimd, "sp": nc.sync}
        self.sem = {}
        self.cnt = {}
        self.seen = {e: {} for e in self.ENG}
        for e in self.ENG:
            self.sem[e] = nc.alloc_semaphore(name="sem_" + e)
            self.cnt[e] = 0
        self.n_dma = n_dma_sems
        for i in range(n_dma_sems):
            a = ("dma", i)
            self.sem[a] = nc.alloc_semaphore(name="sem_dma%d" % i)
            self.cnt[a] = 0
        self.dma_rr = 0
        self.n_inst = 0

    def _need(self, reads, writes):
        need = {}
        for r in reads:
            if r.w is not None:
                a, c = r.w
                if need.get(a, 0) < c:
                    need[a] = c
        for w in writes:
            if w.w is not None:
                a, c = w.w
                if need.get(a, 0) < c:
                    need[a] = c
            for a, c in w.r.items():
                if need.get(a, 0) < c:
                    need[a] = c
        return need

    def _wait(self, e, need):
        seen = self.seen[e]
        eng = self.eng[e]
        for a, c in need.items():
            if a == e and e == "pe":
                continue
            if seen.get(a, 0) < c:
                eng.wait_ge(self.sem[a], c)
                seen[a] = c
                self.n_inst += 1

    def op(self, e, fn, reads=(), writes=()):
        need = self._need(reads, writes)
        self._wait(e, need)
        ins = fn(self.eng[e])
        ins.then_inc(self.sem[e], 1)
        self.cnt[e] += 1
        self.n_inst += 1
        c = self.cnt[e]
        for r in reads:
            r.r[e] = c
        for w in writes:
            w.w = (e, c)
            w.r = {}
        return ins

    def mm_group(self, fns, reads=(), writes=()):
        need = self._need(reads, writes)
        self._wait("pe", need)
        ins = None
        for fn in fns:
            ins = fn(self.nc.tensor)
            self.n_inst += 1
        ins.then_inc(self.sem["pe"], 1)
        self.cnt["pe"] += 1
        c = self.cnt["pe"]
        for r in reads:
            r.r["pe"] = c
        for w in writes:
            w.w = ("pe", c)
            w.r = {}

    def dma(self, e, out, in_, reads=(), writes=(), **kw):
        i = self.dma_rr
        self.dma_rr = (self.dma_rr + 1) % self.n_dma
        a = ("dma", i)
        need = self._need(reads, writes)
        if self.cnt[a] > 0 and need.get(a, 0) < self.cnt[a]:
            need[a] = self.cnt[a]
        self._wait(e, need)
        ins = self.eng[e].dma_start(out=out, in_=in_, **kw)
        ins.then_inc(self.sem[a], 16)
        self.cnt[a] += 16
        self.n_inst += 1
        c = self.cnt[a]
        for r in reads:
            r.r[a] = c
        for w in writes:
            w.w = (a, c)
            w.r = {}
        return ins

    def barrier(self):
        tot = dict(self.cnt)
        for e in self.ENG:
            self._wait(e, {a: c for a, c in tot.items() if c > 0 and not (a == e)})

    def finish(self):
        tot = {a: c for a, c in self.cnt.items() if c > 0 and a != "sp"}
        self._wait("sp", tot)


TT = 688
SUB = 344
NSUB = TT // SUB


class Ctx:
    pass


_UID = [0]


def _u(n):
    return "%s_%d" % (n, _UID[0])

def rstd_from_banks(S, cx, ps_banks, dst, r_dst, nsub, sub, inv_n, eps_col=None):
    eps_col = cx.eps_col if eps_col is None else eps_col
    for s in range(nsub):
        b = ps_banks[s]
        S.op("act", lambda a, b=b, s=s: a.activation(
            out=dst[:, s * sub:(s + 1) * sub], in_=cx.bank[b][:, 0:sub], func=AF.Sqrt, scale=inv_n, bias=eps_col[:]),
            reads=[cx.r_bank[b]], writes=[r_dst])
    S.op("dve", lambda v: v.reciprocal(out=dst[:, 0:nsub * sub], in_=dst[:, 0:nsub * sub]), reads=[r_dst], writes=[r_dst])


class OutBufs:
    def __init__(self, nc, pfx, K):
        self.nc, self.pfx, self.K = nc, pfx, K

    def __enter__(self):
        nc, p, K = self.nc, self.pfx, self.K
        self._cms = [nc.sbuf_tensor(_u(p + "_rstd2"), [128, TT], F32), nc.sbuf_tensor(_u(p + "_y"), [128, KC, TT], F32),
                     nc.sbuf_tensor(_u(p + "_t1"), [128, 2, TT], F32), nc.sbuf_tensor(_u(p + "_wout"), [128, 2, K, 128], BF16)]
        self.rstd2, self.y, self.t1, self.wout = [c.__enter__() for c in self._cms]
        self.r_rstd2 = Res()
        self.r_y = [Res() for _ in range(KC)]
        self.r_t1 = [Res() for _ in range(2)]
        self.r_wout = [Res() for _ in range(2)]
        self.hb_i = [0]
        self.wo_i = [0]
        self.sq_i = [0]
        return self

    def __exit__(self, *a):
        for c in reversed(self._cms):
            c.__exit__(*a)


def out_proj_D(S, cx, ob, src, r_src, K, w_r, next_bank, pre_n=None):
    bank, r_bank = cx.bank, cx.r_bank
    y, wout, rstd2, sq = ob.y, ob.wout, ob.rstd2, ob.sq
    pending = []
    for n in range(KC):
        if pre_n is not None:
            pre_n(n)
        wo = ob.wo_i[0] % 2
        ob.wo_i[0] += 1
        S.dma("pool", wout[:, wo], w_r[n], writes=[ob.r_wout[wo]], max_dma_last_dim=8192)
        for s in range(NSUB):
            b = next_bank()
            S.mm_group([lambda t, k=k, b=b, s=s, wo=wo: t.matmul(
                bank[b][:, 0:SUB], wout[:, wo, k, :], src[:, k, s * SUB:(s + 1) * SUB],
                start=(k == 0), stop=(k == K - 1)) for k in range(K)],
                reads=[ob.r_wout[wo]] + list(r_src), writes=[r_bank[b]])
            S.op("act", lambda a, b=b, n=n, s=s: a.activation(out=y[:, n, s * SUB:(s + 1) * SUB],
                                                             in_=bank[b][:, 0:SUB], func=AF.Copy),
                 reads=[r_bank[b]], writes=[ob.r_y[n]])
            q = ob.sq_i[0] % 2
            ob.sq_i[0] += 1
            S.op("act", lambda a, b=b, q=q: a.activation(out=sq[:, q, 0:SUB], in_=bank[b][:, 0:SUB], func=AF.Square),
                 reads=[r_bank[b]], writes=[ob.r_sq[q]])
            if pending:
                pending.pop()()
            pending.append(lambda s=s, q=q, n=n: S.mm_group(
                [lambda t: t.matmul(bank[6 + s][:, 0:SUB], cx.ones_bf[:], sq[:, q, 0:SUB], start=(n == 0), stop=(n == KC - 1))],
                reads=[ob.r_sq[q]], writes=[r_bank[6 + s]]))
    pending.pop()()
    rstd_from_banks(S, cx, [6, 7], rstd2, ob.r_rstd2, NSUB, SUB, 1.0 / D)


def out_proj_E_step(S, cx, ob, hT3, t0, gain, gi, n):
    y, t1, rstd2, hbuf = ob.y, ob.t1, ob.rstd2, ob.hbuf
    r_h = cx.r_h
    hb = ob.hb_i[0] % 3
    ob.hb_i[0] += 1
    S.dma("sp", hbuf[:, hb, :], hT3[n, :, t0:t0 + TT], reads=[r_h[n]], writes=[ob.r_hbuf[hb]])
    q = n % 2
    S.op("dve", lambda v: v.scalar_tensor_tensor(
        out=t1[:, q, :], in0=y[:, n, :], scalar=gain[:, gi + n:gi + n + 1], in1=rstd2[:],
        op0=ALU.mult, op1=ALU.mult), reads=[ob.r_y[n], ob.r_rstd2], writes=[ob.r_t1[q]])
    S.op("dve", lambda v: v.tensor_tensor(out=t1[:, q, :], in0=t1[:, q, :], in1=hbuf[:, hb, :], op=ALU.add),
         reads=[ob.r_t1[q], ob.r_hbuf[hb]], writes=[ob.r_t1[q]])
    S.dma("sp", hT3[n, :, t0:t0 + TT], t1[:, q, :], reads=[ob.r_t1[q]], writes=[r_h[n]])


def out_proj_tile(S, cx, ob, hT3, t0, src, r_src, K, w_r, gain, gi, next_bank):
    out_proj_D(S, cx, ob, src, r_src, K, w_r, next_bank)
    for n in range(KC):
        out_proj_E_step(S, cx, ob, hT3, t0, gain, gi, n)


def ffn_phase(S, cx, hT, w_in_r, w_out_r, gi0, gi1, n_tiles):
    _UID[0] += 1
    nc = S.nc
    hT3 = hT.rearrange("(kc p) t -> kc p t", p=128)
    with (
        nc.sbuf_tensor(_u("f_hbuf"), [128, 3, TT], F32) as hbuf,
        nc.sbuf_tensor(_u("f_sq"), [128, 2, TT], BF16) as sq,
        nc.sbuf_tensor(_u("f_xn"), [128, KC, TT], BF16) as xn,
        nc.sbuf_tensor(_u("f_rstd"), [128, TT], F32) as rstd,
        nc.sbuf_tensor(_u("f_hid"), [128, NJ, TT], BF16) as hid,
        nc.sbuf_tensor(_u("f_sg"), [128, 2, SUB], F32) as sg,
        OutBufs(nc, "f", NJ) as ob,
        nc.sbuf_tensor(_u("f_win"), [128, 3, KC, 256], BF16) as win,
    ):
        r_hbuf = [Res() for _ in range(3)]
        r_sq = [Res() for _ in range(2)]
        r_xn = Res()
        r_rstd = Res()
        r_hid = [Res() for _ in range(NJ)]
        r_sg = [Res() for _ in range(2)]
        r_win = [Res() for _ in range(3)]
        r_h = cx.r_h
        bank = cx.bank
        r_bank = cx.r_bank
        nb = [0]

        def next_bank():
            b = nb[0] % 6
            nb[0] += 1
            return b

        ob.hbuf, ob.r_hbuf, ob.sq, ob.r_sq = hbuf, r_hbuf, sq, r_sq
        hb_i = ob.hb_i
        wi_i = [0]

        def rstd_from(ps_banks, dst, r_dst):
            rstd_from_banks(S, cx, ps_banks, dst, r_dst, NSUB, SUB, 1.0 / D)

        def stageA_step(ti, kc):
            t0 = ti * TT
            hb = hb_i[0] % 3
            hb_i[0] += 1
            S.dma("sp", hbuf[:, hb, :], hT3[kc, :, t0:t0 + TT], reads=[r_h[kc]], writes=[r_hbuf[hb]])
            q = ob.sq_i[0] % 2
            ob.sq_i[0] += 1
            S.op("act", lambda a: a.activation(out=sq[:, q, :], in_=hbuf[:, hb, :], func=AF.Square),
                 reads=[r_hbuf[hb]], writes=[r_sq[q]])
            S.mm_group([lambda t, s=s: t.matmul(bank[6 + s][:, 0:SUB], cx.ones_bf[:], sq[:, q, s * SUB:(s + 1) * SUB],
                                                start=(kc == 0), stop=(kc == KC - 1)) for s in range(NSUB)],
                       reads=[r_sq[q]], writes=[r_bank[6], r_bank[7]])
            if kc == KC - 1:
                rstd_from([6, 7], rstd, r_rstd)

        def stageB(ti):
            t0 = ti * TT
            for kc in range(KC):
                hb = hb_i[0] % 3
                hb_i[0] += 1
                S.dma("sp", hbuf[:, hb, :], hT3[kc, :, t0:t0 + TT], reads=[r_h[kc]], writes=[r_hbuf[hb]])
                S.op("dve", lambda v, hb=hb, kc=kc: v.scalar_tensor_tensor(
                    out=xn[:, kc, :], in0=hbuf[:, hb, :], scalar=cx.gcol[:, gi0 + kc:gi0 + kc + 1], in1=rstd[:],
                    op0=ALU.mult, op1=ALU.mult), reads=[r_hbuf[hb], r_rstd], writes=[r_xn])

        for kc in range(KC):
            stageA_step(0, kc)
        stageB(0)
        for ti in range(n_tiles):
            t0 = ti * TT
            for j in range(NJ):
                wi = wi_i[0] % 3
                wi_i[0] += 1
                S.dma("pool", win[:, wi], w_in_r[j], writes=[r_win[wi]], max_dma_last_dim=8192)
                banks_j = []
                for s in range(NSUB):
                    bg = next_bank()
                    bu = next_bank()
                    banks_j.append((bg, bu))
                    for half, b in ((0, bg), (1, bu)):
                        S.mm_group([lambda t, kc=kc, half=half, b=b, s=s, wi=wi: t.matmul(
                            bank[b][:, 0:SUB], win[:, wi, kc, half * 128:(half + 1) * 128],
                            xn[:, kc, s * SUB:(s + 1) * SUB], start=(kc == 0), stop=(kc == KC - 1))
                            for kc in range(KC)], reads=[r_win[wi], r_xn], writes=[r_bank[b]])
                if ti > 0 and j < KC:
                    out_proj_E_step(S, cx, ob, hT3, (ti - 1) * TT, cx.gcol_half, gi1, j)
                if ti + 1 < n_tiles and KC <= j < 2 * KC:
                    stageA_step(ti + 1, j - KC)
                for s in range(NSUB):
                    bg, bu = banks_j[s]
                    q = (j * NSUB + s) % 2
                    S.op("act", lambda a, q=q, bg=bg: a.activation(out=sg[:, q, :], in_=bank[bg][:, 0:SUB], func=AF.Silu),
                         reads=[r_bank[bg]], writes=[r_sg[q]])
                    S.op("dve", lambda v, q=q, bu=bu, j=j, s=s: v.tensor_tensor(
                        out=hid[:, j, s * SUB:(s + 1) * SUB], in0=sg[:, q, :], in1=bank[bu][:, 0:SUB], op=ALU.mult),
                        reads=[r_sg[q], r_bank[bu]], writes=[r_hid[j]])
            if ti + 1 < n_tiles:
                stageB(ti + 1)
            out_proj_D(S, cx, ob, hid, r_hid, NJ, w_out_r, next_bank)
        for n in range(KC):
            out_proj_E_step(S, cx, ob, hT3, (n_tiles - 1) * TT, cx.gcol_half, gi1, n)
        S.barrier()


def make_ctx(S, gcol_dram, n_g, consts_dram=None):
    nc = S.nc
    cx = Ctx()
    cx.bank = [nc.alloc_psum_tensor("bank%d" % i, [128, 512], F32) for i in range(8)]
    cx.r_bank = [Res("bank%d" % i) for i in range(8)]
    cx.ones_bf = nc.alloc_sbuf_tensor("ones_bf", [128, 128], BF16)
    cx.ones_f = nc.alloc_sbuf_tensor("ones_f", [128, 128], F32)
    cx.gcol = nc.alloc_sbuf_tensor("sb_gcol", [128, n_g], F32)
    cx.gcol_half = nc.alloc_sbuf_tensor("sb_gcol_half", [128, n_g], F32)
    cx.r_h = [Res("h%d" % k) for k in range(KC)]
    cx.eps_col = nc.alloc_sbuf_tensor("eps_col", [128, 1], F32)
    S.op("dve", lambda v: v.memset(cx.eps_col[:], EPS))
    cx.one_col = nc.alloc_sbuf_tensor("one_col", [128, 1], F32)
    S.op("dve", lambda v: v.memset(cx.one_col[:], 1.0))
    cx.r_scr = Res("scratch")
    cx.r_scr_c = Res("scratch_c")
    r_c = Res("consts")
    S.op("dve", lambda v: v.memset(cx.ones_bf[:], 1.0), writes=[r_c])
    S.op("dve", lambda v: v.memset(cx.ones_f[:], 1.0), writes=[r_c])
    S.dma("sp", cx.gcol[:], gcol_dram, writes=[r_c])
    S.op("dve", lambda v: v.tensor_scalar(out=cx.gcol_half[:], in0=cx.gcol[:], scalar1=0.5, scalar2=None, op0=ALU.mult),
         reads=[r_c], writes=[r_c])
    if consts_dram is not None:
        cx.ident_bf = nc.alloc_sbuf_tensor("ident_bf", [128, 128], BF16)
        cx.maskneg_bf = nc.alloc_sbuf_tensor("maskneg_bf", [128, 128], BF16)
        S.dma("pool", cx.ident_bf[:], consts_dram[0], writes=[r_c])
        S.dma("pool", cx.maskneg_bf[:], consts_dram[1], writes=[r_c])
        cx.ident_f = nc.alloc_sbuf_tensor("ident_f", [128, 128], F32)
        cx.blockones_f = nc.alloc_sbuf_tensor("blockones_f", [128, 128], F32)
        cx.blockones_bf = nc.alloc_sbuf_tensor("blockones_bf", [128, 128], BF16)
        cx.maskG = nc.alloc_sbuf_tensor("maskG", [128, 128], F32)
        cx.mask3 = nc.alloc_sbuf_tensor("mask3", [128, 64], F32)
        cx.identstack = nc.alloc_sbuf_tensor("identstack", [128, 64], F32)
        cx.lnx_eps_col = nc.alloc_sbuf_tensor("lnx_eps_col", [128, 1], F32)
        S.dma("sp", cx.ident_f[:], consts_dram[0], writes=[r_c])
        S.dma("sp", cx.blockones_f[:], consts_dram[2], writes=[r_c])
        S.dma("pool", cx.blockones_bf[:], consts_dram[2], writes=[r_c])
        S.dma("sp", cx.maskG[:], consts_dram[3], writes=[r_c])
        S.dma("sp", cx.mask3[:], consts_dram[4][:, 0:64], writes=[r_c])
        S.dma("sp", cx.identstack[:], consts_dram[5][:, 0:64], writes=[r_c])
        S.op("dve", lambda v: v.memset(cx.lnx_eps_col[:], LNX_EPS), writes=[r_c])
    S.barrier()
    return cx


NS6 = LSEQ // SUB


def seq_prenorm(S, cx, hT3, cb, xn, r_xn, hbig, r_hbig, sqb, r_sqb, rstd, r_rstd, gi):
    bank, r_bank = cx.bank, cx.r_bank
    for kc in range(KC):
        hb = kc % 2
        S.dma("sp", hbig[:, hb, :], hT3[kc, :, cb:cb + LSEQ], reads=[cx.r_h[kc]], writes=[r_hbig[hb]])
        S.op("act", lambda a, hb=hb: a.activation(out=sqb[:, hb, :], in_=hbig[:, hb, :], func=AF.Square),
             reads=[r_hbig[hb]], writes=[r_sqb[hb]])
        S.mm_group([lambda t, s=s, hb=hb, kc=kc: t.matmul(bank[s][:, 0:SUB], cx.ones_bf[:], sqb[:, hb, s * SUB:(s + 1) * SUB],
                                                         start=(kc == 0), stop=(kc == KC - 1)) for s in range(NS6)],
                   reads=[r_sqb[hb]], writes=[r_bank[s] for s in range(NS6)])
    rstd_from_banks(S, cx, list(range(NS6)), rstd, r_rstd, NS6, SUB, 1.0 / D)
    for kc in range(KC):
        hb = kc % 2
        S.dma("sp", hbig[:, hb, :], hT3[kc, :, cb:cb + LSEQ], reads=[cx.r_h[kc]], writes=[r_hbig[hb]])
        S.op("dve", lambda v, hb=hb, kc=kc: v.scalar_tensor_tensor(
            out=xn[:, kc, :], in0=hbig[:, hb, :], scalar=cx.gcol[:, gi + kc:gi + kc + 1], in1=rstd[:],
            op0=ALU.mult, op1=ALU.mult), reads=[r_hbig[hb], r_rstd], writes=[r_xn])


def mixer_out_phase(S, cx, hT, cb, src_scr, w_r, gi):
    _UID[0] += 1
    nc = S.nc
    hT3 = hT.rearrange("(kc p) t -> kc p t", p=128)
    src3 = src_scr.rearrange("(kc p) t -> p kc t", p=128)
    with (
        nc.sbuf_tensor(_u("mo_hbuf"), [128, 3, TT], F32) as hbuf,
        nc.sbuf_tensor(_u("mo_sq"), [128, 2, TT], BF16) as sq,
        nc.sbuf_tensor(_u("mo_src"), [128, 2, KC, TT], BF16) as src,
        OutBufs(nc, "mo", KC) as ob,
    ):
        ob.hbuf, ob.r_hbuf = hbuf, [Res() for _ in range(3)]
        ob.sq, ob.r_sq = sq, [Res() for _ in range(2)]
        r_src = [Res(), Res()]
        nb = [0]

        def next_bank():
            b = nb[0] % 6
            nb[0] += 1
            return b
        nt = LSEQ // TT
        for ti in range(nt):
            t0 = ti * TT
            sb = ti % 2
            S.dma("sp", src[:, sb], src3[:, :, t0:t0 + TT], reads=[cx.r_scr], writes=[r_src[sb]])
            pre = None
            if ti > 0:
                pre = lambda n, tp=cb + (ti - 1) * TT: out_proj_E_step(S, cx, ob, hT3, tp, cx.gcol, gi, n)
            out_proj_D(S, cx, ob, src[:, sb], [r_src[sb]], KC, w_r, next_bank, pre_n=pre)
        for n in range(KC):
            out_proj_E_step(S, cx, ob, hT3, cb + (nt - 1) * TT, cx.gcol, gi, n)
        S.barrier()


HD = 16
NBLK = 17
QT = [(0, 512), (512, 512), (1024, 512), (1536, 512), (2048, 16)]


def fox_phase(S, cx, hT, cb, W, gi2, gi3, scr):
    _UID[0] += 1
    nc = S.nc
    L = LSEQ
    hT3 = hT.rearrange("(kc p) t -> kc p t", p=128)
    bank, r_bank = cx.bank, cx.r_bank
    with (
        nc.sbuf_tensor(_u("x_xn"), [128, KC, L], BF16) as xn,
        nc.sbuf_tensor(_u("x_small"), [128, 4], F32) as small,
        nc.sbuf_tensor(_u("x_cq"), [6, 2, L], BF16) as cq,
        nc.sbuf_tensor(_u("x_ck"), [6, 2, L], BF16) as ck,
    ):
        r_xn, r_small = Res(), Res()
        r_cq, r_ck = [Res(), Res()], [Res(), Res()]
        S.dma("sp", small[:], W["small"], writes=[r_small])
        S.op("dve", lambda v: v.tensor_scalar(out=small[:, 0:1], in0=small[:, 0:1], scalar1=128 ** -0.5, scalar2=None, op0=ALU.mult),
             reads=[r_small], writes=[r_small])
        S.op("dve", lambda v: v.tensor_scalar(out=small[0:16, 2:3], in0=small[0:16, 2:3], scalar1=-1.0, scalar2=None, op0=ALU.mult),
             reads=[r_small], writes=[r_small])
        with (
            nc.sbuf_tensor(_u("x_hbig"), [128, 2, L], F32) as hbig,
            nc.sbuf_tensor(_u("x_sqb"), [128, 2, L], BF16) as sqb,
            nc.sbuf_tensor(_u("x_rstd"), [128, L], F32) as rstd,
            nc.sbuf_tensor(_u("x_wf"), [128, KC, 16], BF16) as wf,
            nc.sbuf_tensor(_u("x_e"), [16, L], F32) as ee,
            nc.sbuf_tensor(_u("x_cp"), [16, L], F32) as cp,
            nc.sbuf_tensor(_u("x_r1"), [16, L], F32) as r1,
            nc.sbuf_tensor(_u("x_c6"), [16, 6, L], BF16) as c6,
        ):
            r_hbig, r_sqb, r_rstd = [Res(), Res()], [Res(), Res()], Res()
            r_wf, r_e, r_cp, r_r1, r_c6 = Res(), Res(), Res(), Res(), Res()
            seq_prenorm(S, cx, hT3, cb, xn, r_xn, hbig, r_hbig, sqb, r_sqb, rstd, r_rstd, gi2)
            S.dma("pool", wf[:], W["wf_r"], writes=[r_wf])
            for s in range(NS6):
                b = s % 4
                S.mm_group([lambda t, kc=kc, b=b, s=s: t.matmul(bank[b][0:16, 0:SUB], wf[:, kc, :], xn[:, kc, s * SUB:(s + 1) * SUB],
                                                               start=(kc == 0), stop=(kc == KC - 1)) for kc in range(KC)],
                           reads=[r_wf, r_xn], writes=[r_bank[b]])
                S.op("act", lambda a, b=b, s=s: a.activation(out=ee[:, s * SUB:(s + 1) * SUB], in_=bank[b][0:16, 0:SUB],
                                                             func=AF.Exp, scale=-1.0, bias=small[0:16, 2:3]),
                     reads=[r_bank[b], r_small], writes=[r_e])
            S.op("act", lambda a: a.activation(out=ee[:], in_=ee[:], func=AF.Ln, bias=cx.one_col[0:16, :], scale=1.0),
                 reads=[r_e], writes=[r_e])
            S.op("dve", lambda v: v.tensor_tensor_scan(out=cp[:], data0=cx.ones_f[0:16, 0:1].to_broadcast([16, L]), data1=ee[:],
                                                       initial=0.0, op0=ALU.mult, op1=ALU.add), reads=[r_e], writes=[r_cp])
            S.op("dve", lambda v: v.tensor_copy(out=c6[:, 0, :], in_=cp[:]), reads=[r_cp], writes=[r_c6])
            S.op("dve", lambda v: v.tensor_tensor(out=r1[:], in0=cp[:], in1=c6[:, 0, :], op=ALU.subtract), reads=[r_cp, r_c6], writes=[r_r1])
            S.op("dve", lambda v: v.tensor_copy(out=c6[:, 1, :], in_=r1[:]), reads=[r_r1], writes=[r_c6])
            S.op("dve", lambda v: v.tensor_tensor(out=r1[:], in0=r1[:], in1=c6[:, 1, :], op=ALU.subtract), reads=[r_r1, r_c6], writes=[r_r1])
            S.op("dve", lambda v: v.tensor_copy(out=c6[:, 2, :], in_=r1[:]), reads=[r_r1], writes=[r_c6])
            S.op("dve", lambda v: v.tensor_scalar(out=c6[:, 3:6, :], in0=c6[:, 0:3, :], scalar1=-1.0, scalar2=None, op0=ALU.mult),
                 reads=[r_c6], writes=[r_c6])
            S.dma("sp", scr["c"].rearrange("r h t -> h r t"), c6[:], reads=[r_c6], writes=[cx.r_scr_c])
            S.op("dve", lambda v: v.memset(cq[:], 1.0), writes=r_cq)
            S.op("dve", lambda v: v.memset(ck[:], 1.0), writes=r_ck)
            S.barrier()
        with (
            nc.sbuf_tensor(_u("x_wsl"), [128, 2, 4, KC, 128], BF16) as wsl,
            nc.sbuf_tensor(_u("x_qn"), [128, L], BF16) as qn,
            nc.sbuf_tensor(_u("x_kn"), [128, L], BF16) as kn,
            nc.sbuf_tensor(_u("x_vT"), [128, L], BF16) as vT,
            nc.sbuf_tensor(_u("x_vtok"), [128, NBLK, 128], BF16) as vtok,
            nc.sbuf_tensor(_u("x_sig"), [128, L], F32) as sig,
            nc.sbuf_tensor(_u("x_P"), [128, 4, 512], BF16) as P,
            nc.sbuf_tensor(_u("x_rd"), [128, 512], F32) as rd,
            nc.sbuf_tensor(_u("x_tmp"), [128, 512], F32) as tmp,
            nc.sbuf_tensor(_u("x_oo"), [128, 2, L], BF16) as oo,
            nc.sbuf_tensor(_u("x_sqs"), [128, 2, SUB], BF16) as sqs,
            nc.sbuf_tensor(_u("x_rq"), [128, 2, SUB], F32) as rq,
        ):
            r_wsl = [Res(), Res()]
            r_qn, r_kn, r_vT, r_vtok, r_sig = Res(), Res(), Res(), Res(), Res()
            r_P = [Res() for _ in range(4)]
            r_rd, r_tmp = Res(), Res()
            r_oo = [Res(), Res()]
            r_sqs, r_rq = [Res(), Res()], [Res(), Res()]
            pa_i = [0]

            def pbank_all():
                b = pa_i[0] % 8
                pa_i[0] += 1
                return b
            pb_i = [0]

            def pbank():
                b = (0, 1, 2, 7)[pb_i[0] % 4]
                pb_i[0] += 1
                return b
            sq_i = [0]
            p_i = [0]
            for hd in range(HD):
                wb = hd % 2
                S.dma("pool", wsl[:, wb], W["win_r"][:, hd].rearrange("f p k c -> p f k c"), writes=[r_wsl[wb]], max_dma_last_dim=8192)
                S.dma("sp", cq[0:3, wb, :], scr["c"][3:6, hd, :], reads=[cx.r_scr_c], writes=[r_cq[wb]])
                S.dma("sp", ck[3:6, wb, :], scr["c"][0:3, hd, :], reads=[cx.r_scr_c], writes=[r_ck[wb]])
                pend = []

                def norm_chain(b, q, s, dst, r_dst, gc):
                    b2 = pbank_all()
                    S.mm_group([lambda t: t.matmul(bank[b2][:, 0:SUB], cx.ones_bf[:], sqs[:, q, :], start=True, stop=True)],
                               reads=[r_sqs[q]], writes=[r_bank[b2]])
                    S.op("act", lambda a: a.activation(out=rq[:, q, :], in_=bank[b2][:, 0:SUB], func=AF.Sqrt,
                                                       scale=1.0 / 128, bias=cx.eps_col[:]),
                         reads=[r_bank[b2]], writes=[r_rq[q]])
                    S.op("dve", lambda v: v.reciprocal(out=rq[:, q, :], in_=rq[:, q, :]), reads=[r_rq[q]], writes=[r_rq[q]])
                    S.op("dve", lambda v: v.scalar_tensor_tensor(
                        out=dst[:, s * SUB:(s + 1) * SUB], in0=bank[b][:, 0:SUB], scalar=small[:, gc:gc + 1], in1=rq[:, q, :],
                        op0=ALU.mult, op1=ALU.mult), reads=[r_bank[b], r_rq[q], r_small], writes=[r_dst])
                for which, dst, r_dst, gc in ((0, qn, r_qn, 0), (1, kn, r_kn, 1)):
                    for s in range(NS6):
                        b = pbank_all()
                        S.mm_group([lambda t, kc=kc, b=b, s=s, which=which: t.matmul(
                            bank[b][:, 0:SUB], wsl[:, wb, which, kc, :], xn[:, kc, s * SUB:(s + 1) * SUB],
                            start=(kc == 0), stop=(kc == KC - 1)) for kc in range(KC)], reads=[r_wsl[wb], r_xn], writes=[r_bank[b]])
                        q = sq_i[0] % 2
                        sq_i[0] += 1
                        S.op("act", lambda a, b=b, q=q: a.activation(out=sqs[:, q, :], in_=bank[b][:, 0:SUB], func=AF.Square),
                             reads=[r_bank[b]], writes=[r_sqs[q]])
                        if pend:
                            pend.pop()()
                        pend.append(lambda b=b, q=q, s=s, dst=dst, r_dst=r_dst, gc=gc: norm_chain(b, q, s, dst, r_dst, gc))
                for s in range(NS6):
                    if s == 1 and pend:
                        pend.pop()()
                    b = pbank_all()
                    S.mm_group([lambda t, kc=kc, b=b, s=s: t.matmul(bank[b][:, 0:SUB], wsl[:, wb, 2, kc, :], xn[:, kc, s * SUB:(s + 1) * SUB],
                                                                   start=(kc == 0), stop=(kc == KC - 1)) for kc in range(KC)],
                               reads=[r_wsl[wb], r_xn], writes=[r_bank[b]])
                    S.op("act", lambda a, b=b, s=s: a.activation(out=vT[:, s * SUB:(s + 1) * SUB], in_=bank[b][:, 0:SUB], func=AF.Copy),
                         reads=[r_bank[b]], writes=[r_vT])
                    b = pbank_all()
                    S.mm_group([lambda t, kc=kc, b=b, s=s: t.matmul(bank[b][:, 0:SUB], wsl[:, wb, 3, kc, :], xn[:, kc, s * SUB:(s + 1) * SUB],
                                                                   start=(kc == 0), stop=(kc == KC - 1)) for kc in range(KC)],
                               reads=[r_wsl[wb], r_xn], writes=[r_bank[b]])
                    S.op("act", lambda a, b=b, s=s: a.activation(out=sig[:, s * SUB:(s + 1) * SUB], in_=bank[b][:, 0:SUB], func=AF.Sigmoid),
                         reads=[r_bank[b]], writes=[r_sig])
                for g0 in range(0, NBLK, 8):
                    b = pbank_all()
                    bb = bank[b][:].bitcast(BF16)
                    blks = list(range(g0, min(NBLK, g0 + 8)))
                    S.mm_group([lambda t, j=j, bb=bb, g0=g0: t.transpose(
                        bb[0:min(128, L - j * 128), (j - g0) * 128:(j - g0 + 1) * 128], vT[:, j * 128:min(L, (j + 1) * 128)], cx.ident_bf[:])
                        for j in blks], reads=[r_vT], writes=[r_bank[b]])
                    nfull = sum(1 for j in blks if (j + 1) * 128 <= L)
                    if nfull:
                        S.op("dve", lambda v, bb=bb, g0=g0, nfull=nfull: v.tensor_copy(
                            out=vtok[:, g0:g0 + nfull, :], in_=bb[:, 0:nfull * 128].rearrange("p (j d) -> p j d", d=128)),
                            reads=[r_bank[b]], writes=[r_vtok])
                    if nfull < len(blks):
                        j = blks[-1]
                        rows = L - j * 128
                        S.op("dve", lambda v, bb=bb, g0=g0, j=j, rows=rows: v.tensor_copy(
                            out=vtok[0:rows, j, :], in_=bb[0:rows, (j - g0) * 128:(j - g0 + 1) * 128]),
                            reads=[r_bank[b]], writes=[r_vtok])
                ob_ = hd % 2
                for qi, (t0, Wd) in enumerate(QT):
                    ao, ad = ((3, 4), (5, 6))[qi % 2]
                    nkb = min(NBLK, (t0 + Wd + 127) // 128)
                    pv_pend = []
                    for j in range(nkb):
                        s0 = j * 128
                        rows = min(128, L - s0)
                        c0 = max(0, s0 - t0)
                        diag = s0 >= t0
                        b = pbank()
                        fns = [lambda t, b=b, s0=s0, rows=rows, c0=c0: t.matmul(
                                   bank[b][0:rows, c0:Wd], kn[:, s0:s0 + rows], qn[:, t0 + c0:t0 + Wd], start=True, stop=False),
                               lambda t, b=b, s0=s0, rows=rows, c0=c0: t.matmul(
                                   bank[b][0:rows, c0:Wd], ck[0:6, wb, s0:s0 + rows], cq[0:6, wb, t0 + c0:t0 + Wd], start=False, stop=not diag)]
                        if diag:
                            dw = min(128, Wd - c0)
                            fns.append(lambda t, b=b, rows=rows, c0=c0, dw=dw: t.matmul(
                                bank[b][0:rows, c0:c0 + dw], cx.ident_bf[0:rows, 0:rows], cx.maskneg_bf[0:rows, 0:dw], start=False, stop=True))
                        S.mm_group(fns, reads=[r_kn, r_qn, r_ck[wb], r_cq[wb]], writes=[r_bank[b]])
                        pi = p_i[0] % 4
                        p_i[0] += 1
                        S.op("act", lambda a, b=b, rows=rows, c0=c0, pi=pi: a.activation(
                            out=P[0:rows, pi, c0:Wd], in_=bank[b][0:rows, c0:Wd], func=AF.Exp), reads=[r_bank[b]], writes=[r_P[pi]])
                        if pv_pend:
                            pv_pend.pop()()
                        pv_pend.append(lambda rows=rows, c0=c0, pi=pi, j=j, ao=ao, ad=ad, Wd=Wd, nkb=nkb: S.mm_group(
                            [lambda t: t.matmul(bank[ao][:, c0:Wd], vtok[0:rows, j, :], P[0:rows, pi, c0:Wd], start=(j == 0), stop=(j == nkb - 1)),
                             lambda t: t.matmul(bank[ad][:, c0:Wd], cx.ones_bf[0:rows, :], P[0:rows, pi, c0:Wd], start=(j == 0), stop=(j == nkb - 1))],
                            reads=[r_vtok, r_P[pi]], writes=[r_bank[ao], r_bank[ad]]))
                    pv_pend.pop()()
                    S.op("dve", lambda v, ad=ad: v.reciprocal(out=rd[:, 0:Wd], in_=bank[ad][:, 0:Wd]), reads=[r_bank[ad]], writes=[r_rd])
                    S.op("dve", lambda v, ao=ao: v.tensor_tensor(out=tmp[:, 0:Wd], in0=bank[ao][:, 0:Wd], in1=rd[:, 0:Wd], op=ALU.mult),
                         reads=[r_bank[ao], r_rd], writes=[r_tmp])
                    S.op("dve", lambda v, t0=t0: v.tensor_tensor(out=oo[:, ob_, t0:t0 + Wd], in0=tmp[:, 0:Wd], in1=sig[:, t0:t0 + Wd], op=ALU.mult),
                         reads=[r_tmp, r_sig], writes=[r_oo[ob_]])
                S.dma("sp", scr["og"][hd * 128:(hd + 1) * 128, :], oo[:, ob_, :], reads=[r_oo[ob_]], writes=[cx.r_scr])
            S.barrier()
    mixer_out_phase(S, cx, hT, cb, scr["og"], W["wout_r"], gi3)


WT = 704
NCH = 11
CH = 64
SW = 352
LPAD = 3 * WT
MU, CW_, W0_, A0_, KK_, KA_, RK_, LG_, LB_, NSM = 0, 27, 51, 59, 67, 75, 83, 91, 99, 107
LNX_EPS = 64e-5
DEC = 0.6065306597126334


def even_phase(S, cx, hT, cb, W, gi2, gi3, scr):
    _UID[0] += 1
    nc = S.nc
    hT3 = hT.rearrange("(kc p) t -> kc p t", p=128)
    bank, r_bank = cx.bank, cx.r_bank
    ym = scr["ym"]
    nb_i = [0]

    def nb():
        b = nb_i[0] % 8
        nb_i[0] += 1
        return b

    def V(fn, reads, writes):
        S.op("dve", fn, reads, writes)

    def A(fn, reads, writes):
        S.op("act", fn, reads, writes)

    F = lambda name: nc.sbuf_tensor(name, [128, WT], F32)
    names_f = ["ur", "uk", "uv", "aa", "kkn", "kf", "bs", "lw", "cw", "E", "BtT", "KtT", "BhT", "KhT", "gg", "bonv",
               "t1", "t2", "t3", "Y"]
    with ExitStack() as es:
        xn = es.enter_context(nc.sbuf_tensor(_u("e_xn"), [128, KC, WT], BF16))
        hbuf = es.enter_context(nc.sbuf_tensor(_u("e_hbuf"), [128, 2, WT], F32))
        sq = es.enter_context(nc.sbuf_tensor(_u("e_sq"), [128, 2, WT], BF16))
        rstd = es.enter_context(nc.sbuf_tensor(_u("e_rstd"), [128, WT], F32))
        lowA = es.enter_context(nc.sbuf_tensor(_u("e_lowA"), [128, WT], BF16))
        lowG1 = es.enter_context(nc.sbuf_tensor(_u("e_lowG1"), [128, WT], BF16))
        lowG2 = es.enter_context(nc.sbuf_tensor(_u("e_lowG2"), [32, WT], BF16))
        raw = es.enter_context(nc.sbuf_tensor(_u("e_raw"), [128, 2, WT + 2], F32))
        AR = es.enter_context(nc.sbuf_tensor(_u("e_AR"), [128, 2, WT], F32))
        tok = es.enter_context(nc.sbuf_tensor(_u("e_tok"), [128, 4, NCH, CH], F32))
        G1s = es.enter_context(nc.sbuf_tensor(_u("e_G1s"), [128, NCH, 128], F32))
        G2s = es.enter_context(nc.sbuf_tensor(_u("e_G2s"), [128, NCH, 128], F32))
        NM = es.enter_context(nc.sbuf_tensor(_u("e_NM"), [128, 2, 2, NCH, CH], BF16))
        Xb = es.enter_context(nc.sbuf_tensor(_u("e_Xb"), [128, NCH, 128], BF16))
        X = es.enter_context(nc.sbuf_tensor(_u("e_X"), [128, NCH, 128], F32))
        GamT = es.enter_context(nc.sbuf_tensor(_u("e_GamT"), [128, NCH, CH], F32))
        WC = es.enter_context(nc.sbuf_tensor(_u("e_WC"), [128, NCH], F32))
        Zb = es.enter_context(nc.sbuf_tensor(_u("e_Z"), [128, 2, CH], F32))
        Zst = es.enter_context(nc.sbuf_tensor(_u("e_Zst"), [128, 8, CH], F32))
        carry = es.enter_context(nc.sbuf_tensor(_u("e_carry"), [128, 27], F32))
        zcarry = es.enter_context(nc.sbuf_tensor(_u("e_zcarry"), [128, 8, 2], F32))
        wsl = es.enter_context(nc.sbuf_tensor(_u("e_wsl"), [128, 2, 3, KC, 128], BF16))
        wlow = es.enter_context(nc.sbuf_tensor(_u("e_wlow"), [128, KC, 288], BF16))
        w2a2 = es.enter_context(nc.sbuf_tensor(_u("e_w2a2"), [128, 1024], BF16))
        g2a = es.enter_context(nc.sbuf_tensor(_u("e_g2a"), [128, 1024], BF16))
        g2b = es.enter_context(nc.sbuf_tensor(_u("e_g2b"), [32, 1024], BF16))
        sm = es.enter_context(nc.sbuf_tensor(_u("e_small"), [128, NSM], F32))
        omm = es.enter_context(nc.sbuf_tensor(_u("e_omm"), [128, 27], F32))
        omka = es.enter_context(nc.sbuf_tensor(_u("e_omka"), [128, 8], F32))
        rmask = es.enter_context(nc.sbuf_tensor(_u("e_rmask"), [128, WT], F32))
        sqs = es.enter_context(nc.sbuf_tensor(_u("e_sqs"), [128, 2, SW], BF16))
        ybf = es.enter_context(nc.sbuf_tensor(_u("e_ybf"), [128, 2, WT], BF16))
        Fall = es.enter_context(nc.sbuf_tensor(_u("e_F"), [128, len(names_f), WT], F32))
        Fd = {n: Fall[:, i, :] for i, n in enumerate(names_f)}
        rF = {n: Res(n) for n in names_f}
        r_xn, r_rstd = Res(), Res()
        r_hbuf, r_sq = [Res(), Res()], [Res(), Res()]
        r_lowA, r_lowG1, r_lowG2 = Res(), Res(), Res()
        r_raw = [Res(), Res()]
        r_AR = Res()
        r_tok = [Res() for _ in range(4)]
        r_G1s, r_G2s = Res(), Res()
        r_NMc = [[[Res() for _ in range(NCH)] for _ in range(2)] for _ in range(2)]
        r_Xc = [Res() for _ in range(NCH)]
        r_Xbc = [Res() for _ in range(NCH)]
        r_GamT, r_WC = Res(), Res()
        r_Zb = [Res(), Res()]
        r_Zst, r_carry, r_zcarry = Res(), Res(), Res()
        r_wsl = [Res(), Res()]
        r_wlow, r_w2a2, r_g2, r_sm = Res(), Res(), Res(), Res()
        r_sqs = [Res(), Res()]
        r_ybf = [Res(), Res()]
        S.dma("sp", sm[:], W["esmall"], writes=[r_sm])
        S.dma("pool", wlow[:], W["ew_low_r"], writes=[r_wlow], max_dma_last_dim=8192)
        S.dma("pool", w2a2[:], W["w2a2"], writes=[r_w2a2])
        S.dma("pool", g2a[:], W["g2"][0:128, :], writes=[r_g2])
        S.dma("pool", g2b[:], W["g2"][128:160, :], writes=[r_g2])
        V(lambda v: v.tensor_scalar(out=omm[:], in0=sm[:, MU:MU + 27], scalar1=-1.0, scalar2=1.0, op0=ALU.mult, op1=ALU.add),
          [r_sm], [r_sm])
        V(lambda v: v.tensor_scalar(out=omka[:], in0=sm[:, KA_:KA_ + 8], scalar1=-1.0, scalar2=1.0, op0=ALU.mult, op1=ALU.add),
          [r_sm], [r_sm])
        V(lambda v: v.memset(rmask[:], 1.0), [], [r_sm])
        V(lambda v: v.memset(rmask[:].rearrange("p (c t) -> p c t", t=CH)[:, :, 0:1], 0.0), [], [r_sm])
        V(lambda v: v.memset(carry[:], 0.0), [], [r_carry])
        V(lambda v: v.memset(zcarry[:], 0.0), [], [r_zcarry])
        V(lambda v: v.memset(Zst[:], 0.0), [], [r_Zst])
        sq_i = [0]
        raw_i = [0]
        w_i = [0]
        wq = []

        def load_slab(kind, idx):
            wb = w_i[0] % 2
            w_i[0] += 1
            src = W["ew_conv_r"] if kind == 0 else W["ew_rkv_r"]
            S.dma("pool", wsl[:, wb], src[idx].rearrange("f p k c -> p f k c"), writes=[r_wsl[wb]], max_dma_last_dim=8192)
            return wb

        def proj_bank(lhs_fn, m, s, reads):
            b = nb()
            S.mm_group([lambda t, kc=kc, b=b: t.matmul(bank[b][0:m, 0:SW], lhs_fn(kc), xn[:, kc, s * SW:(s + 1) * SW],
                                                       start=(kc == 0), stop=(kc == KC - 1)) for kc in range(KC)],
                       reads=reads + [r_xn], writes=[r_bank[b]])
            return b

        def proj_shift(lhs_fn, m, reads, ci, out_ap, r_out, first_tile):
            rb = raw_i[0] % 2
            raw_i[0] += 1
            for s in range(2):
                b = proj_bank(lhs_fn, m, s, reads)
                A(lambda a, b=b, s=s: a.activation(out=raw[0:m, rb, 1 + s * SW:1 + (s + 1) * SW], in_=bank[b][0:m, 0:SW], func=AF.Copy),
                  [r_bank[b]], [r_raw[rb]])
            A(lambda a: a.activation(out=raw[0:m, rb, 0:1], in_=carry[0:m, ci:ci + 1], func=AF.Copy), [r_carry], [r_raw[rb]])
            A(lambda a: a.activation(out=Fd["t1"][0:m, :], in_=raw[0:m, rb, 0:WT], func=AF.Copy, scale=sm[0:m, MU + ci:MU + ci + 1]),
              [r_raw[rb], r_sm], [rF["t1"]])
            V(lambda v: v.scalar_tensor_tensor(out=out_ap, in0=raw[0:m, rb, 1:WT + 1], scalar=omm[0:m, ci:ci + 1], in1=Fd["t1"][0:m, :],
                                               op0=ALU.mult, op1=ALU.add), [r_raw[rb], rF["t1"], r_sm], [r_out])
            A(lambda a: a.activation(out=carry[0:m, ci:ci + 1], in_=raw[0:m, rb, WT:WT + 1], func=AF.Copy), [r_raw[rb]], [r_carry])

        def head_sum(src_bf_fn, reads, s):
            b = nb()
            S.mm_group([lambda t, b=b: t.matmul(bank[b][:, 0:SW], cx.blockones_bf[:], src_bf_fn(), start=True, stop=True)],
                       reads=reads, writes=[r_bank[b]])
            return b

        for ti in range(3):
            t0 = ti * WT
            wv = min(WT, LSEQ - t0)
            first = ti == 0
            if wv < WT:
                V(lambda v: v.memset(xn[:, :, wv:WT], 0.0), [], [r_xn])
            for kc in range(KC):
                hb = kc % 2
                S.dma("sp", hbuf[:, hb, 0:wv], hT3[kc, :, cb + t0:cb + t0 + wv], reads=[cx.r_h[kc]], writes=[r_hbuf[hb]])
                A(lambda a, hb=hb: a.activation(out=sq[:, hb, 0:wv], in_=hbuf[:, hb, 0:wv], func=AF.Square), [r_hbuf[hb]], [r_sq[hb]])
                if wv < WT:
                    A(lambda a, hb=hb: a.activation(out=sq[:, hb, wv:WT], in_=xn[:, 0, wv:WT], func=AF.Copy), [r_xn], [r_sq[hb]])
                S.mm_group([lambda t, s=s, hb=hb, kc=kc: t.matmul(bank[s][:, 0:SW], cx.ones_bf[:], sq[:, hb, s * SW:(s + 1) * SW],
                                                                 start=(kc == 0), stop=(kc == KC - 1)) for s in range(2)],
                           reads=[r_sq[hb]], writes=[r_bank[0], r_bank[1]])
            rstd_from_banks(S, cx, [0, 1], rstd, r_rstd, 2, SW, 1.0 / D)
            for kc in range(KC):
                hb = kc % 2
                S.dma("sp", hbuf[:, hb, 0:wv], hT3[kc, :, cb + t0:cb + t0 + wv], reads=[cx.r_h[kc]], writes=[r_hbuf[hb]])
                V(lambda v, hb=hb, kc=kc: v.scalar_tensor_tensor(out=xn[:, kc, 0:wv], in0=hbuf[:, hb, 0:wv],
                                                                 scalar=cx.gcol[:, gi2 + kc:gi2 + kc + 1], in1=rstd[:, 0:wv],
                                                                 op0=ALU.mult, op1=ALU.mult), [r_hbuf[hb], r_rstd], [r_xn])
            proj_shift(lambda kc: wlow[:, kc, 0:128], 128, [r_wlow], 24, Fd["t2"], rF["t2"], first)
            A(lambda a: a.activation(out=lowA[0:64, :], in_=Fd["t2"][0:64, :], func=AF.Tanh), [rF["t2"]], [r_lowA])
            A(lambda a: a.activation(out=lowA[64:128, :], in_=Fd["t2"][64:128, :], func=AF.Copy), [rF["t2"]], [r_lowA])
            proj_shift(lambda kc: wlow[:, kc, 128:256], 128, [r_wlow], 25, Fd["t2"], rF["t2"], first)
            A(lambda a: a.activation(out=lowG1[:], in_=Fd["t2"], func=AF.Sigmoid), [rF["t2"]], [r_lowG1])
            proj_shift(lambda kc: wlow[:, kc, 256:288], 32, [r_wlow], 26, Fd["t2"][0:32, :], rF["t2"], first)
            A(lambda a: a.activation(out=lowG2[:], in_=Fd["t2"][0:32, :], func=AF.Sigmoid), [rF["t2"]], [r_lowG2])
            for jc in range(8):
                wb = load_slab(0, jc)
                gb, gc, zb = Fd["t2"], Fd["t3"], raw
                rb = raw_i[0] % 2
                raw_i[0] += 1
                for s in range(2):
                    b0 = proj_bank(lambda kc: wsl[:, wb, 0, kc, :], 128, s, [r_wsl[wb]])
                    A(lambda a, b0=b0, s=s: a.activation(out=gb[:, s * SW:(s + 1) * SW], in_=bank[b0][:, 0:SW], func=AF.Copy),
                      [r_bank[b0]], [rF["t2"]])
                    b1 = proj_bank(lambda kc: wsl[:, wb, 1, kc, :], 128, s, [r_wsl[wb]])
                    A(lambda a, b1=b1, s=s: a.activation(out=gc[:, s * SW:(s + 1) * SW], in_=bank[b1][:, 0:SW], func=AF.Copy),
                      [r_bank[b1]], [rF["t3"]])
                    b2 = proj_bank(lambda kc: wsl[:, wb, 2, kc, :], 128, s, [r_wsl[wb]])
                    V(lambda v, b2=b2, s=s: v.tensor_tensor(out=zb[:, rb, 2 + s * SW:2 + (s + 1) * SW], in0=gc[:, s * SW:(s + 1) * SW],
                                                            in1=bank[b2][:, 0:SW], op=ALU.mult), [rF["t3"], r_bank[b2]], [r_raw[rb]])
                V(lambda v: v.tensor_copy(out=zb[:, rb, 0:2], in_=zcarry[:, jc, :]), [r_zcarry], [r_raw[rb]])
                A(lambda a: a.activation(out=Fd["t1"], in_=zb[:, rb, 0:WT], func=AF.Copy, scale=sm[:, CW_ + jc:CW_ + jc + 1]),
                  [r_raw[rb], r_sm], [rF["t1"]])
                V(lambda v: v.scalar_tensor_tensor(out=Fd["t1"], in0=zb[:, rb, 1:WT + 1], scalar=sm[:, CW_ + 8 + jc:CW_ + 9 + jc],
                                                   in1=Fd["t1"], op0=ALU.mult, op1=ALU.add), [r_raw[rb], rF["t1"], r_sm], [rF["t1"]])
                V(lambda v: v.scalar_tensor_tensor(out=Fd["t1"], in0=zb[:, rb, 2:WT + 2], scalar=sm[:, CW_ + 16 + jc:CW_ + 17 + jc],
                                                   in1=Fd["t1"], op0=ALU.mult, op1=ALU.add), [r_raw[rb], rF["t1"], r_sm], [rF["t1"]])
                V(lambda v: v.tensor_copy(out=zcarry[:, jc, :], in_=zb[:, rb, WT:WT + 2]), [r_raw[rb]], [r_zcarry])
                yb = jc % 2
                V(lambda v, yb=yb: v.tensor_tensor(out=ybf[:, yb, :], in0=gb, in1=Fd["t1"], op=ALU.mult), [rF["t2"], rF["t1"]], [r_ybf[yb]])
                S.dma("sp", ym[jc * 128:(jc + 1) * 128, t0:t0 + WT], ybf[:, yb, :], reads=[r_ybf[yb]], writes=[cx.r_scr])
            for j in range(8):
                wb = load_slab(1, j)
                proj_shift(lambda kc: wsl[:, wb, 0, kc, :], 128, [r_wsl[wb]], 0 + j, Fd["ur"], rF["ur"], first)
                proj_shift(lambda kc: wsl[:, wb, 1, kc, :], 128, [r_wsl[wb]], 8 + j, Fd["uk"], rF["uk"], first)
                proj_shift(lambda kc: wsl[:, wb, 2, kc, :], 128, [r_wsl[wb]], 16 + j, Fd["uv"], rF["uv"], first)
                cs = slice(j * 128, (j + 1) * 128)
                for s in range(2):
                    ss = slice(s * SW, (s + 1) * SW)
                    b = nb()
                    S.mm_group([lambda t, b=b: t.matmul(bank[b][:, 0:SW], w2a2[0:64, cs], lowA[0:64, ss], start=True, stop=True)],
                               reads=[r_w2a2, r_lowA], writes=[r_bank[b]])
                    A(lambda a, b=b: a.activation(out=Fd["lw"][:, ss], in_=bank[b][:, 0:SW], func=AF.Sigmoid, bias=sm[:, W0_ + j:W0_ + j + 1]),
                      [r_bank[b], r_sm], [rF["lw"]])
                    b = nb()
                    S.mm_group([lambda t, b=b: t.matmul(bank[b][:, 0:SW], w2a2[64:128, cs], lowA[64:128, ss], start=True, stop=True)],
                               reads=[r_w2a2, r_lowA], writes=[r_bank[b]])
                    A(lambda a, b=b: a.activation(out=Fd["aa"][:, ss], in_=bank[b][:, 0:SW], func=AF.Sigmoid, bias=sm[:, A0_ + j:A0_ + j + 1]),
                      [r_bank[b], r_sm], [rF["aa"]])
                    b = nb()
                    S.mm_group([lambda t, b=b: t.matmul(bank[b][:, 0:SW], g2a[:, cs], lowG1[:, ss], start=True, stop=False),
                                lambda t, b=b: t.matmul(bank[b][:, 0:SW], g2b[:, cs], lowG2[:, ss], start=False, stop=True)],
                               reads=[r_g2, r_lowG1, r_lowG2], writes=[r_bank[b]])
                    A(lambda a, b=b: a.activation(out=Fd["gg"][:, ss], in_=bank[b][:, 0:SW], func=AF.Copy), [r_bank[b]], [rF["gg"]])
                    q = sq_i[0] % 2
                    sq_i[0] += 1
                    A(lambda a, q=q: a.activation(out=sqs[:, q, :], in_=Fd["uk"][:, ss], func=AF.Square, scale=sm[:, KK_ + j:KK_ + j + 1]),
                      [rF["uk"], r_sm], [r_sqs[q]])
                    b = head_sum(lambda q=q: sqs[:, q, :], [r_sqs[q]], s)
                    A(lambda a, b=b: a.activation(out=Fd["t2"][:, ss], in_=bank[b][:, 0:SW], func=AF.Sqrt), [r_bank[b]], [rF["t2"]])
                V(lambda v: v.tensor_scalar(out=Fd["lw"], in0=Fd["lw"], scalar1=-DEC, scalar2=None, op0=ALU.mult), [rF["lw"]], [rF["lw"]])
                V(lambda v: v.tensor_scalar(out=Fd["t2"], in0=Fd["t2"], scalar1=1e-12, scalar2=None, op0=ALU.max), [rF["t2"]], [rF["t2"]])
                V(lambda v: v.reciprocal(out=Fd["t2"], in_=Fd["t2"]), [rF["t2"]], [rF["t2"]])
                V(lambda v: v.scalar_tensor_tensor(out=Fd["kkn"], in0=Fd["uk"], scalar=sm[:, KK_ + j:KK_ + j + 1], in1=Fd["t2"],
                                                   op0=ALU.mult, op1=ALU.mult), [rF["uk"], rF["t2"], r_sm], [rF["kkn"]])
                V(lambda v: v.tensor_scalar(out=Fd["t3"], in0=Fd["aa"], scalar1=sm[:, KA_ + j:KA_ + j + 1], scalar2=omka[:, j:j + 1],
                                            op0=ALU.mult, op1=ALU.add), [rF["aa"], r_sm], [rF["t3"]])
                V(lambda v: v.tensor_tensor(out=Fd["kf"], in0=Fd["uk"], in1=Fd["t3"], op=ALU.mult), [rF["uk"], rF["t3"]], [rF["kf"]])
                V(lambda v: v.tensor_tensor(out=Fd["bs"], in0=Fd["kkn"], in1=Fd["aa"], op=ALU.mult), [rF["kkn"], rF["aa"]], [rF["bs"]])
                V(lambda v: v.tensor_tensor(out=Fd["t3"], in0=Fd["ur"], in1=Fd["kf"], op=ALU.mult), [rF["ur"], rF["kf"]], [rF["t3"]])
                for s in range(2):
                    ss = slice(s * SW, (s + 1) * SW)
                    q = sq_i[0] % 2
                    sq_i[0] += 1
                    A(lambda a, q=q, ss=ss: a.activation(out=sqs[:, q, :], in_=Fd["t3"][:, ss], func=AF.Copy, scale=sm[:, RK_ + j:RK_ + j + 1]),
                      [rF["t3"], r_sm], [r_sqs[q]])
                    b = head_sum(lambda q=q: sqs[:, q, :], [r_sqs[q]], s)
                    V(lambda v, b=b, ss=ss: v.tensor_tensor(out=Fd["bonv"][:, ss], in0=Fd["uv"][:, ss], in1=bank[b][:, 0:SW], op=ALU.mult),
                      [rF["uv"], r_bank[b]], [rF["bonv"]])
                V(lambda v: v.tensor_tensor_scan(out=Fd["cw"], data0=rmask[:], data1=Fd["lw"], initial=0.0, op0=ALU.mult, op1=ALU.add),
                  [rF["lw"], r_sm], [rF["cw"]])
                cw3 = Fd["cw"].rearrange("p (c t) -> p c t", t=CH)
                A(lambda a: a.activation(out=WC[:], in_=cw3[:, :, CH - 1], func=AF.Exp), [rF["cw"]], [r_WC])
                V(lambda v: v.tensor_tensor(out=Fd["t3"], in0=Fd["cw"], in1=Fd["lw"], op=ALU.subtract), [rF["cw"], rF["lw"]], [rF["t3"]])
                A(lambda a: a.activation(out=Fd["E"], in_=Fd["t3"], func=AF.Exp), [rF["t3"]], [rF["E"]])
                V(lambda v: v.scalar_tensor_tensor(out=AR[:, 0, :], in0=Fd["kkn"], scalar=-1.0, in1=Fd["E"], op0=ALU.mult, op1=ALU.mult),
                  [rF["kkn"], rF["E"]], [r_AR])
                A(lambda a: a.activation(out=Fd["E"], in_=Fd["cw"], func=AF.Exp), [rF["cw"]], [rF["E"]])
                V(lambda v: v.tensor_tensor(out=AR[:, 1, :], in0=Fd["ur"], in1=Fd["E"], op=ALU.mult), [rF["ur"], rF["E"]], [r_AR])
                A(lambda a: a.activation(out=Fd["E"], in_=Fd["cw"], func=AF.Exp, scale=-1.0), [rF["cw"]], [rF["E"]])
                V(lambda v: v.tensor_tensor(out=Fd["BtT"], in0=Fd["bs"], in1=Fd["E"], op=ALU.mult), [rF["bs"], rF["E"]], [rF["BtT"]])
                V(lambda v: v.tensor_tensor(out=Fd["KtT"], in0=Fd["kf"], in1=Fd["E"], op=ALU.mult), [rF["kf"], rF["E"]], [rF["KtT"]])
                V(lambda v: v.tensor_tensor(out=Fd["t3"].rearrange("p (c t) -> p c t", t=CH), in0=cw3[:, :, CH - 1:CH].to_broadcast([128, NCH, CH]),
                                            in1=cw3, op=ALU.subtract), [rF["cw"]], [rF["t3"]])
                A(lambda a: a.activation(out=Fd["E"], in_=Fd["t3"], func=AF.Exp), [rF["t3"]], [rF["E"]])
                V(lambda v: v.tensor_tensor(out=Fd["BhT"], in0=Fd["bs"], in1=Fd["E"], op=ALU.mult), [rF["bs"], rF["E"]], [rF["BhT"]])
                V(lambda v: v.tensor_tensor(out=Fd["KhT"], in0=Fd["kf"], in1=Fd["E"], op=ALU.mult), [rF["kf"], rF["E"]], [rF["KhT"]])
                srcs = [(AR[:, 0, :], r_AR), (Fd["BhT"], rF["BhT"]), (Fd["KhT"], rF["KhT"]), (Fd["uv"], rF["uv"])]
                for ai, (src, r_src) in enumerate(srcs):
                    for g0 in range(0, NCH, 8):
                        b = nb()
                        cl = list(range(g0, min(NCH, g0 + 8)))
                        S.mm_group([lambda t, b=b, c=c, h=h, src=src, g0=g0: t.matmul(
                            bank[b][h * 64:(h + 1) * 64, (c - g0) * CH:(c - g0 + 1) * CH], src[h * 64:(h + 1) * 64, c * CH:(c + 1) * CH],
                            cx.ident_f[h * 64:(h + 1) * 64, h * 64:(h + 1) * 64], start=True, stop=True) for c in cl for h in range(2)],
                            reads=[r_src], writes=[r_bank[b]])
                        A(lambda a, b=b, g0=g0, n=len(cl), ai=ai: a.activation(
                            out=tok[:, ai, g0:g0 + n, :], in_=bank[b][:, 0:n * CH].rearrange("p (c t) -> p c t", t=CH), func=AF.Copy),
                          [r_bank[b]], [r_tok[ai]])
                for which, dstG, r_dstG, lhs in ((0, G1s, r_G1s, Fd["BtT"]), (1, G2s, r_G2s, Fd["KtT"])):
                    for g0 in range(0, NCH, 4):
                        b = nb()
                        cl = list(range(g0, min(NCH, g0 + 4)))
                        S.mm_group([lambda t, b=b, c=c, h=h, g0=g0, lhs=lhs: t.matmul(
                            bank[b][h * 64:(h + 1) * 64, (c - g0) * 128:(c - g0 + 1) * 128], lhs[h * 64:(h + 1) * 64, c * CH:(c + 1) * CH],
                            AR[h * 64:(h + 1) * 64, :, c * CH:(c + 1) * CH], start=True, stop=True) for c in cl for h in range(2)],
                            reads=[rF["BtT"], rF["KtT"], r_AR], writes=[r_bank[b]])
                        V(lambda v, b=b, g0=g0, n=len(cl), dstG=dstG: v.tensor_tensor(
                            out=dstG[:, g0:g0 + n, :], in0=bank[b][:, 0:n * 128].rearrange("p (c t) -> p c t", t=128),
                            in1=cx.maskG[:].unsqueeze(1).to_broadcast([128, n, 128]), op=ALU.mult), [r_bank[b]], [r_dstG])
                pp = 0
                for g0 in range(0, NCH, 8):
                    b = nb()
                    cl = list(range(g0, min(NCH, g0 + 8)))
                    S.mm_group([lambda t, b=b, c=c, h=h, g0=g0: t.matmul(
                        bank[b][h * 64:(h + 1) * 64, (c - g0) * CH:(c - g0 + 1) * CH], AR[h * 64:(h + 1) * 64, 0, c * CH:(c + 1) * CH],
                        Fd["BtT"][h * 64:(h + 1) * 64, c * CH:(c + 1) * CH], start=True, stop=True) for c in cl for h in range(2)],
                        reads=[rF["BtT"], r_AR], writes=[r_bank[b]])
                    V(lambda v, b=b, g0=g0, n=len(cl): v.tensor_tensor(
                        out=NM[:, 0, 1, g0:g0 + n, :], in0=bank[b][:, 0:n * CH].rearrange("p (c t) -> p c t", t=CH),
                        in1=cx.mask3[:].unsqueeze(1).to_broadcast([128, n, CH]), op=ALU.mult), [r_bank[b]], [r_NMc[0][1][c] for c in cl])
                A(lambda a: a.activation(out=NM[:, 0, 0, :, :], in_=G1s[:, :, 0:CH], func=AF.Copy), [r_G1s], r_NMc[0][0])
                A(lambda a: a.activation(out=X[:, :, 0:CH], in_=tok[:, 0, :, :], func=AF.Copy), [r_tok[0]], r_Xc)
                for g0 in range(0, NCH, 8):
                    b = nb()
                    cl = list(range(g0, min(NCH, g0 + 8)))
                    S.mm_group([lambda t, b=b, c=c, h=h, g0=g0: t.matmul(
                        bank[b][h * 64:(h + 1) * 64, (c - g0) * CH:(c - g0 + 1) * CH], G2s[h * 64:(h + 1) * 64, c, 0:CH],
                        tok[h * 64:(h + 1) * 64, 3, c, :], start=True, stop=True) for c in cl for h in range(2)],
                        reads=[r_G2s, r_tok[3]], writes=[r_bank[b]])
                    A(lambda a, b=b, g0=g0, n=len(cl): a.activation(
                        out=X[:, g0:g0 + n, CH:128], in_=bank[b][:, 0:n * CH].rearrange("p (c t) -> p c t", t=CH), func=AF.Copy),
                      [r_bank[b]], [r_Xc[c] for c in cl])
                for g0 in range(0, NCH, 4):
                    cl = list(range(g0, min(NCH, g0 + 4)))
                    A(lambda a, g0=g0, n=len(cl): a.activation(out=Xb[:, g0:g0 + n, :], in_=X[:, g0:g0 + n, :], func=AF.Copy),
                      [r_Xc[c] for c in cl], [r_Xbc[c] for c in cl])
                for it in range(6):
                    Ncur, Mcur = NM[:, pp, 0], NM[:, pp, 1]
                    for g0 in range(0, NCH, 4):
                        b = nb()
                        cl = list(range(g0, min(NCH, g0 + 4)))
                        S.mm_group([lambda t, b=b, c=c, h=h, g0=g0, Ncur=Ncur: t.matmul(
                            bank[b][h * 64:(h + 1) * 64, (c - g0) * 128:(c - g0 + 1) * 128], Ncur[h * 64:(h + 1) * 64, c, :],
                            Xb[h * 64:(h + 1) * 64, c, :], start=True, stop=True) for c in cl for h in range(2)],
                            reads=[r_NMc[pp][0][c] for c in cl] + [r_Xbc[c] for c in cl], writes=[r_bank[b]])
                        V(lambda v, b=b, g0=g0, n=len(cl): v.tensor_tensor(
                            out=X[:, g0:g0 + n, :], in0=X[:, g0:g0 + n, :], in1=bank[b][:, 0:n * 128].rearrange("p (c t) -> p c t", t=128),
                            op=ALU.add), [r_bank[b]] + [r_Xc[c] for c in cl], [r_Xc[c] for c in cl])
                        if it < 5:
                            A(lambda a, g0=g0, n=len(cl): a.activation(out=Xb[:, g0:g0 + n, :], in_=X[:, g0:g0 + n, :], func=AF.Copy),
                              [r_Xc[c] for c in cl], [r_Xbc[c] for c in cl])
                    if it < 5:
                        for which in range(2):
                            lhs, rhs = (Mcur, Ncur) if which == 0 else (Ncur, Mcur)
                            for g0 in range(0, NCH, 8):
                                b = nb()
                                cl = list(range(g0, min(NCH, g0 + 8)))
                                S.mm_group([lambda t, b=b, c=c, h=h, g0=g0, lhs=lhs, rhs=rhs: t.matmul(
                                    bank[b][h * 64:(h + 1) * 64, (c - g0) * CH:(c - g0 + 1) * CH], lhs[h * 64:(h + 1) * 64, c, :],
                                    rhs[h * 64:(h + 1) * 64, c, :], start=True, stop=True) for c in cl for h in range(2)],
                                    reads=[r_NMc[pp][0][c] for c in cl] + [r_NMc[pp][1][c] for c in cl], writes=[r_bank[b]])
                                A(lambda a, b=b, g0=g0, n=len(cl), which=which, pp=pp: a.activation(
                                    out=NM[:, 1 - pp, which, g0:g0 + n, :], in_=bank[b][:, 0:n * CH].rearrange("p (c t) -> p c t", t=CH),
                                    func=AF.Copy), [r_bank[b]], [r_NMc[1 - pp][which][c] for c in cl])
                        pp = 1 - pp
                for g0 in range(0, NCH, 8):
                    b = nb()
                    cl = list(range(g0, min(NCH, g0 + 8)))
                    S.mm_group([lambda t, b=b, c=c, h=h, g0=g0: t.matmul(
                        bank[b][h * 64:(h + 1) * 64, (c - g0) * CH:(c - g0 + 1) * CH], X[h * 64:(h + 1) * 64, c, 0:CH],
                        G1s[h * 64:(h + 1) * 64, c, CH:128], start=True, stop=True) for c in cl for h in range(2)],
                        reads=[r_Xc[c] for c in cl] + [r_G1s], writes=[r_bank[b]])
                    V(lambda v, b=b, g0=g0, n=len(cl): v.tensor_tensor(
                        out=AR[:, 1, g0 * CH:(g0 + n) * CH], in0=AR[:, 1, g0 * CH:(g0 + n) * CH], in1=bank[b][:, 0:n * CH], op=ALU.add),
                      [r_bank[b], r_AR], [r_AR])
                    b = nb()
                    S.mm_group([lambda t, b=b, c=c, h=h, g0=g0: t.matmul(
                        bank[b][h * 64:(h + 1) * 64, (c - g0) * CH:(c - g0 + 1) * CH], X[h * 64:(h + 1) * 64, c, 0:CH],
                        tok[h * 64:(h + 1) * 64, 1, c, :], start=True, stop=True) for c in cl for h in range(2)],
                        reads=[r_Xc[c] for c in cl] + [r_tok[1]], writes=[r_bank[b]])
                    for c in cl:
                        V(lambda v, b=b, c=c, g0=g0: v.scalar_tensor_tensor(
                            out=GamT[:, c, :], in0=cx.identstack[:], scalar=WC[:, c:c + 1], in1=bank[b][:, (c - g0) * CH:(c - g0 + 1) * CH],
                            op0=ALU.mult, op1=ALU.add), [r_bank[b], r_WC], [r_GamT])
                V(lambda v: v.tensor_copy(out=Zb[:, 0, :], in_=Zst[:, j, :]), [r_Zst], [r_Zb[0]])
                zp = 0
                for c in range(NCH):
                    b = nb()
                    fns = []
                    for h in range(2):
                        hs = slice(h * 64, (h + 1) * 64)
                        fns += [lambda t, b=b, hs=hs, c=c, zp=zp: t.matmul(bank[b][hs, 0:CH], GamT[hs, c, :], Zb[hs, zp, :], start=True, stop=False),
                                lambda t, b=b, hs=hs, c=c: t.matmul(bank[b][hs, 0:CH], tok[hs, 1, c, :], X[hs, c, CH:128], start=False, stop=False),
                                lambda t, b=b, hs=hs, c=c: t.matmul(bank[b][hs, 0:CH], tok[hs, 2, c, :], tok[hs, 3, c, :], start=False, stop=True)]
                    S.mm_group(fns, reads=[r_Zb[zp], r_GamT, r_Xc[c], r_tok[1], r_tok[2], r_tok[3]], writes=[r_bank[b]])
                    V(lambda v, b=b, zp=zp: v.tensor_copy(out=Zb[:, 1 - zp, :], in_=bank[b][:, 0:CH]), [r_bank[b]], [r_Zb[1 - zp]])
                    b = nb()
                    fns = []
                    for h in range(2):
                        hs = slice(h * 64, (h + 1) * 64)
                        fns += [lambda t, b=b, hs=hs, c=c, zp=zp: t.matmul(bank[b][hs, 0:CH], Zb[hs, zp, :], AR[hs, 1, c * CH:(c + 1) * CH],
                                                                         start=True, stop=False),
                                lambda t, b=b, hs=hs, c=c: t.matmul(bank[b][hs, 0:CH], X[hs, c, CH:128], G1s[hs, c, CH:128], start=False, stop=False),
                                lambda t, b=b, hs=hs, c=c: t.matmul(bank[b][hs, 0:CH], tok[hs, 3, c, :], G2s[hs, c, CH:128], start=False, stop=True)]
                    S.mm_group(fns, reads=[r_Zb[zp], r_AR, r_Xc[c], r_G1s, r_G2s, r_tok[3]], writes=[r_bank[b]])
                    A(lambda a, b=b, c=c: a.activation(out=Fd["Y"][:, c * CH:(c + 1) * CH], in_=bank[b][:, 0:CH], func=AF.Copy),
                      [r_bank[b]], [rF["Y"]])
                    zp = 1 - zp
                V(lambda v, zp=zp: v.tensor_copy(out=Zst[:, j, :], in_=Zb[:, zp, :]), [r_Zb[zp]], [r_Zst])
                yb = j % 2
                for s in range(2):
                    ss = slice(s * SW, (s + 1) * SW)
                    b1 = nb()
                    S.mm_group([lambda t, b1=b1, ss=ss: t.matmul(bank[b1][:, 0:SW], cx.blockones_f[:], Fd["Y"][:, ss], start=True, stop=True)],
                               reads=[rF["Y"]], writes=[r_bank[b1]])
                    A(lambda a, ss=ss: a.activation(out=Fd["t1"][:, ss], in_=Fd["Y"][:, ss], func=AF.Square), [rF["Y"]], [rF["t1"]])
                    b2 = nb()
                    S.mm_group([lambda t, b2=b2, ss=ss: t.matmul(bank[b2][:, 0:SW], cx.blockones_f[:], Fd["t1"][:, ss], start=True, stop=True)],
                               reads=[rF["t1"]], writes=[r_bank[b2]])
                    A(lambda a, b1=b1, ss=ss: a.activation(out=Fd["t2"][:, ss], in_=bank[b1][:, 0:SW], func=AF.Copy, scale=1.0 / 64),
                      [r_bank[b1]], [rF["t2"]])
                    V(lambda v, ss=ss: v.tensor_tensor(out=Fd["t3"][:, ss], in0=Fd["Y"][:, ss], in1=Fd["t2"][:, ss], op=ALU.subtract),
                      [rF["Y"], rF["t2"]], [rF["t3"]])
                    V(lambda v, ss=ss: v.tensor_tensor(out=Fd["t2"][:, ss], in0=Fd["t2"][:, ss], in1=Fd["t2"][:, ss], op=ALU.mult),
                      [rF["t2"]], [rF["t2"]])
                    V(lambda v, b2=b2, ss=ss: v.scalar_tensor_tensor(out=Fd["E"][:, ss], in0=bank[b2][:, 0:SW], scalar=1.0 / 64, in1=Fd["t2"][:, ss],
                                                                     op0=ALU.mult, op1=ALU.subtract), [r_bank[b2], rF["t2"]], [rF["E"]])
                    A(lambda a, ss=ss: a.activation(out=Fd["E"][:, ss], in_=Fd["E"][:, ss], func=AF.Sqrt, bias=cx.lnx_eps_col[:]), [rF["E"]], [rF["E"]])
                V(lambda v: v.reciprocal(out=Fd["E"], in_=Fd["E"]), [rF["E"]], [rF["E"]])
                V(lambda v: v.tensor_tensor(out=Fd["t3"], in0=Fd["t3"], in1=Fd["E"], op=ALU.mult), [rF["t3"], rF["E"]], [rF["t3"]])
                V(lambda v: v.tensor_scalar(out=Fd["t3"], in0=Fd["t3"], scalar1=sm[:, LG_ + j:LG_ + j + 1], scalar2=sm[:, LB_ + j:LB_ + j + 1],
                                            op0=ALU.mult, op1=ALU.add), [rF["t3"], r_sm], [rF["t3"]])
                V(lambda v: v.tensor_tensor(out=Fd["t3"], in0=Fd["t3"], in1=Fd["bonv"], op=ALU.add), [rF["t3"], rF["bonv"]], [rF["t3"]])
                V(lambda v, yb=yb: v.tensor_tensor(out=ybf[:, yb, :], in0=Fd["t3"], in1=Fd["gg"], op=ALU.mult), [rF["t3"], rF["gg"]], [r_ybf[yb]])
                S.dma("sp", ym[1024 + j * 128:1024 + (j + 1) * 128, t0:t0 + WT], ybf[:, yb, :], reads=[r_ybf[yb]], writes=[cx.r_scr])
        S.barrier()
    mixer_out_phase(S, cx, hT, cb, ym[:, 0:LSEQ], W["wout_r"], gi3)


DEPTH = 4
N_META = 16


def _consts_np():
    c = np.zeros((6, 128, 128), np.float32)
    c[0] = np.eye(128)
    r = np.arange(128)[:, None]
    cc = np.arange(128)[None, :]
    c[1] = np.where(cc < r, -30000.0, 0.0)
    c[2, :64, :64] = 1
    c[2, 64:, 64:] = 1
    s = (np.arange(128) % 64)[:, None]
    t = (np.arange(128) % 64)[None, :]
    c[3] = np.where(np.arange(128)[None, :] < 64, t > s, t >= s).astype(np.float32)
    c[4, :, :64] = (np.arange(64)[None, :] < s).astype(np.float32)
    c[5, :, :64] = (np.arange(64)[None, :] == s).astype(np.float32)
    return c


def _prep_w_in(w):
    wg = w[:, :DFF].reshape(KC, 128, NJ, 128)
    wu = w[:, DFF:].reshape(KC, 128, NJ, 128)
    return np.ascontiguousarray(np.concatenate([wg, wu], axis=3).transpose(2, 1, 0, 3))


def _prep_w_out(w, K):
    return np.ascontiguousarray(w.reshape(K, 128, KC, 128).transpose(2, 1, 0, 3))


def _prep_even(e_w_in, e_conv_w, e_mu, e_w0, e_w2, e_a0, e_a2, e_g2, e_k_k, e_k_a, e_r_k, e_lnx_g, e_lnx_b, e_w_out):
    def slabs(cols0, n):
        w = e_w_in[:, cols0:cols0 + n * 128].reshape(KC, 128, n, 128)
        return w.transpose(2, 1, 0, 3)
    conv = np.stack([slabs(0, 8), slabs(1024, 8), slabs(2048, 8)], axis=1)
    rkv = np.stack([slabs(3072, 8), slabs(4096, 8), slabs(5120, 8)], axis=1)
    low = np.ascontiguousarray(e_w_in[:, 6144:6432].reshape(KC, 128, 288).transpose(1, 0, 2))
    sm = np.zeros((128, NSM), np.float32)
    col = lambda v: v.reshape(-1, 128).T
    sm[:, MU:MU + 24] = col(e_mu[:3072])
    sm[:64, MU + 24] = e_mu[3072:3136]
    sm[64:, MU + 24] = e_mu[3136:3200]
    sm[:, MU + 25] = e_mu[3200:3328]
    sm[:32, MU + 26] = e_mu[3328:3360]
    for jj in range(3):
        sm[:, CW_ + 8 * jj:CW_ + 8 * jj + 8] = col(e_conv_w[jj])
    for base, v in ((W0_, e_w0), (A0_, e_a0), (KK_, e_k_k), (KA_, e_k_a), (RK_, e_r_k), (LG_, e_lnx_g), (LB_, e_lnx_b)):
        sm[:, base:base + 8] = col(v)
    return dict(ew_conv_r=np.ascontiguousarray(conv), ew_rkv_r=np.ascontiguousarray(rkv), ew_low_r=low, esmall=sm,
                w2a2=np.ascontiguousarray(np.concatenate([e_w2, e_a2], axis=0)), g2=np.ascontiguousarray(e_g2),
                wout_r=_prep_w_out(e_w_out, KC))


def _prep_fox(w_in, b_f, q_g, k_g, w_out):
    win_r = np.ascontiguousarray(w_in[:, :4 * D].reshape(KC, 128, 4, 16, 128).transpose(2, 3, 1, 0, 4))
    wf_r = np.ascontiguousarray(w_in[:, 4 * D:].reshape(KC, 128, 16).transpose(1, 0, 2))
    small = np.zeros((128, 4), np.float32)
    small[:, 0] = q_g
    small[:, 1] = k_g
    small[:16, 2] = b_f
    return dict(win_r=win_r, wf_r=wf_r, small=small, wout_r=_prep_w_out(w_out, KC))


EVEN_SHAPES = dict(ew_conv_r=[8, 3, 128, 16, 128], ew_rkv_r=[8, 3, 128, 16, 128], ew_low_r=[128, 16, 288], esmall=[128, NSM],
                   w2a2=[128, 1024], g2=[160, 1024], wout_r=[16, 128, 16, 128])
FOX_SHAPES = dict(win_r=[4, 16, 128, 16, 128], wf_r=[128, 16, 16], small=[128, 4], wout_r=[16, 128, 16, 128])


def build_program():
    nc = bass.Bass("TRN2", target_bir_lowering=False)
    ext = lambda name, shape: nc.dram_tensor(name, shape, F32, kind="ExternalInput").ap()
    h0 = ext("h0", [D, TC])
    gcol = ext("gcol", [128, DEPTH * 6 * KC])
    consts = ext("consts", [6, 128, 128])
    ffn_in = ext("ffn_in_r", [DEPTH * 2, NJ, 128, KC, 256])
    ffn_out = ext("ffn_out_r", [DEPTH * 2, KC, 128, NJ, 128])
    EW = [{k: ext("e%d_%s" % (i, k), v) for k, v in EVEN_SHAPES.items()} for i in range(2)]
    OW = [{k: ext("o%d_%s" % (i, k), v) for k, v in FOX_SHAPES.items()} for i in range(2)]
    hT = nc.dram_tensor("hT", [D, TC], F32, kind="ExternalOutput").ap()
    scr = {"c": nc.dram_tensor("scr_c", [6, 16, LSEQ], BF16, kind="Internal").ap(),
           "og": nc.dram_tensor("scr_og", [D, LSEQ], BF16, kind="Internal").ap(),
           "ym": nc.dram_tensor("scr_ym", [D, LPAD], BF16, kind="Internal").ap()}
    S = Sched(nc)
    cx = make_ctx(S, gcol, DEPTH * 6 * KC, consts)
    S.dma("sp", hT, h0, writes=cx.r_h)
    for l in range(DEPTH):
        gi = lambda i: (l * 6 + i) * KC
        ffn_phase(S, cx, hT, ffn_in[l * 2], ffn_out[l * 2], gi(0), gi(1), TC // TT)
        for seq in range(NSEQ):
            if l % 2 == 0:
                even_phase(S, cx, hT, seq * LSEQ, EW[l // 2], gi(2), gi(3), scr)
            else:
                fox_phase(S, cx, hT, seq * LSEQ, OW[l // 2], gi(2), gi(3), scr)
        ffn_phase(S, cx, hT, ffn_in[l * 2 + 1], ffn_out[l * 2 + 1], gi(4), gi(5), TC // TT)
    S.finish()
    return nc, S


def kernel(x, meta, norm_g, ffn_in, ffn_out, e_w_in, e_conv_w, e_mu, e_w0, e_w2, e_a0, e_a2, e_g2, e_k_k, e_k_a, e_r_k,
           e_lnx_g, e_lnx_b, e_w_out, o_w_in, o_b_f, o_q_g, o_k_g, o_w_out):
    f = lambda a: np.asarray(a, dtype=np.float32)
    x, meta, norm_g = f(x), f(meta), f(norm_g)
    shared = {}
    shared["gcol"] = np.ascontiguousarray(norm_g.reshape(DEPTH * 6, KC, 128).transpose(2, 0, 1).reshape(128, DEPTH * 6 * KC))
    shared["consts"] = _consts_np()
    fi, fo = f(ffn_in), f(ffn_out)
    shared["ffn_in_r"] = np.stack([_prep_w_in(fi[l, k]) for l in range(DEPTH) for k in range(2)])
    shared["ffn_out_r"] = np.stack([_prep_w_out(fo[l, k], NJ) for l in range(DEPTH) for k in range(2)])
    ev = [e_w_in, e_conv_w, e_mu, e_w0, e_w2, e_a0, e_a2, e_g2, e_k_k, e_k_a, e_r_k, e_lnx_g, e_lnx_b, e_w_out]
    for i in range(2):
        for k, v in _prep_even(*[f(a)[i] for a in ev]).items():
            shared["e%d_%s" % (i, k)] = v
        for k, v in _prep_fox(f(o_w_in)[i], f(o_b_f)[i], f(o_q_g)[i], f(o_k_g)[i], f(o_w_out)[i]).items():
            shared["o%d_%s" % (i, k)] = v
    in_maps = []
    for c in range(NCORES):
        hs = [np.concatenate([meta, x[c * NSEQ + s]], axis=0) for s in range(NSEQ)]
        h0 = np.ascontiguousarray(np.concatenate(hs, axis=0).T)
        m = dict(shared)
        m["h0"] = h0
        in_maps.append(m)
    nc, _ = build_program()
    res = run_bass_kernel_spmd(nc, in_maps, core_ids=list(range(NCORES)))
    out = np.empty((NCORES * NSEQ, LSEQ - N_META, D), np.float32)
    for c in range(NCORES):
        hT = res.results[c]["hT"]
        for s in range(NSEQ):
            out[c * NSEQ + s] = hT[:, s * LSEQ + N_META:(s + 1) * LSEQ].T
    return out
```
